# Optimizing a Trainium2 kernel written in Bass

```python
import math
import jax
import jax.numpy as jnp
from jax import lax
import numpy as np

D_MODEL = 2048
BATCH = 16
SEQ = 2048
DEPTH = 4

DIFF_WIDTH = D_MODEL // 2
DN_WIDTH = D_MODEL - DIFF_WIDTH
DIFF_HEAD_DIM = 64
N_DIFF_HEADS = DIFF_WIDTH // (2 * DIFF_HEAD_DIM)
DIFF_QK = 2 * N_DIFF_HEADS * DIFF_HEAD_DIM
DIFF_V = N_DIFF_HEADS * 2 * DIFF_HEAD_DIM
DN_HEAD_DIM = 128
N_DN_HEADS = DN_WIDTH // DN_HEAD_DIM
MIX_WIDTH = DIFF_V + DN_WIDTH
D_FF = 5632
CONV_K = 4
CHUNK = 64
Q_BLOCK = 128
ROPE_THETA = 500000.0
ROPE_DIM = DIFF_HEAD_DIM // 4
DEEPNORM_ALPHA = (2 * DEPTH) ** 0.25
DEEPNORM_BETA = (8 * DEPTH) ** -0.25
LN_EPS = 1e-5
SUBLN_EPS = 1e-5
GATED_NORM_EPS = 1e-6
L2_EPS = 1e-6
IN_SIZES = (DIFF_QK, DIFF_QK, DIFF_V, DN_WIDTH, DN_WIDTH, DN_WIDTH, DN_WIDTH, N_DN_HEADS, N_DN_HEADS)
IN_WIDTH = sum(IN_SIZES)

kernel_name = 'hymba_diffattn_gdn_macaron_deepnorm'


def layer_norm(x, g, b):
    xf = x.astype(jnp.float32)
    mu = jnp.mean(xf, axis=-1, keepdims=True)
    var = jnp.mean(jnp.square(xf - mu), axis=-1, keepdims=True)
    return ((xf - mu) * lax.rsqrt(var + LN_EPS) * g + b).astype(x.dtype)


def rms_norm(x, g, eps):
    xf = x.astype(jnp.float32)
    return (xf * lax.rsqrt(jnp.mean(jnp.square(xf), axis=-1, keepdims=True) + eps) * g).astype(x.dtype)


def l2_norm(x):
    return x * lax.rsqrt(jnp.sum(jnp.square(x), axis=-1, keepdims=True) + L2_EPS)


def swiglu(x, w_in, w_out):
    gate, up = jnp.split(x @ w_in, 2, axis=-1)
    return (jax.nn.silu(gate) * up) @ w_out


def rope_tables(positions):
    inv_freq = ROPE_THETA ** (-jnp.arange(0, ROPE_DIM, 2, dtype=jnp.float32) / ROPE_DIM)
    ang = positions.astype(jnp.float32)[..., None] * inv_freq
    return jnp.cos(ang)[:, :, None, :], jnp.sin(ang)[:, :, None, :]


def partial_rope(x, cos, sin):
    half = ROPE_DIM // 2
    x1 = x[..., :half].astype(jnp.float32)
    x2 = x[..., half:ROPE_DIM].astype(jnp.float32)
    rot = jnp.concatenate([x1 * cos - x2 * sin, x2 * cos + x1 * sin], axis=-1).astype(x.dtype)
    return jnp.concatenate([rot, x[..., ROPE_DIM:]], axis=-1)


def causal_depthwise_conv(u, w):
    S = u.shape[1]
    u_pad = jnp.pad(u, ((0, 0), (CONV_K - 1, 0), (0, 0)))
    return sum(w[j] * u_pad[:, j:j + S] for j in range(CONV_K))


def diff_attention(q, k, v, lam):
    S = q.shape[3]
    n_blk = S // Q_BLOCK
    scale = DIFF_HEAD_DIM ** -0.5
    k_pos = jnp.arange(S)

    def one_block(i):
        q_blk = lax.dynamic_slice_in_dim(q, i * Q_BLOCK, Q_BLOCK, axis=3)
        s = jnp.einsum('bhcqd,bhckd->bhcqk', q_blk, k).astype(jnp.float32) * scale
        q_pos = i * Q_BLOCK + jnp.arange(Q_BLOCK)
        s = jnp.where(k_pos[None, :] <= q_pos[:, None], s, -jnp.inf)
        p = jax.nn.softmax(s, axis=-1)
        p = p[:, :, 0] - lam * p[:, :, 1]
        return jnp.einsum('bhqk,bhkd->bhqd', p.astype(v.dtype), v)

    o = lax.map(one_block, jnp.arange(n_blk))
    B, H = q.shape[0], q.shape[1]
    return jnp.transpose(o, (1, 0, 3, 2, 4)).reshape(B, S, H, v.shape[-1])


def chunk_gated_delta_rule(q, k, v, beta, g):
    B, H, S, dk = q.shape
    dv = v.shape[-1]
    n = S // CHUNK
    q = q * dk ** -0.5
    q, k, v = (t.reshape(B, H, n, CHUNK, t.shape[-1]) for t in (q, k, v))
    beta = beta.reshape(B, H, n, CHUNK)
    g = jnp.cumsum(g.reshape(B, H, n, CHUNK), axis=-1)
    idx = jnp.arange(CHUNK)
    incl = idx[:, None] >= idx[None, :]
    strict = idx[:, None] > idx[None, :]
    decay = jnp.exp(jnp.where(incl, g[..., :, None] - g[..., None, :], -jnp.inf))
    k_beta = k * beta[..., None]
    L = jnp.where(strict, jnp.einsum('bhncd,bhnkd->bhnck', k_beta, k) * decay, 0.0)
    a = L + jnp.eye(CHUNK, dtype=jnp.float32)
    rhs = jnp.concatenate([v * beta[..., None], k_beta * jnp.exp(g)[..., None]], axis=-1)
    sol = lax.linalg.triangular_solve(a, rhs, left_side=True, lower=True, unit_diagonal=True)
    u, w = sol[..., :dv], sol[..., dv:]
    attn_intra = jnp.einsum('bhncd,bhnkd->bhnck', q, k) * decay
    g_last = g[..., -1:]
    q_dec = q * jnp.exp(g)[..., None]
    k_dec = k * jnp.exp(g_last - g)[..., None]
    chunk_decay = jnp.exp(g_last[..., 0])
    xs = tuple(jnp.moveaxis(t, 2, 0) for t in (q_dec, k_dec, u, w, attn_intra, chunk_decay))

    def step(state, inp):
        qd, kd, u_c, w_c, A, cd = inp
        v_new = u_c - jnp.einsum('bhck,bhkv->bhcv', w_c, state)
        o = jnp.einsum('bhck,bhkv->bhcv', qd, state) + jnp.einsum('bhcj,bhjv->bhcv', A, v_new)
        state = state * cd[..., None, None] + jnp.einsum('bhck,bhcv->bhkv', kd, v_new)
        return state, o

    s0 = jnp.zeros((B, H, dk, dv), jnp.float32)
    _, o = lax.scan(step, s0, xs)
    return jnp.moveaxis(o, 0, 2).reshape(B, H, S, dv)


def hybrid_mixer(x, cos, sin, w_in, conv_w, a_log, dt_bias, lam_q1, lam_k1, lam_q2, lam_k2,
                 diff_norm_g, delta_norm_g, w_out, lam_init):
    B, S, _ = x.shape
    offsets = tuple(int(o) for o in np.cumsum(IN_SIZES)[:-1])
    aq, ak, av, dq, dk, dv, dz, db, da = jnp.split(x @ w_in, offsets, axis=-1)

    aq = partial_rope(aq.reshape(B, S, 2 * N_DIFF_HEADS, DIFF_HEAD_DIM), cos, sin)
    ak = partial_rope(ak.reshape(B, S, 2 * N_DIFF_HEADS, DIFF_HEAD_DIM), cos, sin)
    aq = jnp.transpose(aq.reshape(B, S, N_DIFF_HEADS, 2, DIFF_HEAD_DIM), (0, 2, 3, 1, 4))
    ak = jnp.transpose(ak.reshape(B, S, N_DIFF_HEADS, 2, DIFF_HEAD_DIM), (0, 2, 3, 1, 4))
    av = jnp.transpose(av.reshape(B, S, N_DIFF_HEADS, 2 * DIFF_HEAD_DIM), (0, 2, 1, 3))
    lam = (jnp.exp(jnp.sum(lam_q1.astype(jnp.float32) * lam_k1.astype(jnp.float32)))
           - jnp.exp(jnp.sum(lam_q2.astype(jnp.float32) * lam_k2.astype(jnp.float32))) + lam_init)
    o_diff = diff_attention(aq, ak, av, lam)
    o_diff = (rms_norm(o_diff, diff_norm_g, SUBLN_EPS) * (1.0 - lam_init)).reshape(B, S, DIFF_V)

    qkv = jax.nn.silu(causal_depthwise_conv(jnp.concatenate([dq, dk, dv], axis=-1), conv_w))
    dq, dk, dv = jnp.split(qkv, 3, axis=-1)
    heads = lambda t: jnp.transpose(t.reshape(B, S, N_DN_HEADS, DN_HEAD_DIM), (0, 2, 1, 3)).astype(jnp.float32)
    dq, dk, dv = l2_norm(heads(dq)), l2_norm(heads(dk)), heads(dv)
    beta = jnp.transpose(jax.nn.sigmoid(db.astype(jnp.float32)), (0, 2, 1))
    g = -jnp.exp(a_log.astype(jnp.float32)) * jax.nn.softplus(da.astype(jnp.float32) + dt_bias.astype(jnp.float32))
    g = jnp.transpose(g, (0, 2, 1))
    o_dn = chunk_gated_delta_rule(dq, dk, dv, beta, g)
    o_dn = jnp.transpose(o_dn, (0, 2, 1, 3)).astype(x.dtype)
    o_dn = rms_norm(o_dn, delta_norm_g, GATED_NORM_EPS) * jax.nn.silu(dz.reshape(B, S, N_DN_HEADS, DN_HEAD_DIM))
    o_dn = o_dn.reshape(B, S, DN_WIDTH)

    return jnp.concatenate([o_diff, o_dn], axis=-1) @ w_out


def setup_inputs(seed: int = 0) -> dict:
    key = jax.random.key(seed)
    ks = jax.random.split(key, 24)
    f32 = jnp.float32

    def nrm(k, shape, scale):
        return jax.random.normal(k, shape, f32) * scale

    x = nrm(ks[0], (BATCH, SEQ, D_MODEL), 1.0)
    offset = jax.random.randint(ks[1], (BATCH, 1), 0, 4096, dtype=jnp.int32)
    positions = offset + jnp.arange(SEQ, dtype=jnp.int32)[None, :]
    ffn1_w_in = nrm(ks[2], (DEPTH, D_MODEL, 2 * D_FF), D_MODEL ** -0.5)
    ffn1_w_out = nrm(ks[3], (DEPTH, D_FF, D_MODEL), D_FF ** -0.5 * DEEPNORM_BETA)
    ln1_g = 1.0 + nrm(ks[4], (DEPTH, D_MODEL), 0.02)
    ln1_b = nrm(ks[5], (DEPTH, D_MODEL), 0.02)
    w_in = nrm(ks[6], (DEPTH, D_MODEL, IN_WIDTH), D_MODEL ** -0.5)
    conv_w = nrm(ks[7], (DEPTH, CONV_K, 3 * DN_WIDTH), CONV_K ** -0.5)
    a_log = jnp.log(jax.random.uniform(ks[8], (DEPTH, N_DN_HEADS), f32, 1.0, 16.0))
    dt = jnp.exp(jax.random.uniform(ks[9], (DEPTH, N_DN_HEADS), f32, math.log(1e-3), math.log(1e-1)))
    dt_bias = dt + jnp.log(-jnp.expm1(-dt))
    lam_q1 = nrm(ks[10], (DEPTH, DIFF_HEAD_DIM), 0.1)
    lam_k1 = nrm(ks[11], (DEPTH, DIFF_HEAD_DIM), 0.1)
    lam_q2 = nrm(ks[12], (DEPTH, DIFF_HEAD_DIM), 0.1)
    lam_k2 = nrm(ks[13], (DEPTH, DIFF_HEAD_DIM), 0.1)
    diff_norm_g = 1.0 + nrm(ks[14], (DEPTH, 2 * DIFF_HEAD_DIM), 0.02)
    delta_norm_g = 1.0 + nrm(ks[15], (DEPTH, DN_HEAD_DIM), 0.02)
    w_out = nrm(ks[16], (DEPTH, MIX_WIDTH, D_MODEL), MIX_WIDTH ** -0.5 * DEEPNORM_BETA)
    ln2_g = 1.0 + nrm(ks[17], (DEPTH, D_MODEL), 0.02)
    ln2_b = nrm(ks[18], (DEPTH, D_MODEL), 0.02)
    ffn2_w_in = nrm(ks[19], (DEPTH, D_MODEL, 2 * D_FF), D_MODEL ** -0.5)
    ffn2_w_out = nrm(ks[20], (DEPTH, D_FF, D_MODEL), D_FF ** -0.5 * DEEPNORM_BETA)
    ln3_g = 1.0 + nrm(ks[21], (DEPTH, D_MODEL), 0.02)
    ln3_b = nrm(ks[22], (DEPTH, D_MODEL), 0.02)
    return {'x': x, 'positions': positions, 'ffn1_w_in': ffn1_w_in, 'ffn1_w_out': ffn1_w_out,
            'ln1_g': ln1_g, 'ln1_b': ln1_b, 'w_in': w_in, 'conv_w': conv_w, 'a_log': a_log,
            'dt_bias': dt_bias, 'lam_q1': lam_q1, 'lam_k1': lam_k1, 'lam_q2': lam_q2, 'lam_k2': lam_k2,
            'diff_norm_g': diff_norm_g, 'delta_norm_g': delta_norm_g, 'w_out': w_out,
            'ln2_g': ln2_g, 'ln2_b': ln2_b, 'ffn2_w_in': ffn2_w_in, 'ffn2_w_out': ffn2_w_out,
            'ln3_g': ln3_g, 'ln3_b': ln3_b}


def reference(x, positions, ffn1_w_in, ffn1_w_out, ln1_g, ln1_b, w_in, conv_w, a_log, dt_bias,
              lam_q1, lam_k1, lam_q2, lam_k2, diff_norm_g, delta_norm_g, w_out, ln2_g, ln2_b,
              ffn2_w_in, ffn2_w_out, ln3_g, ln3_b):
    cos, sin = rope_tables(positions)
    for l in range(DEPTH):
        lam_init = 0.8 - 0.6 * math.exp(-0.3 * l)
        x = layer_norm(DEEPNORM_ALPHA * x + 0.5 * swiglu(x, ffn1_w_in[l], ffn1_w_out[l]), ln1_g[l], ln1_b[l])
        mix = hybrid_mixer(x, cos, sin, w_in[l], conv_w[l], a_log[l], dt_bias[l], lam_q1[l], lam_k1[l],
                           lam_q2[l], lam_k2[l], diff_norm_g[l], delta_norm_g[l], w_out[l], lam_init)
        x = layer_norm(DEEPNORM_ALPHA * x + mix, ln2_g[l], ln2_b[l])
        x = layer_norm(DEEPNORM_ALPHA * x + 0.5 * swiglu(x, ffn2_w_in[l], ffn2_w_out[l]), ln3_g[l], ln3_b[l])
    return x
```

```python
import contextlib
import math
import numpy as np
import concourse.bass as bass
import concourse.mybir as mybir
from concourse.bass_utils import run_bass_kernel_spmd

F32 = mybir.dt.float32
BF16 = mybir.dt.bfloat16
I32 = mybir.dt.int32
AF = mybir.ActivationFunctionType
ALU = mybir.AluOpType

D = 2048
NC_ = 16
DFF = 5632
NFC = 44
DEPTH = 4
SEQ = 2048
ALPHA = (2 * DEPTH) ** 0.25
LN_EPS = 1e-5
IN_WIDTH = 7184


class Slot:
    __slots__ = ("name", "w", "r")

    def __init__(self, name):
        self.name = name
        self.w = None
        self.r = {}


class Prog:
    def __init__(self, nc):
        self.nc = nc
        self.q = {"pe": [], "act": [], "dve": [], "pool": [], "sp": []}
        self.cnt = {}
        self.waited = {e: {} for e in self.q}
        self.nslots = 0
        self.qmap = {"pool": "sp"}

    def slot(self, name=None):
        self.nslots += 1
        return Slot(name or f"s{self.nslots}")

    def _deps(self, eng, reads, writes, extra=()):
        deps = {}

        def add(ev):
            if ev is None:
                return
            k, v = ev
            if deps.get(k, 0) < v:
                deps[k] = v

        for s in reads:
            add(s.w)
        for s in writes:
            add(s.w)
            for k, v in s.r.items():
                add((k, v))
        for ev in extra:
            add(ev)
        waits = []
        wd = self.waited[eng]
        for k, v in deps.items():
            if k == "pe" and eng == "pe":
                continue
            if wd.get(k, 0) >= v:
                continue
            wd[k] = v
            waits.append((k, v))
        return waits

    def _mark(self, ev, reads, writes):
        k, v = ev
        for s in reads:
            if s.r.get(k, 0) < v:
                s.r[k] = v
        for s in writes:
            s.w = ev
            s.r = {}

    def op(self, eng, fn, reads=(), writes=(), signal=True):
        waits = self._deps(eng, reads, writes)
        c = self.cnt.get(eng, 0)
        if signal:
            c += 1
            self.cnt[eng] = c
            ev = (eng, c)
            inc = (eng, 1)
        else:
            ev = (eng, c + 1)
            inc = None
        self._mark(ev, reads, writes)
        self.q[eng].append((waits, fn, inc))

    def dma(self, qeng, chan, fn, reads=(), writes=()):
        qeng = self.qmap.get(qeng, qeng)
        key = "d:" + chan
        c = self.cnt.get(key, 0)
        waits = self._deps(qeng, reads, writes, extra=[(key, c)] if c else [])
        c += 16
        self.cnt[key] = c
        self._mark((key, c), reads, writes)
        self.q[qeng].append((waits, fn, (key, 16)))

    def fence(self):
        for e in self.q:
            waits = []
            wd = self.waited[e]
            for k, v in self.cnt.items():
                if k == "pe" and e == "pe":
                    continue
                if wd.get(k, 0) >= v:
                    continue
                wd[k] = v
                waits.append((k, v))
            if waits:
                self.q[e].append((waits, None, None))

    def wait_all(self, eng, slots):
        waits = self._deps(eng, slots, ())
        self.q[eng].append((waits, None, None))

    def emit(self, stack):
        nc = self.nc
        sems = {}
        for k in self.cnt:
            sems[k] = stack.enter_context(nc.semaphore("sem_" + k.replace(":", "_")))
        engs = {"pe": "tensor", "act": "scalar", "dve": "vector", "pool": "gpsimd", "sp": "sync"}
        q = self.q

        def run(e, lst):
            for waits, fn, inc in lst:
                for k, v in waits:
                    e.wait_ge(sems[k], v)
                if fn is not None:
                    ins = fn(e)
                    if inc is not None:
                        ins.then_inc(sems[inc[0]], inc[1])

        with nc.Block() as block:
            for name, attr in engs.items():
                if not q[name]:
                    continue

                def mk(lst):
                    def f(e):
                        run(e, lst)
                    return f

                getattr(block, attr)(mk(q[name]))


class Buf:
    uid = 0

    def __init__(self, P, stack, name, shape, dtype, n=1, psum=False):
        self.n = n
        self.t = []
        self.s = []
        for i in range(n):
            Buf.uid += 1
            nm = f"{name}{i}_{Buf.uid}"
            if psum:
                t = stack.enter_context(P.nc.psum_tensor(nm, shape, dtype))
            else:
                t = stack.enter_context(P.nc.sbuf_tensor(nm, shape, dtype))
            self.t.append(t)
            self.s.append(P.slot(nm))
        self.i = -1

    def next(self):
        self.i = (self.i + 1) % self.n
        return self.t[self.i], self.s[self.i]

    def cur(self):
        return self.t[self.i], self.s[self.i]


def build_program(cfg):
    NT = cfg["ntok"]
    depth = cfg["depth"]
    stages = cfg.get("stages", ("ffn1", "mix", "ffn2"))
    TT = 512
    ntile = NT // TT
    n128 = NT // 128

    nc = bass.Bass("TRN2", target_bir_lowering=False)
    P = Prog(nc)
    stack = contextlib.ExitStack()

    def din(name, shape, dt=F32):
        return nc.dram_tensor(name, list(shape), dt, kind="ExternalInput").ap()

    x_in = din("x", [NT, D])
    w1i = din("ffn1_w_in", [depth, D, 2 * DFF]) if "ffn1" in stages else None
    w1o = din("ffn1_w_out", [depth, DFF, D]) if "ffn1" in stages else None
    w2i = din("ffn2_w_in", [depth, D, 2 * DFF]) if "ffn2" in stages else None
    w2o = din("ffn2_w_out", [depth, DFF, D]) if "ffn2" in stages else None
    if "mix" in stages:
        w_pr = din("w_in", [depth, D, IN_WIDTH])
        w_sw = din("w_in_sw", [depth, D, 2048])
        w_mo = din("w_out", [depth, D, D])
        convw_in = din("convw", [128, depth * 24 * 4])
        hp8_in = din("hp8", [128, depth * 2])
        lamv_in = din("lamv", [128, depth * 256])
        normg_in = din("normg", [128, depth * 256])
        pos_in = din("pos", [128, NT], I32)
        ropec_in = din("ropec", [128, 2])
        masks_in = din("masks", [128, 640])
    lnp = din("lnp", [128, depth * 6 * NC_])
    ident_in = din("ident", [128, 128])
    y_out = nc.dram_tensor("y", [NT, D], F32, kind="ExternalOutput").ap()

    def dscr(name, shape, dt):
        return nc.dram_tensor(name, list(shape), dt, kind="Internal").ap()

    XF = dscr("XF", [ntile, 128, NC_, TT], F32)
    XB = dscr("XB", [ntile, 128, NC_, TT], BF16)
    WIN = {}
    WOUT = {}
    for l in range(depth):
        for f in (1, 2):
            WIN[l, f] = dscr(f"WIN{l}_{f}", [NFC, 128, NC_, 256], BF16)
            WOUT[l, f] = dscr(f"WOUT{l}_{f}", [NC_, 128, NFC, 128], BF16)
    nseq = NT // SEQ if NT >= SEQ else 1
    SQ = min(SEQ, NT)
    if "mix" in stages:
        WPR = {l: dscr(f"WPR{l}", [56, 128, NC_, 128], BF16) for l in range(depth)}
        WSW = {l: dscr(f"WSW{l}", [16, 128, NC_, 128], BF16) for l in range(depth)}
        WBA = {l: dscr(f"WBA{l}", [128, NC_, 16], BF16) for l in range(depth)}
        WMO = {l: dscr(f"WMO{l}", [NC_, 128, NC_, 128], BF16) for l in range(depth)}
        ROPE = dscr("ROPE", [nseq, 2, 128, SQ], F32)
        MIX = dscr("MIX", [ntile, 128, NC_, TT], BF16)
        mix_slots = [[P.slot(f"MIX{t}_{j}") for j in range(NC_)] for t in range(ntile)]
        rope_slots = [P.slot(f"ROPE{q}") for q in range(nseq)]
    xf_slots = [P.slot(f"XF{t}") for t in range(ntile)]
    xb_slots = [P.slot(f"XB{t}") for t in range(ntile)]
    wslot = {}

    with stack:
        ident = Buf(P, stack, "ident", [128, 128], F32)
        ones_bf = Buf(P, stack, "ones_bf", [128, 128], BF16)
        lnp_sb = Buf(P, stack, "lnp_sb", [128, depth * 6 * NC_], F32)
        it, isl = ident.next()
        P.dma("pool", "const", lambda e: e.dma_start(out=it[:], in_=ident_in[:, :]), writes=[isl])
        ot, osl = ones_bf.next()
        P.op("pool", lambda e: e.memset(ot[:], 1.0), writes=[osl])
        lt, lsl = lnp_sb.next()
        if not (cfg.get("dbg", 0) & 2):
            P.dma("pool", "const", lambda e: e.dma_start(out=lt[:], in_=lnp[:, :]), writes=[lsl])

        def lnvec(l, which, c):
            o = (l * 6 + which) * NC_ + c
            return lt[:, o:o + 1]

        banks = Buf(P, stack, "bank", [128, 512], F32, n=8, psum=True)
        bk = list(zip(banks.t, banks.s))

        ph = contextlib.ExitStack()
        stg = Buf(P, ph, "stg", [128, NC_ * 256], F32, n=2)
        stgb = Buf(P, ph, "stgb", [128, NC_ * 256], BF16, n=2)
        ci = [0]

        def precast(src_fn, dst_ap, width, slot):
            st, ss = stg.next()
            sb, sbs = stgb.next()
            k = ci[0]
            ci[0] += 1
            par = k % 2
            qe = "sp" if par == 0 else "act"
            rs = src_fn(qe, st, par)
            ce = "pool" if par == 0 else "dve"
            P.op(ce, lambda e: e.tensor_copy(out=sb[:, :width], in_=st[:, :width]), reads=rs, writes=[sbs])
            P.dma(qe, f"pcs{par}", lambda e: e.dma_start(out=dst_ap, in_=sb[:, :width]), reads=[sbs], writes=[slot])

        stg_sub = [[P.slot() for _ in range(2)] for _ in range(2)]

        def precast_ffn(l, f, wi, wo):
            for j in range(NFC):
                sl = P.slot()
                wslot["in", l, f, j] = sl

                def src(qe, st, par, j=j):
                    v = st[:, :].rearrange("p (k h c) -> p k h c", k=NC_, h=2)
                    for h in range(2):
                        col = h * DFF + j * 128
                        srcap = wi[l, :, col:col + 128].rearrange("(k p) c -> p k c", p=128)
                        P.dma(qe, f"pcl{par}{h}", lambda e, h=h, srcap=srcap: e.dma_start(out=v[:, :, h, :], in_=srcap),
                              writes=[stg_sub[par][h]])
                    return stg_sub[par]
                precast(src, WIN[l, f][j].rearrange("p k c -> p (k c)"), NC_ * 256, sl)
            for c in range(NC_):
                sl = P.slot()
                wslot["out", l, f, c] = sl
                for half in range(2):
                    def src(qe, st, par, c=c, half=half):
                        v = st[:, :22 * 128].rearrange("p (k c) -> p k c", k=22)
                        srcap = wo[l, half * 2816:(half + 1) * 2816, c * 128:(c + 1) * 128].rearrange(
                            "(k p) c -> p k c", p=128)
                        P.dma(qe, f"pcl{par}0", lambda e: e.dma_start(out=v, in_=srcap), writes=stg_sub[par])
                        return stg_sub[par]
                    dst = WOUT[l, f][c][:, half * 22:(half + 1) * 22, :].rearrange("p k c -> p (k c)")
                    precast(src, dst, 22 * 128, sl)

        def precast_cols(src2d, col0, dsts, slots):
            nW = len(dsts)
            st, ss = stg.next()
            sb, sbs = stgb.next()
            k = ci[0]
            ci[0] += 1
            par = k % 2
            qe = "sp" if par == 0 else "act"
            v = st[:, :NC_ * 128 * nW].rearrange("p (k w) -> p k w", k=NC_)
            srcap = src2d[:, col0:col0 + 128 * nW].rearrange("(k p) w -> p k w", p=128)
            P.dma(qe, f"pcl{par}0", lambda e: e.dma_start(out=v, in_=srcap), writes=stg_sub[par])
            ce = "pool" if par == 0 else "dve"
            ov = sb[:, :NC_ * 128 * nW].rearrange("p (h k c) -> p h k c", h=nW, k=NC_)
            iv = st[:, :NC_ * 128 * nW].rearrange("p (k h c) -> p h k c", k=NC_, h=nW)
            P.op(ce, lambda e: e.tensor_copy(out=ov, in_=iv), reads=stg_sub[par], writes=[sbs])
            for h in range(nW):
                P.dma(qe, f"pcs{par}", lambda e, h=h: e.dma_start(out=dsts[h], in_=sb[:, h * 2048:(h + 1) * 2048]),
                      reads=[sbs], writes=[slots[h]])

        def precast_mix(l):
            for m in range(0, 56, 2):
                sl = [P.slot(), P.slot()]
                wslot["pr", l, m], wslot["pr", l, m + 1] = sl
                precast_cols(w_pr[l], m * 128, [WPR[l][m + h].rearrange("p k c -> p (k c)") for h in range(2)], sl)
            for m in range(0, 16, 2):
                sl = [P.slot(), P.slot()]
                wslot["sw", l, m], wslot["sw", l, m + 1] = sl
                precast_cols(w_sw[l], m * 128, [WSW[l][m + h].rearrange("p k c -> p (k c)") for h in range(2)], sl)
            for m in range(0, 16, 2):
                sl = [P.slot(), P.slot()]
                wslot["mo", l, m], wslot["mo", l, m + 1] = sl
                precast_cols(w_mo[l], m * 128, [WMO[l][m + h].rearrange("p k c -> p (k c)") for h in range(2)], sl)
            st, ss = stg.next()
            sb, sbs = stgb.next()
            k = ci[0]
            ci[0] += 1
            par = k % 2
            qe = "sp" if par == 0 else "act"
            v = st[:, :NC_ * 16].rearrange("p (k w) -> p k w", k=NC_)
            srcap = w_pr[l][:, 7168:7184].rearrange("(k p) w -> p k w", p=128)
            P.dma(qe, f"pcl{par}0", lambda e: e.dma_start(out=v, in_=srcap), writes=stg_sub[par])
            ce = "pool" if par == 0 else "dve"
            P.op(ce, lambda e: e.tensor_copy(out=sb[:, :NC_ * 16], in_=st[:, :NC_ * 16]), reads=stg_sub[par], writes=[sbs])
            sl = P.slot()
            wslot["ba", l] = sl
            P.dma(qe, f"pcs{par}", lambda e: e.dma_start(out=WBA[l].rearrange("p k c -> p (k c)"), in_=sb[:, :NC_ * 16]),
                  reads=[sbs], writes=[sl])

        for l in range(depth):
            if "mix" in stages:
                precast_mix(l)
            if "ffn1" in stages:
                precast_ffn(l, 1, w1i, w1o)
            if "ffn2" in stages:
                precast_ffn(l, 2, w2i, w2o)

        P.fence()
        ph.close()
        ph = contextlib.ExitStack()
        xtok = Buf(P, ph, "xtok", [128, D], F32, n=2)
        xfst = Buf(P, ph, "xfst", [128, NC_, TT], F32, n=2)
        xbst = Buf(P, ph, "xbst", [128, NC_, TT], BF16, n=2)
        for t in range(ntile):
            ft, fs = xfst.next()
            bt, bs = xbst.next()
            for q4 in range(4):
                tt = t * 4 + q4
                xt, xs = xtok.next()
                P.dma("pool", f"xtok{xtok.i}", lambda e, xt=xt, tt=tt: e.dma_start(out=xt[:], in_=x_in[tt * 128:(tt + 1) * 128, :]),
                      writes=[xs])
                for g in range(4):
                    pb, ps = bk[g % 2]
                    for i in range(4):
                        c = 4 * g + i
                        P.op("pe", lambda e, pb=pb, xt=xt, c=c, i=i: e.transpose(pb[:, i * 128:(i + 1) * 128], xt[:, c * 128:(c + 1) * 128], it[:]),
                             reads=[xs, isl], writes=[ps], signal=(i == 3))
                    pv = pb[:, :].rearrange("p (a b) -> p a b", a=4)
                    P.op("act", lambda e, ft=ft, pv=pv, g=g, q4=q4: e.copy(out=ft[:, 4 * g:4 * g + 4, q4 * 128:(q4 + 1) * 128], in_=pv),
                         reads=[ps], writes=[fs])
                    P.op("dve", lambda e, bt=bt, ft=ft, g=g, q4=q4: e.tensor_copy(out=bt[:, 4 * g:4 * g + 4, q4 * 128:(q4 + 1) * 128],
                                                                                 in_=ft[:, 4 * g:4 * g + 4, q4 * 128:(q4 + 1) * 128]),
                         reads=[fs], writes=[bs])
            tok = slice(t * TT, (t + 1) * TT)
            P.dma("pool", f"xfst{xfst.i}", lambda e, ft=ft, t=t: e.dma_start(out=XF[t], in_=ft[:]),
                  reads=[fs], writes=[xf_slots[t]])
            P.dma("pool", f"xbst{xbst.i}", lambda e, bt=bt, t=t: e.dma_start(out=XB[t], in_=bt[:]),
                  reads=[bs], writes=[xb_slots[t]])

        P.fence()
        ph.close()

        def ln_bufs(ph):
            b = {}
            b["yb"] = Buf(P, ph, "ybuf", [128, NC_, TT], F32)
            b["xnb"] = Buf(P, ph, "xnb", [128, NC_, TT], BF16)
            b["ybf"] = Buf(P, ph, "ybf", [128, TT], BF16, n=2)
            b["ysq"] = Buf(P, ph, "ysq", [128, TT], BF16, n=2)
            b["mean"] = Buf(P, ph, "mean", [128, TT], F32)
            b["rstd"] = Buf(P, ph, "rstd", [128, TT], F32)
            b["tmp"] = Buf(P, ph, "tmpn", [128, TT], F32, n=2)
            b["y_cs"] = [P.slot() for c in range(NC_)]
            for k in ("yb", "xnb", "mean", "rstd"):
                b[k].next()
            return b

        def out_ln(l, which_ln, t, nk, rhs_fn, wload, s_res, b):
            eps = LN_EPS / (ALPHA * ALPHA)
            y_t, y_s = b["yb"].cur()
            y_cs = b["y_cs"]
            xn_t, xn_s = b["xnb"].cur()
            mean_t, mean_s = b["mean"].cur()
            rstd_t, rstd_s = b["rstd"].cur()
            tmp, ybf, ysq = b["tmp"], b["ybf"], b["ysq"]
            S1, S1s = bk[6]
            S2, S2s = bk[7]
            P.dma("pool", "yld", lambda e: e.dma_start(out=y_t[:], in_=XF[t]), reads=[xf_slots[t]], writes=[y_s] + y_cs)
            for c in range(NC_):
                wt, ws = wload(c)
                pb, ps = bk[4 + (c % 2)]
                for j in range(nk):
                    ra, rs = rhs_fn(j)
                    P.op("pe", lambda e, pb=pb, wt=wt, j=j, ra=ra: e.matmul(pb[:, :], wt[:, j, :], ra, start=(j == 0), stop=(j == nk - 1)),
                         reads=[ws, rs], writes=[ps], signal=(j == nk - 1))
                P.op("dve", lambda e, pb=pb, c=c: e.scalar_tensor_tensor(
                    out=y_t[:, c, :], in0=pb[:, :], scalar=s_res, in1=y_t[:, c, :], op0=ALU.mult, op1=ALU.add),
                    reads=[ps, y_cs[c]], writes=[y_cs[c]])
                yt, ys = ybf.next()
                qt, qs = ysq.next()
                P.op("pool", lambda e, yt=yt, c=c: e.tensor_copy(out=yt[:], in_=y_t[:, c, :]), reads=[y_cs[c]], writes=[ys])
                P.op("act", lambda e, qt=qt, c=c: e.activation(out=qt[:], in_=y_t[:, c, :], func=AF.Square), reads=[y_cs[c]], writes=[qs])
                P.op("pe", lambda e, yt=yt, c=c: e.matmul(S1[:, :], ot[:], yt[:], start=(c == 0), stop=(c == NC_ - 1)), reads=[osl, ys], writes=[S1s])
                P.op("pe", lambda e, qt=qt, c=c: e.matmul(S2[:, :], ot[:], qt[:], start=(c == 0), stop=(c == NC_ - 1)), reads=[osl, qs], writes=[S2s])
            P.op("dve", lambda e: e.tensor_scalar(out=mean_t[:], in0=S1[:, :], scalar1=1.0 / D, scalar2=None, op0=ALU.mult), reads=[S1s], writes=[mean_s])
            m2, m2s = tmp.next()
            P.op("dve", lambda e: e.tensor_tensor(out=m2[:], in0=mean_t[:], in1=mean_t[:], op=ALU.mult), reads=[mean_s], writes=[m2s])
            P.op("dve", lambda e: e.scalar_tensor_tensor(out=rstd_t[:], in0=S2[:, :], scalar=1.0 / D, in1=m2[:], op0=ALU.mult, op1=ALU.subtract),
                 reads=[S2s, m2s], writes=[rstd_s])
            P.op("dve", lambda e: e.tensor_scalar(out=rstd_t[:], in0=rstd_t[:], scalar1=eps, scalar2=None, op0=ALU.add), reads=[rstd_s], writes=[rstd_s])
            P.op("act", lambda e: e.activation(out=rstd_t[:], in_=rstd_t[:], func=AF.Sqrt), reads=[rstd_s], writes=[rstd_s])
            P.op("dve", lambda e: e.reciprocal(out=rstd_t[:], in_=rstd_t[:]), reads=[rstd_s], writes=[rstd_s])
            for c in range(NC_):
                tt_, ts_ = tmp.next()
                P.op("dve", lambda e, tt_=tt_, c=c: e.tensor_tensor(out=tt_[:], in0=y_t[:, c, :], in1=mean_t[:], op=ALU.subtract),
                     reads=[y_cs[c], mean_s], writes=[ts_])
                P.op("dve", lambda e, tt_=tt_: e.tensor_tensor(out=tt_[:], in0=tt_[:], in1=rstd_t[:], op=ALU.mult), reads=[ts_, rstd_s], writes=[ts_])
                g_ap = lnvec(l, 2 * which_ln, c)
                b_ap = lnvec(l, 2 * which_ln + 1, c)
                P.op("act", lambda e, tt_=tt_, c=c, g_ap=g_ap, b_ap=b_ap: e.activation(
                    out=y_t[:, c, :], in_=tt_[:], func=AF.Identity, bias=b_ap, scale=g_ap), reads=[ts_, lsl], writes=[y_cs[c]])
                P.op("pool", lambda e, c=c: e.tensor_copy(out=xn_t[:, c, :], in_=y_t[:, c, :]), reads=[y_cs[c]], writes=[xn_s])
            P.dma("pool", "yst", lambda e: e.dma_start(out=XF[t], in_=y_t[:]), reads=[y_s] + y_cs, writes=[xf_slots[t]])
            P.dma("pool", "xnst", lambda e: e.dma_start(out=XB[t], in_=xn_t[:]), reads=[xn_s], writes=[xb_slots[t]])

        def ffn_stage(l, f, which_ln):
            ph = contextlib.ExitStack()
            xT = Buf(P, ph, "xT", [128, NC_, TT], BF16, n=2)
            aT = Buf(P, ph, "aT", [128, NFC, TT], BF16)
            wib = Buf(P, ph, "wib", [128, NC_, 256], BF16, n=3)
            wob = Buf(P, ph, "wob", [128, NFC, 128], BF16, n=2)
            sg = Buf(P, ph, "sg", [128, TT], F32, n=2)
            b = ln_bufs(ph)
            aT_t, aT_s = aT.next()
            aT_cs = [P.slot(f"aT{j}") for j in range(NFC)]
            for t in range(ntile):
                xt, xs = xT.next()
                P.dma("pool", f"xT{xT.i}", lambda e, xt=xt, t=t: e.dma_start(out=xt[:], in_=XB[t]), reads=[xb_slots[t]], writes=[xs])
                for j in range(NFC):
                    wt, ws = wib.next()
                    P.dma("sp", f"wib{wib.i}", lambda e, wt=wt, j=j: e.dma_start(out=wt[:], in_=WIN[l, f][j]),
                          reads=[wslot["in", l, f, j]], writes=[ws])
                    gb, gs = bk[(j % 2) * 2]
                    ub, us = bk[(j % 2) * 2 + 1]
                    for h, (pb, ps) in enumerate(((gb, gs), (ub, us))):
                        for k in range(NC_):
                            P.op("pe", lambda e, pb=pb, wt=wt, xt=xt, k=k, h=h: e.matmul(
                                pb[:, :], wt[:, k, h * 128:(h + 1) * 128], xt[:, k, :], start=(k == 0), stop=(k == NC_ - 1)),
                                reads=[ws, xs], writes=[ps], signal=(k == NC_ - 1))
                    st_, ss_ = sg.next()
                    P.op("act", lambda e, st_=st_, gb=gb: e.activation(out=st_[:], in_=gb[:, :], func=AF.Silu), reads=[gs], writes=[ss_])
                    P.op("dve", lambda e, st_=st_, ub=ub, j=j: e.tensor_tensor(out=aT_t[:, j, :], in0=ub[:, :], in1=st_[:], op=ALU.mult),
                         reads=[us, ss_], writes=[aT_cs[j]])

                def wload(c):
                    wt, ws = wob.next()
                    P.dma("sp", f"wob{wob.i}", lambda e: e.dma_start(out=wt[:], in_=WOUT[l, f][c]), reads=[wslot["out", l, f, c]], writes=[ws])
                    return wt, ws
                out_ln(l, which_ln, t, NFC, lambda j: (aT_t[:, j, :], aT_cs[j]), wload, 0.5 / ALPHA, b)
            P.fence()
            ph.close()

        AQ, AK, AV, DQ, DK, DV, DZ = 0, 8, 16, 24, 32, 40, 48
        NB = SQ // 128
        NQ = SQ // 512
        AX = mybir.AxisListType.X
        if "mix" in stages:
            def cload(name, shape, src, dt=F32):
                bf = Buf(P, stack, name, shape, dt)
                t_, s_ = bf.next()
                P.dma("pool", "const", lambda e: e.dma_start(out=t_[:], in_=src), writes=[s_])
                return t_, s_
            mk_t, mk_s = cload("masks", [128, 640], masks_in[:, :])
            cw_t, cw_s = cload("convw", [128, depth * 96], convw_in[:, :])
            hp_t, hp_s = cload("hp8", [128, depth * 2], hp8_in[:, :])
            lv_t, lv_s = cload("lamv", [128, depth * 256], lamv_in[:, :])
            ng_t, ng_s = cload("normg", [128, depth * 256], normg_in[:, :])
            rc_t, rc_s = cload("ropec", [128, 2], ropec_in[:, :])
            TRIf, LSTR, UINC, CIND, BLKM = mk_t[:, 0:128], mk_t[:, 128:256], mk_t[:, 256:384], mk_t[:, 384:386], mk_t[:, 512:640]
            tribf_t, tribf_s = Buf(P, stack, "tribf", [128, 128], BF16).next()
            P.op("dve", lambda e: e.tensor_copy(out=tribf_t[:], in_=TRIf), reads=[mk_s], writes=[tribf_s])
            zer_t, zer_s = Buf(P, stack, "zerf", [128, 128], F32).next()
            P.op("pool", lambda e: e.memset(zer_t[:], 0.0), writes=[zer_s])
            onf_t, onf_s = Buf(P, stack, "onesf", [128, 128], F32).next()
            P.op("pool", lambda e: e.memset(onf_t[:], 1.0), writes=[onf_s])
            lam_t, lam_s = Buf(P, stack, "lamt", [128, depth * 2], F32).next()
            nA_t, nA_s = Buf(P, stack, "negA", [128, depth], F32).next()
            sc_t, sc_s = Buf(P, stack, "lamsc", [128, 64], F32).next()
            sc2_t, sc2_s = Buf(P, stack, "lamsc2", [128, 4], F32).next()
            for l in range(depth):
                lam_init = 0.8 - 0.6 * math.exp(-0.3 * l)
                for z in range(2):
                    o_ = l * 256 + z * 128
                    P.op("dve", lambda e, o_=o_: e.tensor_tensor(out=sc_t[:], in0=lv_t[:, o_:o_ + 64], in1=lv_t[:, o_ + 64:o_ + 128], op=ALU.mult),
                         reads=[lv_s], writes=[sc_s])
                    P.op("dve", lambda e, z=z: e.reduce_sum(out=sc2_t[:, z:z + 1], in_=sc_t[:], axis=AX), reads=[sc_s], writes=[sc2_s])
                P.op("act", lambda e: e.activation(out=sc2_t[:, 0:2], in_=sc2_t[:, 0:2], func=AF.Exp), reads=[sc2_s], writes=[sc2_s])
                P.op("dve", lambda e: e.tensor_tensor(out=sc2_t[:, 2:3], in0=sc2_t[:, 0:1], in1=sc2_t[:, 1:2], op=ALU.subtract), reads=[sc2_s], writes=[sc2_s])
                P.op("dve", lambda e, l=l, lam_init=lam_init: e.tensor_scalar(out=lam_t[:, 2 * l:2 * l + 1], in0=sc2_t[:, 2:3], scalar1=lam_init, scalar2=None, op0=ALU.add),
                     reads=[sc2_s], writes=[lam_s])
                P.op("dve", lambda e, l=l: e.tensor_scalar(out=lam_t[:, 2 * l + 1:2 * l + 2], in0=lam_t[:, 2 * l:2 * l + 1], scalar1=-1.0, scalar2=None, op0=ALU.mult),
                     reads=[lam_s], writes=[lam_s])
                P.op("dve", lambda e, l=l, lam_init=lam_init: e.tensor_scalar(out=ng_t[:, l * 256:l * 256 + 128], in0=ng_t[:, l * 256:l * 256 + 128],
                                                                              scalar1=1.0 - lam_init, scalar2=None, op0=ALU.mult), reads=[ng_s], writes=[ng_s])
                P.op("act", lambda e, l=l: e.activation(out=nA_t[:, l:l + 1], in_=hp_t[:, 2 * l:2 * l + 1], func=AF.Exp), reads=[hp_s], writes=[nA_s])
                P.op("dve", lambda e, l=l: e.tensor_scalar(out=nA_t[:, l:l + 1], in0=nA_t[:, l:l + 1], scalar1=-1.0, scalar2=None, op0=ALU.mult), reads=[nA_s], writes=[nA_s])

        def rope_tables():
            P.fence()
            ph = contextlib.ExitStack()
            posi, posi_s = Buf(P, ph, "posi", [128, SQ], I32).next()
            pf, pf_s = Buf(P, ph, "posf", [128, SQ], F32).next()
            ti, ti_s = Buf(P, ph, "rti", [128, SQ], I32).next()
            tf, tf_s = Buf(P, ph, "rtf", [128, SQ], F32).next()
            u, u_s = Buf(P, ph, "ru", [128, SQ], F32).next()
            tabs = Buf(P, ph, "rtab", [128, SQ], F32, n=2)
            for q in range(nseq):
                P.dma("pool", "posld", lambda e, q=q: e.dma_start(out=posi[:], in_=pos_in[:, q * SQ:(q + 1) * SQ]), writes=[posi_s])
                P.op("dve", lambda e: e.tensor_copy(out=pf[:], in_=posi[:]), reads=[posi_s], writes=[pf_s])
                P.op("dve", lambda e: e.tensor_scalar(out=pf[:], in0=pf[:], scalar1=rc_t[:, 0:1], scalar2=None, op0=ALU.mult), reads=[pf_s, rc_s], writes=[pf_s])
                for idx, off in ((1, 0.0), (0, 0.25)):
                    P.op("dve", lambda e, off=off: e.tensor_scalar(out=u[:], in0=pf[:], scalar1=off, scalar2=None, op0=ALU.add), reads=[pf_s], writes=[u_s])
                    P.op("dve", lambda e: e.tensor_copy(out=ti[:], in_=u[:]), reads=[u_s], writes=[ti_s])
                    P.op("dve", lambda e: e.tensor_copy(out=tf[:], in_=ti[:]), reads=[ti_s], writes=[tf_s])
                    P.op("dve", lambda e: e.tensor_tensor(out=u[:], in0=u[:], in1=tf[:], op=ALU.subtract), reads=[u_s, tf_s], writes=[u_s])
                    P.op("dve", lambda e: e.tensor_single_scalar(out=tf[:], in_=u[:], scalar=0.5, op=ALU.is_gt), reads=[u_s], writes=[tf_s])
                    P.op("dve", lambda e: e.tensor_tensor(out=u[:], in0=u[:], in1=tf[:], op=ALU.subtract), reads=[u_s, tf_s], writes=[u_s])
                    P.op("dve", lambda e: e.tensor_single_scalar(out=tf[:], in_=u[:], scalar=-0.5, op=ALU.is_lt), reads=[u_s], writes=[tf_s])
                    P.op("dve", lambda e: e.tensor_tensor(out=u[:], in0=u[:], in1=tf[:], op=ALU.add), reads=[u_s, tf_s], writes=[u_s])
                    tb, tbs = tabs.next()
                    P.op("act", lambda e, tb=tb: e.activation(out=tb[:], in_=u[:], func=AF.Sin, scale=2.0 * math.pi), reads=[u_s], writes=[tbs])
                    if idx == 1:
                        P.op("dve", lambda e, tb=tb: e.tensor_scalar(out=tb[:], in0=tb[:], scalar1=rc_t[:, 1:2], scalar2=None, op0=ALU.mult), reads=[tbs, rc_s], writes=[tbs])
                    P.dma("pool", f"ropest{tabs.i}", lambda e, tb=tb, q=q, idx=idx: e.dma_start(out=ROPE[q, idx], in_=tb[:]), reads=[tbs], writes=[rope_slots[q]])
            P.fence()
            ph.close()

        def mixer_stage(l, q, parts=("diff", "gdn")):
            tiles = list(range(q * NQ, (q + 1) * NQ))
            phm = contextlib.ExitStack()
            xTs, xTs_s = Buf(P, phm, "xTs", [128, NC_, SQ], BF16).next()
            for iq, t in enumerate(tiles):
                P.dma("pool", "xTsld", lambda e, iq=iq, t=t: e.dma_start(out=xTs[:, :, iq * 512:(iq + 1) * 512], in_=XB[t]), reads=[xb_slots[t]], writes=[xTs_s])
            wpt = Buf(P, phm, "wpt", [128, NC_, 128], BF16, n=3)

            def wget(kind, m):
                wt, ws = wpt.next()
                src = {"pr": WPR, "sw": WSW}[kind][l][m]
                P.dma("sp", f"wpt{wpt.i}", lambda e: e.dma_start(out=wt[:], in_=src), reads=[wslot[kind, l, m]], writes=[ws])
                return wt, ws

            def proj_fm(wt, ws, iq, pb, ps, mlo=0, mhi=128):
                for k in range(NC_):
                    P.op("pe", lambda e, k=k: e.matmul(pb[0:mhi - mlo, :], wt[:, k, mlo:mhi], xTs[:, k, iq * 512:(iq + 1) * 512],
                                                      start=(k == 0), stop=(k == NC_ - 1)),
                         reads=[ws, xTs_s], writes=[ps], signal=(k == NC_ - 1))

            def store_mix(oT, oT_s, j):
                for iq, t in enumerate(tiles):
                    P.dma("pool", "mixst", lambda e, iq=iq, t=t: e.dma_start(out=MIX[t][:, j, :], in_=oT[:, iq * 512:(iq + 1) * 512]),
                          reads=[oT_s], writes=[mix_slots[t][j]])

            if "diff" in parts:
                ph = contextlib.ExitStack()
                Ct, Ct_s = Buf(P, ph, "Ct", [128, SQ], F32).next()
                St, St_s = Buf(P, ph, "St", [128, SQ], F32).next()
                P.dma("pool", "ropeld", lambda e: e.dma_start(out=Ct[:], in_=ROPE[q, 0]), reads=[rope_slots[q]], writes=[Ct_s])
                P.dma("pool", "ropeld", lambda e: e.dma_start(out=St[:], in_=ROPE[q, 1]), reads=[rope_slots[q]], writes=[St_s])
                qT, qT_s = Buf(P, ph, "qT", [128, SQ], BF16).next()
                kT, kT_s = Buf(P, ph, "kT", [128, SQ], BF16).next()
                va, va_s = Buf(P, ph, "vaug", [128, NB, 132], BF16).next()
                P.op("pool", lambda e: e.memset(va[:, :, 128:132], 1.0), writes=[va_s])
                Ea, _ = Buf(P, ph, "Eall", [128, NB, 512], BF16).next()
                Es = [P.slot() for _ in range(NB)]
                r1 = Buf(P, ph, "r1", [128, 512], F32, n=2)
                r2 = Buf(P, ph, "r2", [128, 512], F32, n=2)
                o1, o1_s = Buf(P, ph, "o1", [128, 4, 128], F32).next()
                ofb = Buf(P, ph, "of", [128, 128], F32, n=2)
                osq = Buf(P, ph, "osq", [128, 128], F32, n=2)
                onb = Buf(P, ph, "onb", [128, 128], F32, n=2)
                sm = Buf(P, ph, "smd", [128, 4], F32, n=4)
                oTb = Buf(P, ph, "oTd", [128, SQ], BF16, n=2)
                nlam = lam_t[:, 2 * l + 1:2 * l + 2]
                gdr = ng_t[:, l * 256:l * 256 + 128]
                for h in range(8):
                    for (ma, mb, dst, dst_s) in ((AQ + h, h, qT, qT_s), (AK + h, 8 + h, kT, kT_s)):
                        wa, was = wget("pr", ma)
                        wb, wbs = wget("sw", mb)
                        for iq in range(NQ):
                            tok = slice(iq * 512, (iq + 1) * 512)
                            pa, pas = bk[0]
                            pb_, pbs = bk[1]
                            proj_fm(wa, was, iq, pa, pas)
                            proj_fm(wb, wbs, iq, pb_, pbs)
                            t1, t1s = r1.next()
                            t2, t2s = r2.next()
                            P.op("dve", lambda e, t1=t1, tok=tok: e.tensor_tensor(out=t1[:], in0=pb_[:, :], in1=St[:, tok], op=ALU.mult), reads=[St_s], writes=[t1s, pbs])
                            P.op("dve", lambda e, t2=t2, tok=tok: e.tensor_tensor(out=t2[:], in0=pa[:, :], in1=Ct[:, tok], op=ALU.mult), reads=[Ct_s], writes=[t2s, pas])
                            P.op("pool", lambda e, t1=t1, t2=t2, dst=dst, tok=tok: e.tensor_tensor(out=dst[:, tok], in0=t1[:], in1=t2[:], op=ALU.add),
                                 reads=[t1s, t2s], writes=[dst_s])
                    wv, wvs = wget("pr", AV + h)
                    for g4 in range(NQ):
                        pb, ps = bk[2 + g4 % 2]
                        for i4 in range(4):
                            tt = g4 * 4 + i4
                            for k in range(NC_):
                                P.op("pe", lambda e, pb=pb, i4=i4, tt=tt, k=k, wv=wv: e.matmul(pb[:, i4 * 128:(i4 + 1) * 128], xTs[:, k, tt * 128:(tt + 1) * 128], wv[:, k, :],
                                                                                      start=(k == 0), stop=(k == NC_ - 1)),
                                     reads=[wvs, xTs_s], writes=[ps], signal=(k == NC_ - 1))
                        P.op("act", lambda e, pb=pb, g4=g4: e.copy(out=va[:, g4 * 4:(g4 + 1) * 4, 0:128], in_=pb[:, :].rearrange("p (a b) -> p a b", a=4)),
                             reads=[], writes=[va_s, ps])
                    oT, oT_s = oTb.next()
                    for J in range(NQ):
                        for c in range(2):
                            cs = slice(c * 64, (c + 1) * 64)
                            for i in range(4 * J + 4):
                                sb_, ss_ = bk[i % 2]
                                P.op("pe", lambda e, sb_=sb_, i=i, cs=cs, J=J: e.matmul(sb_[:, :], kT[cs, i * 128:(i + 1) * 128], qT[cs, J * 512:(J + 1) * 512], start=True, stop=True),
                                     reads=[kT_s, qT_s], writes=[ss_])
                                P.op("act", lambda e, sb_=sb_, i=i: e.activation(out=Ea[:, i, :], in_=sb_[:, :], func=AF.Exp, scale=0.125), reads=[], writes=[Es[i], ss_])
                                r = i - 4 * J
                                if r >= 0:
                                    P.op("pool", lambda e, i=i, r=r: e.tensor_tensor(out=Ea[:, i, r * 128:(r + 1) * 128], in0=Ea[:, i, r * 128:(r + 1) * 128], in1=tribf_t[:], op=ALU.mult),
                                         reads=[tribf_s], writes=[Es[i]])
                            for u in range(4):
                                ob, obs = bk[2 + u]
                                last = 4 * J + u
                                for i in range(last + 1):
                                    P.op("pe", lambda e, ob=ob, i=i, u=u, last=last: e.matmul(ob[:, 0:129], Ea[:, i, u * 128:(u + 1) * 128], va[:, i, 0:129], start=(i == 0), stop=(i == last)),
                                         reads=[Es[i], va_s], writes=[obs], signal=(i == last))
                                s4, s4s = sm.next()
                                P.op("dve", lambda e, ob=ob, s4=s4: e.reciprocal(out=s4[:, 0:1], in_=ob[:, 128:129]), reads=[], writes=[s4s, obs])
                                if c == 0:
                                    P.op("act", lambda e, ob=ob, s4=s4, u=u: e.activation(out=o1[:, u, :], in_=ob[:, 0:128], func=AF.Identity, scale=s4[:, 0:1]),
                                         reads=[s4s], writes=[o1_s, obs])
                                else:
                                    P.op("dve", lambda e, s4=s4: e.tensor_scalar(out=s4[:, 1:2], in0=s4[:, 0:1], scalar1=nlam, scalar2=None, op0=ALU.mult), reads=[lam_s], writes=[s4s])
                                    of, ofs = ofb.next()
                                    P.op("dve", lambda e, ob=ob, s4=s4, u=u, of=of: e.scalar_tensor_tensor(out=of[:], in0=ob[:, 0:128], scalar=s4[:, 1:2], in1=o1[:, u, :],
                                                                                                              op0=ALU.mult, op1=ALU.add), reads=[s4s, o1_s], writes=[ofs, obs])
                                    sq, sqs = osq.next()
                                    P.op("dve", lambda e, sq=sq, of=of: e.tensor_tensor(out=sq[:], in0=of[:], in1=of[:], op=ALU.mult), reads=[ofs], writes=[sqs])
                                    P.op("dve", lambda e, sq=sq, s4=s4: e.reduce_sum(out=s4[:, 2:3], in_=sq[:], axis=AX), reads=[sqs], writes=[s4s])
                                    P.op("dve", lambda e, s4=s4: e.tensor_scalar(out=s4[:, 2:3], in0=s4[:, 2:3], scalar1=1.0 / 128, scalar2=1e-5, op0=ALU.mult, op1=ALU.add), reads=[s4s], writes=[s4s])
                                    P.op("act", lambda e, s4=s4: e.activation(out=s4[:, 2:3], in_=s4[:, 2:3], func=AF.Sqrt), reads=[s4s], writes=[s4s])
                                    P.op("dve", lambda e, s4=s4: e.reciprocal(out=s4[:, 3:4], in_=s4[:, 2:3]), reads=[s4s], writes=[s4s])
                                    on, ons = onb.next()
                                    P.op("dve", lambda e, on=on, of=of, s4=s4: e.scalar_tensor_tensor(out=on[:], in0=of[:], scalar=s4[:, 3:4], in1=gdr, op0=ALU.mult, op1=ALU.mult),
                                         reads=[ofs, s4s, ng_s], writes=[ons])
                                    tb, tbs = bk[6]
                                    P.op("pe", lambda e, tb=tb, on=on, u=u: e.transpose(tb[:, u * 128:(u + 1) * 128], on[:], it[:]), reads=[ons, isl], writes=[tbs])
                            if c == 1:
                                tb, tbs = bk[6]
                                P.op("act", lambda e, tb=tb, J=J, oT=oT: e.copy(out=oT[:, J * 512:(J + 1) * 512], in_=tb[:, :]), reads=[], writes=[oT_s, tbs])
                    store_mix(oT, oT_s, h)
                P.fence()
                ph.close()

            if "gdn" in parts:
                ph = contextlib.ExitStack()

                def tmb(name):
                    return Buf(P, ph, name, [128, NB, 8], F32).next()
                btm, btm_s = tmb("btm")
                nbtm, nbtm_s = tmb("nbtm")
                gtm, gtm_s = tmb("gtm")
                gctm, gctm_s = tmb("gctm")
                gltm, gltm_s = tmb("gltm")
                egtm, egtm_s = tmb("egtm")
                kdtm, kdtm_s = tmb("kdtm")
                bkeg, bkeg_s = tmb("bkeg")
                ph2 = contextlib.ExitStack()
                wba, wba_s = Buf(P, ph2, "wba", [128, NC_, 16], BF16).next()
                P.dma("sp", "wba", lambda e: e.dma_start(out=wba[:], in_=WBA[l]), reads=[wslot["ba", l]], writes=[wba_s])
                bfm, bfm_s = Buf(P, ph2, "bfm", [8, SQ], F32).next()
                gfm, gfm_s = Buf(P, ph2, "gfm", [8, SQ], F32).next()
                for iq in range(NQ):
                    tok = slice(iq * 512, (iq + 1) * 512)
                    pa, pas = bk[0]
                    pb_, pbs = bk[1]
                    proj_fm(wba, wba_s, iq, pa, pas, 0, 8)
                    proj_fm(wba, wba_s, iq, pb_, pbs, 8, 16)
                    P.op("act", lambda e, tok=tok: e.activation(out=bfm[:, tok], in_=pa[0:8, :], func=AF.Sigmoid), reads=[], writes=[bfm_s, pas])
                    P.op("act", lambda e, tok=tok: e.activation(out=gfm[:, tok], in_=pb_[0:8, :], func=AF.Exp, bias=hp_t[0:8, 2 * l + 1:2 * l + 2], scale=1.0),
                         reads=[hp_s], writes=[gfm_s, pbs])
                P.op("dve", lambda e: e.tensor_scalar(out=gfm[:, :], in0=gfm[:, :], scalar1=1.0, scalar2=None, op0=ALU.add), reads=[gfm_s], writes=[gfm_s])
                P.op("act", lambda e: e.activation(out=gfm[:, :], in_=gfm[:, :], func=AF.Ln), reads=[gfm_s], writes=[gfm_s])
                P.op("dve", lambda e: e.tensor_scalar(out=gfm[:, :], in0=gfm[:, :], scalar1=nA_t[0:8, l:l + 1], scalar2=None, op0=ALU.mult), reads=[gfm_s, nA_s], writes=[gfm_s])
                for (src, src_s, dst, dst_s, bki) in ((bfm, bfm_s, btm, btm_s, 2), (gfm, gfm_s, gtm, gtm_s, 3)):
                    pb, ps = bk[bki]
                    for b in range(NB):
                        P.op("pe", lambda e, pb=pb, src=src, b=b: e.transpose(pb[:, b * 8:(b + 1) * 8], src[0:8, b * 128:(b + 1) * 128], it[0:8, 0:8]),
                             reads=[src_s, isl], writes=[ps], signal=(b == NB - 1))
                    P.op("dve", lambda e, pb=pb, dst=dst: e.tensor_copy(out=dst[:, :, :].rearrange("p a b -> p (a b)"), in_=pb[:, 0:NB * 8]), reads=[], writes=[dst_s, ps])
                for (msk, dst, dst_s, bki) in ((UINC, gctm, gctm_s, 4), (BLKM, gltm, gltm_s, 5)):
                    pb, ps = bk[bki]
                    for b in range(NB):
                        P.op("pe", lambda e, pb=pb, msk=msk, b=b: e.matmul(pb[:, b * 8:(b + 1) * 8], msk, gtm[:, b, :], start=True, stop=True),
                             reads=[mk_s, gtm_s], writes=[ps], signal=(b == NB - 1))
                    P.op("dve", lambda e, pb=pb, dst=dst: e.tensor_copy(out=dst[:, :, :].rearrange("p a b -> p (a b)"), in_=pb[:, 0:NB * 8]), reads=[], writes=[dst_s, ps])
                fl = lambda t_: t_[:, :, :].rearrange("p a b -> p (a b)")
                P.op("act", lambda e: e.activation(out=fl(egtm), in_=fl(gctm), func=AF.Exp), reads=[gctm_s], writes=[egtm_s])
                P.op("dve", lambda e: e.tensor_tensor(out=fl(kdtm), in0=fl(gltm), in1=fl(gctm), op=ALU.subtract), reads=[gltm_s, gctm_s], writes=[kdtm_s])
                P.op("act", lambda e: e.activation(out=fl(kdtm), in_=fl(kdtm), func=AF.Exp), reads=[kdtm_s], writes=[kdtm_s])
                P.op("dve", lambda e: e.tensor_scalar(out=fl(nbtm), in0=fl(btm), scalar1=-1.0, scalar2=None, op0=ALU.mult), reads=[btm_s], writes=[nbtm_s])
                P.op("dve", lambda e: e.tensor_tensor(out=fl(bkeg), in0=fl(btm), in1=fl(egtm), op=ALU.mult), reads=[btm_s, egtm_s], writes=[bkeg_s])
                P.fence()
                ph2.close()
                upad, upad_s = Buf(P, ph, "upad", [128, SQ + 4], F32).next()
                P.op("pool", lambda e: e.memset(upad[:, 0:3], 0.0), writes=[upad_s])
                cv, cv_s = Buf(P, ph, "cv", [128, SQ], F32).next()
                gqT, gqT_s = Buf(P, ph, "gqT", [128, SQ], BF16).next()
                gkT, gkT_s = Buf(P, ph, "gkT", [128, SQ], BF16).next()
                qdT, qdT_s = Buf(P, ph, "qdT", [128, SQ], BF16).next()
                ktm, ktm_s = Buf(P, ph, "ktm", [128, NB, 128], BF16).next()
                vtm, vtm_s = Buf(P, ph, "vtm", [128, NB, 128], BF16).next()
                otm, otm_s = Buf(P, ph, "otm", [128, NB, 128], F32).next()
                goTb = Buf(P, ph, "oTg", [128, SQ], BF16, n=2)
                sqb = Buf(P, ph, "sqb", [128, 512], BF16, n=2)
                gbt = Buf(P, ph, "gbt", [128, 128], F32, n=2)
                EGr, EGr_s = Buf(P, ph, "EGr", [128, 512], F32).next()
                cdq, cdq_s = Buf(P, ph, "cdq", [128, 8], F32).next()
                Dt, Dt_s = Buf(P, ph, "Dt", [128, 4, 128], F32).next()
                DTt, DTt_s = Buf(P, ph, "DTt", [128, 4, 128], F32).next()
                Mf, Mf_s = Buf(P, ph, "Mf", [128, 4, 128], F32).next()
                Yf, Yf_s = Buf(P, ph, "Yf", [128, 4, 128], F32).next()
                Pbs = Buf(P, ph, "Pb", [128, 4, 128], BF16, n=2)
                Qbs = Buf(P, ph, "Qb", [128, 4, 128], BF16, n=2)
                Yb, Yb_s = Buf(P, ph, "Yb", [128, 4, 128], BF16).next()
                aTt, aTt_s = Buf(P, ph, "attnT", [128, 4, 128], BF16).next()
                rv, rv_s = Buf(P, ph, "rhsv", [128, 4, 128], BF16).next()
                rk, rk_s = Buf(P, ph, "rhsk", [128, 4, 128], BF16).next()
                kd, kd_s = Buf(P, ph, "kdec", [128, 4, 128], BF16).next()
                nw, nw_s = Buf(P, ph, "nwT", [128, 4, 128], BF16).next()
                vn, vn_s = Buf(P, ph, "vnew", [128, 128], BF16).next()
                Sf, Sf_s = Buf(P, ph, "Sf", [128, 128], F32).next()
                Sb, Sb_s = Buf(P, ph, "Sb", [128, 128], BF16).next()
                gonb = Buf(P, ph, "gonb", [128, 128], F32, n=2)
                gosq = Buf(P, ph, "gosq", [128, 128], F32, n=2)
                gsm = Buf(P, ph, "smg", [128, 4], F32, n=4)
                gdl = ng_t[:, l * 256 + 128:l * 256 + 256]
                flq = lambda t_: t_[:, :, :].rearrange("p a b -> p (a b)")

                def conv_silu(m, ch):
                    wt, ws = wget("pr", m)
                    P.op("pool", lambda e: e.memset(upad[:, 0:3], 0.0), writes=[upad_s])
                    for iq in range(NQ):
                        pb, ps = bk[iq % 2]
                        proj_fm(wt, ws, iq, pb, ps)
                        P.op("act", lambda e, pb=pb, iq=iq: e.copy(out=upad[:, 3 + iq * 512:3 + (iq + 1) * 512], in_=pb[:, :]), reads=[], writes=[upad_s, ps])
                    base = (l * 24 + ch) * 4
                    P.op("dve", lambda e: e.tensor_scalar(out=cv[:], in0=upad[:, 0:SQ], scalar1=cw_t[:, base:base + 1], scalar2=None, op0=ALU.mult),
                         reads=[upad_s, cw_s], writes=[cv_s])
                    for j in range(1, 4):
                        P.op("dve", lambda e, j=j: e.scalar_tensor_tensor(out=cv[:], in0=upad[:, j:j + SQ], scalar=cw_t[:, base + j:base + j + 1], in1=cv[:],
                                                                          op0=ALU.mult, op1=ALU.add), reads=[upad_s, cw_s, cv_s], writes=[cv_s])
                    P.op("act", lambda e: e.activation(out=cv[:], in_=cv[:], func=AF.Silu), reads=[cv_s], writes=[cv_s])

                def l2n():
                    for iq in range(NQ):
                        tok = slice(iq * 512, (iq + 1) * 512)
                        sq, sqs = sqb.next()
                        P.op("act", lambda e, sq=sq, tok=tok: e.activation(out=sq[:], in_=cv[:, tok], func=AF.Square), reads=[cv_s], writes=[sqs])
                        pb, ps = bk[iq % 2]
                        P.op("pe", lambda e, pb=pb, sq=sq: e.matmul(pb[:, :], ot[:], sq[:], start=True, stop=True), reads=[osl, sqs], writes=[ps])
                        P.op("dve", lambda e, pb=pb, tok=tok: e.tensor_scalar(out=upad[:, tok], in0=pb[:, :], scalar1=1e-6, scalar2=None, op0=ALU.add), reads=[], writes=[upad_s, ps])
                    P.op("act", lambda e: e.activation(out=upad[:, 0:SQ], in_=upad[:, 0:SQ], func=AF.Sqrt), reads=[upad_s], writes=[upad_s])
                    P.op("dve", lambda e: e.reciprocal(out=upad[:, 0:SQ], in_=upad[:, 0:SQ]), reads=[upad_s], writes=[upad_s])

                def to_tm(dst, dst_s):
                    for b4 in range(NB // 4):
                        pb, ps = bk[2 + b4 % 2]
                        for i4 in range(4):
                            b = b4 * 4 + i4
                            P.op("pe", lambda e, pb=pb, i4=i4, b=b: e.transpose(pb[:, i4 * 128:(i4 + 1) * 128], cv[:, b * 128:(b + 1) * 128], it[:]),
                                 reads=[cv_s, isl], writes=[ps], signal=(i4 == 3))
                        P.op("act", lambda e, pb=pb, b4=b4: e.copy(out=dst[:, b4 * 4:(b4 + 1) * 4, :], in_=pb[:, :].rearrange("p (a b) -> p a b", a=4)),
                             reads=[], writes=[dst_s, ps])

                for h in range(8):
                    conv_silu(DQ + h, h)
                    l2n()
                    P.op("dve", lambda e: e.scalar_tensor_tensor(out=gqT[:], in0=cv[:], scalar=128 ** -0.5, in1=upad[:, 0:SQ], op0=ALU.mult, op1=ALU.mult),
                         reads=[cv_s, upad_s], writes=[gqT_s])
                    conv_silu(DK + h, 8 + h)
                    l2n()
                    P.op("dve", lambda e: e.tensor_tensor(out=cv[:], in0=cv[:], in1=upad[:, 0:SQ], op=ALU.mult), reads=[cv_s, upad_s], writes=[cv_s])
                    P.op("pool", lambda e: e.tensor_copy(out=gkT[:], in_=cv[:]), reads=[cv_s], writes=[gkT_s])
                    to_tm(ktm, ktm_s)
                    conv_silu(DV + h, 16 + h)
                    to_tm(vtm, vtm_s)
                    P.op("pool", lambda e: e.memset(Sf[:], 0.0), writes=[Sf_s])
                    P.op("pool", lambda e: e.memset(Sb[:], 0.0), writes=[Sb_s])
                    hs = slice(h, h + 1)
                    for qd in range(NB // 4):
                        qtok = slice(qd * 512, (qd + 1) * 512)
                        g0, g0s = bk[0]
                        g1, g1s = bk[1]
                        for j4 in range(4):
                            b = qd * 4 + j4
                            gb, gbs = gbt.next()
                            P.op("dve", lambda e, hs=hs, gb=gb, b=b: e.tensor_scalar(out=gb[:], in0=onf_t[:], scalar1=gtm[:, b, hs], scalar2=None, op0=ALU.mult),
                                 reads=[onf_s, gtm_s], writes=[gbs])
                            P.op("pe", lambda e, gb=gb, j4=j4: e.matmul(g0[:, j4 * 128:(j4 + 1) * 128], gb[:], UINC, start=True, stop=True), reads=[gbs, mk_s], writes=[g0s])
                            P.op("pe", lambda e, gb=gb, j4=j4: e.matmul(g1[:, j4 * 2:(j4 + 1) * 2], gb[:], CIND, start=True, stop=True), reads=[gbs, mk_s], writes=[g1s])
                        P.op("act", lambda e: e.activation(out=EGr[:], in_=g0[:, :], func=AF.Exp), reads=[], writes=[EGr_s, g0s])
                        P.op("act", lambda e: e.activation(out=cdq[:], in_=g1[:, 0:8], func=AF.Exp), reads=[], writes=[cdq_s, g1s])
                        P.op("dve", lambda e, qtok=qtok: e.tensor_tensor(out=qdT[:, qtok], in0=gqT[:, qtok], in1=EGr[:], op=ALU.mult), reads=[gqT_s, EGr_s], writes=[qdT_s])
                        for j4 in range(4):
                            b = qd * 4 + j4
                            gsl = slice(j4 * 128, (j4 + 1) * 128)
                            P.op("dve", lambda e, hs=hs, j4=j4, b=b, gsl=gsl: e.scalar_tensor_tensor(out=Dt[:, j4, :], in0=g0[:, gsl], scalar=gctm[:, b, hs], in1=zer_t[:],
                                                                                         op0=ALU.subtract, op1=ALU.max), reads=[gctm_s, zer_s], writes=[Dt_s, g0s])
                            P.op("dve", lambda e, hs=hs, j4=j4, b=b, gsl=gsl: e.scalar_tensor_tensor(out=DTt[:, j4, :], in0=g0[:, gsl], scalar=gctm[:, b, hs], in1=zer_t[:],
                                                                                         op0=ALU.subtract, op1=ALU.min), reads=[gctm_s, zer_s], writes=[DTt_s, g0s])
                        P.op("act", lambda e: e.activation(out=flq(Dt), in_=flq(Dt), func=AF.Exp, scale=-1.0), reads=[Dt_s], writes=[Dt_s])
                        P.op("act", lambda e: e.activation(out=flq(DTt), in_=flq(DTt), func=AF.Exp), reads=[DTt_s], writes=[DTt_s])
                        for j4 in range(4):
                            P.op("pool", lambda e, j4=j4: e.tensor_tensor(out=Dt[:, j4, :], in0=Dt[:, j4, :], in1=LSTR, op=ALU.mult), reads=[mk_s], writes=[Dt_s])
                            P.op("pool", lambda e, j4=j4: e.tensor_tensor(out=DTt[:, j4, :], in0=DTt[:, j4, :], in1=UINC, op=ALU.mult), reads=[mk_s], writes=[DTt_s])
                        a2, a2s = bk[2]
                        a3, a3s = bk[3]
                        Pb, Pb_s = Pbs.next()
                        Qb, Qb_s = Qbs.next()
                        for j4 in range(4):
                            blk = slice((qd * 4 + j4) * 128, (qd * 4 + j4 + 1) * 128)
                            P.op("pe", lambda e, j4=j4, blk=blk: e.matmul(a2[:, j4 * 128:(j4 + 1) * 128], gkT[:, blk], gkT[:, blk], start=True, stop=True),
                                 reads=[gkT_s], writes=[a2s], signal=(j4 == 3))
                        for j4 in range(4):
                            b = qd * 4 + j4
                            P.op("dve", lambda e, hs=hs, j4=j4, b=b: e.scalar_tensor_tensor(out=Mf[:, j4, :], in0=a2[:, j4 * 128:(j4 + 1) * 128], scalar=nbtm[:, b, hs], in1=Dt[:, j4, :],
                                                                                     op0=ALU.mult, op1=ALU.mult), reads=[nbtm_s, Dt_s], writes=[Mf_s, a2s])
                        P.op("pool", lambda e, Pb=Pb: e.tensor_copy(out=flq(Pb), in_=flq(Mf)), reads=[Mf_s], writes=[Pb_s])
                        for j4 in range(4):
                            P.op("pe", lambda e, j4=j4: e.transpose(a3[:, j4 * 128:(j4 + 1) * 128], Mf[:, j4, :], it[:]), reads=[Mf_s, isl], writes=[a3s], signal=(j4 == 3))
                        P.op("act", lambda e, Qb=Qb: e.copy(out=flq(Qb), in_=a3[:, :]), reads=[], writes=[Qb_s, a3s])
                        for j4 in range(4):
                            P.op("dve", lambda e, j4=j4: e.tensor_tensor(out=Yf[:, j4, :], in0=a3[:, j4 * 128:(j4 + 1) * 128], in1=it[:], op=ALU.add), reads=[isl], writes=[Yf_s, a3s])
                        P.op("pool", lambda e: e.tensor_copy(out=flq(Yb), in_=flq(Yf)), reads=[Yf_s], writes=[Yb_s])
                        for n in range(5):
                            p4, p4s = bk[4]
                            p5, p5s = bk[5]
                            p6, p6s = bk[6]
                            Pn, Pn_s = Pbs.next()
                            for j4 in range(4):
                                P.op("pe", lambda e, j4=j4, Qb=Qb, Pb=Pb: e.matmul(p4[:, j4 * 128:(j4 + 1) * 128], Qb[:, j4, :], Pb[:, j4, :], start=True, stop=True),
                                     reads=[Qb_s, Pb_s], writes=[p4s], signal=(j4 == 3))
                            if n < 4:
                                Qn, Qn_s = Qbs.next()
                                for j4 in range(4):
                                    P.op("pe", lambda e, j4=j4, Qb=Qb, Pb=Pb: e.matmul(p5[:, j4 * 128:(j4 + 1) * 128], Pb[:, j4, :], Qb[:, j4, :], start=True, stop=True),
                                         reads=[Qb_s, Pb_s], writes=[p5s], signal=(j4 == 3))
                            P.op("dve", lambda e, Pn=Pn: e.tensor_copy(out=flq(Pn), in_=p4[:, :]), reads=[], writes=[Pn_s, p4s])
                            if n < 4:
                                P.op("act", lambda e, Qn=Qn: e.copy(out=flq(Qn), in_=p5[:, :]), reads=[], writes=[Qn_s, p5s])
                            for j4 in range(4):
                                P.op("pe", lambda e, j4=j4, Pn=Pn: e.matmul(p6[:, j4 * 128:(j4 + 1) * 128], Pn[:, j4, :], Yb[:, j4, :], start=True, stop=True),
                                     reads=[Pn_s, Yb_s], writes=[p6s], signal=(j4 == 3))
                            P.op("dve", lambda e: e.tensor_tensor(out=flq(Yf), in0=p6[:, :], in1=flq(Yf), op=ALU.add), reads=[Yf_s], writes=[Yf_s, p6s])
                            P.op("pool", lambda e: e.tensor_copy(out=flq(Yb), in_=flq(Yf)), reads=[Yf_s], writes=[Yb_s])
                            Pb, Pb_s = Pn, Pn_s
                            if n < 4:
                                Qb, Qb_s = Qn, Qn_s
                        for j4 in range(4):
                            blk = slice((qd * 4 + j4) * 128, (qd * 4 + j4 + 1) * 128)
                            P.op("pe", lambda e, j4=j4, blk=blk: e.matmul(a2[:, j4 * 128:(j4 + 1) * 128], gkT[:, blk], gqT[:, blk], start=True, stop=True),
                                 reads=[gkT_s, gqT_s], writes=[a2s], signal=(j4 == 3))
                        P.op("dve", lambda e: e.tensor_tensor(out=flq(aTt), in0=a2[:, :], in1=flq(DTt), op=ALU.mult), reads=[DTt_s], writes=[aTt_s, a2s])
                        for j4 in range(4):
                            b = qd * 4 + j4
                            P.op("pool", lambda e, hs=hs, j4=j4, b=b: e.tensor_scalar(out=rv[:, j4, :], in0=vtm[:, b, :], scalar1=btm[:, b, hs], scalar2=None, op0=ALU.mult),
                                 reads=[vtm_s, btm_s], writes=[rv_s])
                            P.op("pool", lambda e, hs=hs, j4=j4, b=b: e.tensor_scalar(out=rk[:, j4, :], in0=ktm[:, b, :], scalar1=bkeg[:, b, hs], scalar2=None, op0=ALU.mult),
                                 reads=[ktm_s, bkeg_s], writes=[rk_s])
                            P.op("pool", lambda e, hs=hs, j4=j4, b=b: e.tensor_scalar(out=kd[:, j4, :], in0=ktm[:, b, :], scalar1=kdtm[:, b, hs], scalar2=None, op0=ALU.mult),
                                 reads=[ktm_s, kdtm_s], writes=[kd_s])
                        for j4 in range(4):
                            P.op("pe", lambda e, j4=j4: e.matmul(a3[:, j4 * 128:(j4 + 1) * 128], rk[:, j4, :], Yb[:, j4, :], start=True, stop=True),
                                 reads=[rk_s, Yb_s], writes=[a3s], signal=(j4 == 3))
                        P.op("act", lambda e: e.mul(out=flq(nw), in_=a3[:, :], mul=-1.0), reads=[], writes=[nw_s, a3s])
                        for j4 in range(4):
                            b = qd * 4 + j4
                            blk = slice(b * 128, (b + 1) * 128)
                            for x in range(2):
                                R = slice(64 * x, 64 * x + 64)
                                pv, pvs = bk[7]
                                po, pos_ = bk[4 + x]
                                pst, psts = bk[6]
                                P.op("pe", lambda e, j4=j4: e.matmul(pv[:, 0:128], Yb[:, j4, :], rv[:, j4, :], start=True, stop=False), reads=[Yb_s, rv_s], writes=[pvs], signal=False)
                                P.op("pe", lambda e, j4=j4: e.matmul(pv[:, 0:128], nw[:, j4, :], Sb[:], start=False, stop=True), reads=[nw_s, Sb_s], writes=[pvs])
                                P.op("dve", lambda e, R=R: e.tensor_copy(out=vn[R, :], in_=pv[R, 0:128]), reads=[], writes=[vn_s, pvs])
                                P.op("pe", lambda e, blk=blk, po=po: e.matmul(po[:, 0:128], qdT[:, blk], Sb[:], start=True, stop=False), reads=[qdT_s, Sb_s], writes=[pos_], signal=False)
                                P.op("pe", lambda e, j4=j4, R=R, po=po: e.matmul(po[:, 0:128], aTt[R, j4, :], vn[R, :], start=False, stop=True), reads=[aTt_s, vn_s], writes=[pos_])
                                P.op("act", lambda e, R=R, b=b, po=po: e.copy(out=otm[R, b, :], in_=po[R, 0:128]), reads=[], writes=[otm_s, pos_])
                                P.op("pe", lambda e, j4=j4, R=R: e.matmul(pst[:, 0:128], kd[R, j4, :], vn[R, :], start=True, stop=True), reads=[kd_s, vn_s], writes=[psts])
                                ci_ = 2 * j4 + x
                                P.op("dve", lambda e, ci_=ci_: e.scalar_tensor_tensor(out=Sf[:], in0=Sf[:], scalar=cdq[:, ci_:ci_ + 1], in1=pst[:, 0:128], op0=ALU.mult, op1=ALU.add),
                                     reads=[cdq_s], writes=[Sf_s, psts])
                                P.op("act", lambda e: e.copy(out=Sb[:], in_=Sf[:]), reads=[Sf_s], writes=[Sb_s])
                    wz, wzs = wget("pr", DZ + h)
                    for iq in range(NQ):
                        pb, ps = bk[iq % 2]
                        proj_fm(wz, wzs, iq, pb, ps)
                        P.op("act", lambda e, pb=pb, iq=iq: e.activation(out=cv[:, iq * 512:(iq + 1) * 512], in_=pb[:, :], func=AF.Silu), reads=[], writes=[cv_s, ps])
                    goT, goT_s = goTb.next()
                    for b4 in range(NB // 4):
                        tb, tbs = bk[2 + b4 % 2]
                        for i4 in range(4):
                            b = b4 * 4 + i4
                            sq, sqs = gosq.next()
                            s4, s4s = gsm.next()
                            P.op("dve", lambda e, sq=sq, b=b: e.tensor_tensor(out=sq[:], in0=otm[:, b, :], in1=otm[:, b, :], op=ALU.mult), reads=[otm_s], writes=[sqs])
                            P.op("dve", lambda e, sq=sq, s4=s4: e.reduce_sum(out=s4[:, 0:1], in_=sq[:], axis=AX), reads=[sqs], writes=[s4s])
                            P.op("dve", lambda e, s4=s4: e.tensor_scalar(out=s4[:, 0:1], in0=s4[:, 0:1], scalar1=1.0 / 128, scalar2=1e-6, op0=ALU.mult, op1=ALU.add), reads=[s4s], writes=[s4s])
                            P.op("act", lambda e, s4=s4: e.activation(out=s4[:, 0:1], in_=s4[:, 0:1], func=AF.Sqrt), reads=[s4s], writes=[s4s])
                            P.op("dve", lambda e, s4=s4: e.reciprocal(out=s4[:, 1:2], in_=s4[:, 0:1]), reads=[s4s], writes=[s4s])
                            on, ons = gonb.next()
                            P.op("dve", lambda e, on=on, b=b, s4=s4: e.scalar_tensor_tensor(out=on[:], in0=otm[:, b, :], scalar=s4[:, 1:2], in1=gdl, op0=ALU.mult, op1=ALU.mult),
                                 reads=[otm_s, s4s, ng_s], writes=[ons])
                            P.op("pe", lambda e, tb=tb, on=on, i4=i4: e.transpose(tb[:, i4 * 128:(i4 + 1) * 128], on[:], it[:]), reads=[ons, isl], writes=[tbs])
                        P.op("dve", lambda e, tb=tb, b4=b4, goT=goT: e.tensor_tensor(out=goT[:, b4 * 512:(b4 + 1) * 512], in0=tb[:, :], in1=cv[:, b4 * 512:(b4 + 1) * 512], op=ALU.mult),
                             reads=[cv_s], writes=[goT_s, tbs])
                    store_mix(goT, goT_s, 8 + h)
                P.fence()
                ph.close()
            if len(parts) < 2:
                ph = contextlib.ExitStack()
                zT, zT_s = Buf(P, ph, "zT", [128, SQ], BF16).next()
                P.op("pool", lambda e: e.memset(zT[:], 0.0), writes=[zT_s])
                for j in (range(8, 16) if "gdn" not in parts else range(0, 8)):
                    store_mix(zT, zT_s, j)
                P.fence()
                ph.close()
            P.fence()
            phm.close()
            ph = contextlib.ExitStack()
            b_ = ln_bufs(ph)
            mtb = Buf(P, ph, "mixt", [128, NC_, TT], BF16, n=2)
            wmb = Buf(P, ph, "wmo", [128, NC_, 128], BF16, n=3)
            for t in tiles:
                mt, mts = mtb.next()
                P.dma("pool", f"mixld{mtb.i}", lambda e, mt=mt, t=t: e.dma_start(out=mt[:], in_=MIX[t]), reads=mix_slots[t], writes=[mts])

                def wload(c):
                    wt, ws = wmb.next()
                    P.dma("sp", f"wmo{wmb.i}", lambda e: e.dma_start(out=wt[:], in_=WMO[l][c]), reads=[wslot["mo", l, c]], writes=[ws])
                    return wt, ws
                out_ln(l, 1, t, NC_, lambda j, mt=mt, mts=mts: (mt[:, j, :], mts), wload, 1.0 / ALPHA, b_)
            P.fence()
            ph.close()

        if "mix" in stages:
            rope_tables()
        for l in range(depth):
            if "ffn1" in stages:
                ffn_stage(l, 1, 0)
            if "mix" in stages:
                for q in range(nseq):
                    mixer_stage(l, q, cfg.get("parts", ("diff", "gdn")))
            if "ffn2" in stages:
                ffn_stage(l, 2, 2)

        outs = []
        ph = contextlib.ExitStack()
        xfst = Buf(P, ph, "xfst", [128, NC_, TT], F32, n=2)
        ytok = Buf(P, ph, "ytok", [128, D], F32, n=2)
        for t in range(ntile):
            ft, fs = xfst.next()
            P.dma("pool", f"xfld{xfst.i}", lambda e, ft=ft, t=t: e.dma_start(out=ft[:], in_=XF[t]),
                  reads=[xf_slots[t]], writes=[fs])
            for q4 in range(4):
                tt = t * 4 + q4
                yt, ys = ytok.next()
                for g in range(4):
                    pb, ps = bk[g % 2]
                    for i in range(4):
                        c = 4 * g + i
                        P.op("pe", lambda e, pb=pb, ft=ft, c=c, i=i, q4=q4: e.transpose(pb[:, i * 128:(i + 1) * 128], ft[:, c, q4 * 128:(q4 + 1) * 128], it[:]),
                             reads=[fs, isl], writes=[ps], signal=(i == 3))
                    if g % 2 == 0:
                        P.op("act", lambda e, yt=yt, pb=pb, g=g: e.copy(out=yt[:, g * 512:(g + 1) * 512], in_=pb[:, :]), reads=[ps], writes=[ys])
                    else:
                        P.op("dve", lambda e, yt=yt, pb=pb, g=g: e.tensor_copy(out=yt[:, g * 512:(g + 1) * 512], in_=pb[:, :]), reads=[ps], writes=[ys])
                osl_ = P.slot()
                outs.append(osl_)
                P.dma("pool", f"ytok{ytok.i}", lambda e, yt=yt, tt=tt: e.dma_start(out=y_out[tt * 128:(tt + 1) * 128, :], in_=yt[:]),
                      reads=[ys], writes=[osl_])
        P.wait_all("pool", outs)
        ph.close()
        P.emit(stack)
    return nc


def host_consts(depth, ln):
    a = np.stack(ln, axis=1)
    a = a.reshape(depth, 6, NC_, 128)
    a = np.transpose(a, (3, 0, 1, 2)).reshape(128, depth * 6 * NC_)
    return np.ascontiguousarray(a.astype(np.float32))


def host_mix_inputs(inp, depth, pos_core):
    f32 = np.float32
    w_in = np.asarray(inp["w_in"])[:depth]
    perm = np.arange(2048)
    d = perm % 64
    perm = np.where(d < 8, perm + 8, np.where(d < 16, perm - 8, perm))
    w_sw = np.ascontiguousarray(w_in[:, :, perm])
    conv_w = np.asarray(inp["conv_w"])[:depth]
    cw = conv_w.reshape(depth, 4, 24, 128).transpose(3, 0, 2, 1).reshape(128, depth * 96)
    hp8 = np.zeros((128, depth * 2), f32)
    hp8[:8, 0::2] = np.asarray(inp["a_log"])[:depth].T
    hp8[:8, 1::2] = np.asarray(inp["dt_bias"])[:depth].T
    lam = np.concatenate([np.asarray(inp[k])[:depth] for k in ("lam_q1", "lam_k1", "lam_q2", "lam_k2")], axis=1)
    lamv = np.broadcast_to(lam.reshape(1, depth * 256), (128, depth * 256))
    ng = np.concatenate([np.asarray(inp["diff_norm_g"])[:depth], np.asarray(inp["delta_norm_g"])[:depth]], axis=1)
    normg = np.broadcast_to(ng.reshape(1, depth * 256), (128, depth * 256))
    pos = np.broadcast_to(np.asarray(pos_core).reshape(1, -1).astype(np.int32), (128, pos_core.size))
    return {"w_in": np.ascontiguousarray(w_in), "w_in_sw": w_sw, "w_out": np.ascontiguousarray(np.asarray(inp["w_out"])[:depth]),
            "convw": np.ascontiguousarray(cw.astype(f32)), "hp8": hp8, "lamv": np.ascontiguousarray(lamv.astype(f32)),
            "normg": np.ascontiguousarray(normg.astype(f32)), "pos": np.ascontiguousarray(pos)}


def host_static_consts():
    f32 = np.float32
    p = np.arange(128)
    d = p % 64
    inv = 500000.0 ** (-(d % 8) / 8.0)
    ropec = np.zeros((128, 2), f32)
    ropec[:, 0] = np.where(d < 16, inv / (2 * np.pi), 0.0)
    ropec[:, 1] = np.where(d < 8, -1.0, np.where(d < 16, 1.0, 0.0))
    i = p[:, None]
    j = p[None, :]
    same = (i // 64) == (j // 64)
    masks = np.zeros((128, 640), f32)
    masks[:, 0:128] = (i <= j)
    masks[:, 128:256] = same & (i > j)
    masks[:, 256:384] = same & (i <= j)
    masks[:, 384] = (p < 64)
    masks[:, 385] = (p >= 64)
    masks[:, 512:640] = same
    return {"ropec": ropec, "masks": masks, "ident": np.eye(128, dtype=f32)}


def make_in_maps(inputs, n_cores, depth, stages=("ffn1", "mix", "ffn2")):
    x = np.asarray(inputs["x"])
    B, S, _ = x.shape
    per = B // n_cores
    pos = np.asarray(inputs["positions"])
    lnp = host_consts(depth, [np.asarray(inputs[k])[:depth] for k in ("ln1_g", "ln1_b", "ln2_g", "ln2_b", "ln3_g", "ln3_b")])
    st = host_static_consts()
    shared = {"lnp": lnp, "ident": st["ident"]}
    if "ffn1" in stages:
        shared["ffn1_w_in"] = np.ascontiguousarray(np.asarray(inputs["ffn1_w_in"])[:depth])
        shared["ffn1_w_out"] = np.ascontiguousarray(np.asarray(inputs["ffn1_w_out"])[:depth])
    if "ffn2" in stages:
        shared["ffn2_w_in"] = np.ascontiguousarray(np.asarray(inputs["ffn2_w_in"])[:depth])
        shared["ffn2_w_out"] = np.ascontiguousarray(np.asarray(inputs["ffn2_w_out"])[:depth])
    in_maps = []
    for c in range(n_cores):
        m = dict(shared)
        m["x"] = np.ascontiguousarray(x[c * per:(c + 1) * per].reshape(per * S, D))
        if "mix" in stages:
            mm = host_mix_inputs(inputs, depth, pos[c * per:(c + 1) * per].reshape(-1))
            if c > 0:
                for k in ("w_in", "w_in_sw", "w_out", "convw", "hp8", "lamv", "normg"):
                    mm[k] = in_maps[0][k]
            m.update(mm)
            m["ropec"] = st["ropec"]
            m["masks"] = st["masks"]
        in_maps.append(m)
    return in_maps, per, S


def kernel(**inputs):
    n = 8
    in_maps, per, S = make_in_maps(inputs, n, DEPTH)
    nc = build_program(dict(ntok=per * S, depth=DEPTH))
    res = run_bass_kernel_spmd(nc, in_maps, core_ids=list(range(n)))
    out = np.concatenate([np.asarray(r["y"]).reshape(per, S, D) for r in res.results], axis=0)
    return out.astype(np.float32)
```

```python
import contextlib
import math
import numpy as np
import concourse.bass as bass
import concourse.mybir as mybir
from concourse.bass_utils import run_bass_kernel_spmd

F32 = mybir.dt.float32
BF16 = mybir.dt.bfloat16
I32 = mybir.dt.int32
AF = mybir.ActivationFunctionType
ALU = mybir.AluOpType

D = 2048
NC_ = 16
DFF = 5632
NFC = 44
DEPTH = 4
SEQ = 2048
ALPHA = (2 * DEPTH) ** 0.25
LN_EPS = 1e-5
IN_WIDTH = 7184


class Slot:
    __slots__ = ("name", "w", "r")

    def __init__(self, name):
        self.name = name
        self.w = None
        self.r = {}


class Prog:
    def __init__(self, nc):
        self.nc = nc
        self.q = {"pe": [], "act": [], "dve": [], "pool": [], "sp": []}
        self.cnt = {}
        self.waited = {e: {} for e in self.q}
        self.nslots = 0
        self.qmap = {"pool": "sp"}

    def slot(self, name=None):
        self.nslots += 1
        return Slot(name or f"s{self.nslots}")

    def _deps(self, eng, reads, writes, extra=()):
        deps = {}

        def add(ev):
            if ev is None:
                return
            k, v = ev
            if deps.get(k, 0) < v:
                deps[k] = v

        for s in reads:
            add(s.w)
        for s in writes:
            add(s.w)
            for k, v in s.r.items():
                add((k, v))
        for ev in extra:
            add(ev)
        waits = []
        wd = self.waited[eng]
        for k, v in deps.items():
            if k == "pe" and eng == "pe":
                continue
            if wd.get(k, 0) >= v:
                continue
            wd[k] = v
            waits.append((k, v))
        return waits

    def _mark(self, ev, reads, writes):
        k, v = ev
        for s in reads:
            if s.r.get(k, 0) < v:
                s.r[k] = v
        for s in writes:
            s.w = ev
            s.r = {}

    def op(self, eng, fn, reads=(), writes=(), signal=True):
        waits = self._deps(eng, reads, writes)
        c = self.cnt.get(eng, 0)
        if signal:
            c += 1
            self.cnt[eng] = c
            ev = (eng, c)
            inc = (eng, 1)
        else:
            ev = (eng, c + 1)
            inc = None
        self._mark(ev, reads, writes)
        self.q[eng].append((waits, fn, inc))

    def dma(self, qeng, chan, fn, reads=(), writes=()):
        qeng = self.qmap.get(qeng, qeng)
        key = "d:" + chan
        c = self.cnt.get(key, 0)
        waits = self._deps(qeng, reads, writes, extra=[(key, c)] if c else [])
        c += 16
        self.cnt[key] = c
        self._mark((key, c), reads, writes)
        self.q[qeng].append((waits, fn, (key, 16)))

    def fence(self):
        for e in self.q:
            waits = []
            wd = self.waited[e]
            for k, v in self.cnt.items():
                if k == "pe" and e == "pe":
                    continue
                if wd.get(k, 0) >= v:
                    continue
                wd[k] = v
                waits.append((k, v))
            if waits:
                self.q[e].append((waits, None, None))

    def wait_all(self, eng, slots):
        waits = self._deps(eng, slots, ())
        self.q[eng].append((waits, None, None))

    def emit(self, stack):
        nc = self.nc
        sems = {}
        for k in self.cnt:
            sems[k] = stack.enter_context(nc.semaphore("sem_" + k.replace(":", "_")))
        engs = {"pe": "tensor", "act": "scalar", "dve": "vector", "pool": "gpsimd", "sp": "sync"}
        q = self.q

        def run(e, lst):
            for waits, fn, inc in lst:
                for k, v in waits:
                    e.wait_ge(sems[k], v)
                if fn is not None:
                    ins = fn(e)
                    if inc is not None:
                        ins.then_inc(sems[inc[0]], inc[1])

        with nc.Block() as block:
            for name, attr in engs.items():
                if not q[name]:
                    continue

                def mk(lst):
                    def f(e):
                        run(e, lst)
                    return f

                getattr(block, attr)(mk(q[name]))


class Buf:
    uid = 0

    def __init__(self, P, stack, name, shape, dtype, n=1, psum=False):
        self.n = n
        self.t = []
        self.s = []
        for i in range(n):
            Buf.uid += 1
            nm = f"{name}{i}_{Buf.uid}"
            if psum:
                t = stack.enter_context(P.nc.psum_tensor(nm, shape, dtype))
            else:
                t = stack.enter_context(P.nc.sbuf_tensor(nm, shape, dtype))
            self.t.append(t)
            self.s.append(P.slot(nm))
        self.i = -1

    def next(self):
        self.i = (self.i + 1) % self.n
        return self.t[self.i], self.s[self.i]

    def cur(self):
        return self.t[self.i], self.s[self.i]


def build_program(cfg):
    NT = cfg["ntok"]
    depth = cfg["depth"]
    stages = cfg.get("stages", ("ffn1", "mix", "ffn2"))
    TT = 512
    ntile = NT // TT
    n128 = NT // 128

    nc = bass.Bass("TRN2", target_bir_lowering=False)
    P = Prog(nc)
    stack = contextlib.ExitStack()

    def din(name, shape, dt=F32):
        return nc.dram_tensor(name, list(shape), dt, kind="ExternalInput").ap()

    x_in = din("x", [NT, D])
    w1i = din("ffn1_w_in", [depth, D, 2 * DFF]) if "ffn1" in stages else None
    w1o = din("ffn1_w_out", [depth, DFF, D]) if "ffn1" in stages else None
    w2i = din("ffn2_w_in", [depth, D, 2 * DFF]) if "ffn2" in stages else None
    w2o = din("ffn2_w_out", [depth, DFF, D]) if "ffn2" in stages else None
    if "mix" in stages:
        w_pr = din("w_in", [depth, D, IN_WIDTH])
        w_sw = din("w_in_sw", [depth, D, 2048])
        w_mo = din("w_out", [depth, D, D])
        convw_in = din("convw", [128, depth * 24 * 4])
        hp8_in = din("hp8", [128, depth * 2])
        lamv_in = din("lamv", [128, depth * 256])
        normg_in = din("normg", [128, depth * 256])
        pos_in = din("pos", [128, NT], I32)
        ropec_in = din("ropec", [128, 2])
        masks_in = din("masks", [128, 640])
    lnp = din("lnp", [128, depth * 6 * NC_])
    ident_in = din("ident", [128, 128])
    y_out = nc.dram_tensor("y", [NT, D], F32, kind="ExternalOutput").ap()

    def dscr(name, shape, dt):
        return nc.dram_tensor(name, list(shape), dt, kind="Internal").ap()

    XF = dscr("XF", [ntile, 128, NC_, TT], F32)
    XB = dscr("XB", [ntile, 128, NC_, TT], BF16)
    WIN = {}
    WOUT = {}
    for l in range(depth):
        for f in (1, 2):
            WIN[l, f] = dscr(f"WIN{l}_{f}", [NFC, 128, 2, NC_, 128], BF16)
            WOUT[l, f] = dscr(f"WOUT{l}_{f}", [NC_, 128, NFC, 128], BF16)
    nseq = NT // SEQ if NT >= SEQ else 1
    SQ = min(SEQ, NT)
    if "mix" in stages:
        WPR = {l: dscr(f"WPR{l}", [56, 128, NC_, 128], BF16) for l in range(depth)}
        WSW = {l: dscr(f"WSW{l}", [16, 128, NC_, 128], BF16) for l in range(depth)}
        WBA = {l: dscr(f"WBA{l}", [128, NC_, 16], BF16) for l in range(depth)}
        WMO = {l: dscr(f"WMO{l}", [NC_, 128, NC_, 128], BF16) for l in range(depth)}
        ROPE = dscr("ROPE", [nseq, 2, 128, SQ], F32)
        MIX = dscr("MIX", [ntile, 128, NC_, TT], BF16)
        mix_slots = [[P.slot(f"MIX{t}_{j}") for j in range(NC_)] for t in range(ntile)]
        rope_slots = [P.slot(f"ROPE{q}") for q in range(nseq)]
    xf_slots = [P.slot(f"XF{t}") for t in range(ntile)]
    xb_slots = [P.slot(f"XB{t}") for t in range(ntile)]
    wslot = {}

    with stack:
        ident = Buf(P, stack, "ident", [128, 128], F32)
        ones_bf = Buf(P, stack, "ones_bf", [128, 128], BF16)
        lnp_sb = Buf(P, stack, "lnp_sb", [128, depth * 6 * NC_], F32)
        it, isl = ident.next()
        P.dma("pool", "const", lambda e: e.dma_start(out=it[:], in_=ident_in[:, :]), writes=[isl])
        ot, osl = ones_bf.next()
        P.op("pool", lambda e: e.memset(ot[:], 1.0), writes=[osl])
        lt, lsl = lnp_sb.next()
        if not (cfg.get("dbg", 0) & 2):
            P.dma("pool", "const", lambda e: e.dma_start(out=lt[:], in_=lnp[:, :]), writes=[lsl])

        def lnvec(l, which, c):
            o = (l * 6 + which) * NC_ + c
            return lt[:, o:o + 1]

        banks = Buf(P, stack, "bank", [128, 512], F32, n=8, psum=True)
        bk = list(zip(banks.t, banks.s))

        ph = contextlib.ExitStack()
        NSTG = 3
        stg = Buf(P, ph, "stg", [128, NC_ * 512], F32, n=NSTG)
        stgb = Buf(P, ph, "stgb", [128, NC_ * 512], BF16, n=NSTG)
        ci = [0]

        def precast_span(src2d, row0, nk, col0, ncc, dsts, slots):
            st, ss = stg.next()
            sb, sbs = stgb.next()
            k_ = ci[0]
            ci[0] += 1
            i_ = stg.i
            qe = "sp" if k_ % 2 == 0 else "act"
            W = ncc * 128
            v = st[:, :nk * W].rearrange("p (k w) -> p k w", k=nk)
            srcap = src2d[row0:row0 + nk * 128, col0:col0 + W].rearrange("(k p) w -> p k w", p=128)
            P.dma(qe, f"pcl{i_}", lambda e: e.dma_start(out=v, in_=srcap), writes=[ss])
            ce = ("dve", "act", "dve", "act", "pool")[k_ % 5]
            ov = sb[:, :nk * W].rearrange("p (h k c) -> p h k c", h=ncc, k=nk)
            iv = st[:, :nk * W].rearrange("p (k h c) -> p h k c", k=nk, h=ncc)
            if ce == "act":
                P.op("act", lambda e: e.copy(out=ov, in_=iv), reads=[ss], writes=[sbs])
            else:
                P.op(ce, lambda e: e.tensor_copy(out=ov, in_=iv), reads=[ss], writes=[sbs])
            for h in range(ncc):
                P.dma(qe, f"pcs{i_}", lambda e, h=h: e.dma_start(out=dsts[h], in_=sb[:, h * nk * 128:(h + 1) * nk * 128]),
                      reads=[sbs], writes=[slots[h]])

        def precast_ffn(l, f, wi, wo):
            for j in range(NFC):
                wslot["in", l, f, j] = P.slot()
            for c in range(NC_):
                wslot["out", l, f, c] = P.slot()
            for J in range(NFC // 4):
                for h in range(2):
                    precast_span(wi[l], 0, NC_, h * DFF + J * 512, 4,
                                 [WIN[l, f][4 * J + cc][:, h].rearrange("p k c -> p (k c)") for cc in range(4)],
                                 [wslot["in", l, f, 4 * J + cc] for cc in range(4)])
            for C in range(NC_ // 4):
                for (k0, nk) in ((0, 16), (16, 16), (32, 12)):
                    precast_span(wo[l], k0 * 128, nk, C * 512, 4,
                                 [WOUT[l, f][4 * C + cc][:, k0:k0 + nk, :].rearrange("p k c -> p (k c)") for cc in range(4)],
                                 [wslot["out", l, f, 4 * C + cc] for cc in range(4)])

        def precast_mix(l):
            for (kind, W_, src, n) in (("pr", WPR, w_pr, 56), ("sw", WSW, w_sw, 16), ("mo", WMO, w_mo, 16)):
                for m in range(n):
                    wslot[kind, l, m] = P.slot()
                for s4 in range(n // 4):
                    precast_span(src[l], 0, NC_, s4 * 512, 4,
                                 [W_[l][4 * s4 + cc].rearrange("p k c -> p (k c)") for cc in range(4)],
                                 [wslot[kind, l, 4 * s4 + cc] for cc in range(4)])
            st, ss = stg.next()
            sb, sbs = stgb.next()
            ci[0] += 1
            i_ = stg.i
            v = st[:, :NC_ * 16].rearrange("p (k w) -> p k w", k=NC_)
            srcap = w_pr[l][:, 7168:7184].rearrange("(k p) w -> p k w", p=128)
            P.dma("sp", f"pcl{i_}", lambda e: e.dma_start(out=v, in_=srcap), writes=[ss])
            P.op("dve", lambda e: e.tensor_copy(out=sb[:, :NC_ * 16], in_=st[:, :NC_ * 16]), reads=[ss], writes=[sbs])
            sl = P.slot()
            wslot["ba", l] = sl
            P.dma("sp", f"pcs{i_}", lambda e: e.dma_start(out=WBA[l].rearrange("p k c -> p (k c)"), in_=sb[:, :NC_ * 16]),
                  reads=[sbs], writes=[sl])

        for l in range(depth):
            if "mix" in stages:
                precast_mix(l)
            if "ffn1" in stages:
                precast_ffn(l, 1, w1i, w1o)
            if "ffn2" in stages:
                precast_ffn(l, 2, w2i, w2o)

        P.fence()
        ph.close()
        ph = contextlib.ExitStack()
        xtok = Buf(P, ph, "xtok", [128, D], F32, n=2)
        xfst = Buf(P, ph, "xfst", [128, NC_, TT], F32, n=2)
        xbst = Buf(P, ph, "xbst", [128, NC_, TT], BF16, n=2)
        for t in range(ntile):
            ft, fs = xfst.next()
            bt, bs = xbst.next()
            for q4 in range(4):
                tt = t * 4 + q4
                xt, xs = xtok.next()
                P.dma("pool", f"xtok{xtok.i}", lambda e, xt=xt, tt=tt: e.dma_start(out=xt[:], in_=x_in[tt * 128:(tt + 1) * 128, :]),
                      writes=[xs])
                for g in range(4):
                    pb, ps = bk[g % 2]
                    for i in range(4):
                        c = 4 * g + i
                        P.op("pe", lambda e, pb=pb, xt=xt, c=c, i=i: e.transpose(pb[:, i * 128:(i + 1) * 128], xt[:, c * 128:(c + 1) * 128], it[:]),
                             reads=[xs, isl], writes=[ps], signal=(i == 3))
                    pv = pb[:, :].rearrange("p (a b) -> p a b", a=4)
                    P.op("act", lambda e, ft=ft, pv=pv, g=g, q4=q4: e.copy(out=ft[:, 4 * g:4 * g + 4, q4 * 128:(q4 + 1) * 128], in_=pv),
                         reads=[ps], writes=[fs])
                    P.op("dve", lambda e, bt=bt, ft=ft, g=g, q4=q4: e.tensor_copy(out=bt[:, 4 * g:4 * g + 4, q4 * 128:(q4 + 1) * 128],
                                                                                 in_=ft[:, 4 * g:4 * g + 4, q4 * 128:(q4 + 1) * 128]),
                         reads=[fs], writes=[bs])
            tok = slice(t * TT, (t + 1) * TT)
            P.dma("pool", f"xfst{xfst.i}", lambda e, ft=ft, t=t: e.dma_start(out=XF[t], in_=ft[:]),
                  reads=[fs], writes=[xf_slots[t]])
            P.dma("pool", f"xbst{xbst.i}", lambda e, bt=bt, t=t: e.dma_start(out=XB[t], in_=bt[:]),
                  reads=[bs], writes=[xb_slots[t]])

        P.fence()
        ph.close()

        def ln_bufs(ph, b_ny=1):
            b = {}
            b["yb"] = Buf(P, ph, "ybuf", [128, NC_, TT], F32, n=b_ny)
            b["xnb"] = Buf(P, ph, "xnb", [128, TT], BF16, n=3)
            b["ybf"] = Buf(P, ph, "ybf", [128, TT], BF16, n=3)
            b["ysq"] = Buf(P, ph, "ysq", [128, TT], BF16, n=3)
            b["mean"] = Buf(P, ph, "mean", [128, TT], F32)
            b["rstd"] = Buf(P, ph, "rstd", [128, TT], F32)
            b["tmp"] = Buf(P, ph, "tmpn", [128, TT], F32, n=2)
            b["y_cs_all"] = [[P.slot() for c in range(NC_)] for _ in range(b_ny)]
            for k in ("mean", "rstd"):
                b[k].next()
            return b

        def out_ln(l, which_ln, t, nk, rhs_fn, wload, s_res, b):
            eps = LN_EPS / (ALPHA * ALPHA)
            y_t, y_s = b["yb"].next()
            y_cs = b["y_cs_all"][b["yb"].i]
            mean_t, mean_s = b["mean"].cur()
            rstd_t, rstd_s = b["rstd"].cur()
            tmp, ybf, ysq = b["tmp"], b["ybf"], b["ysq"]
            S1, S1s = bk[6]
            S2, S2s = bk[7]
            P.dma("pool", "yld", lambda e: e.dma_start(out=y_t[:], in_=XF[t]), reads=[xf_slots[t]], writes=[y_s] + y_cs)
            pend = [None]
            for c in range(NC_):
                wt, ws = wload(c)
                pb, ps = bk[4 + (c % 2)]
                for j in range(nk):
                    ra, rs = rhs_fn(j)
                    P.op("pe", lambda e, pb=pb, wt=wt, j=j, ra=ra: e.matmul(pb[:, :], wt[:, j, :], ra, start=(j == 0), stop=(j == nk - 1)),
                         reads=[ws, rs], writes=[ps], signal=(j == nk - 1))
                P.op("dve", lambda e, pb=pb, c=c: e.scalar_tensor_tensor(
                    out=y_t[:, c, :], in0=pb[:, :], scalar=s_res, in1=y_t[:, c, :], op0=ALU.mult, op1=ALU.add),
                    reads=[ps, y_cs[c]], writes=[y_cs[c]])
                yt, ys = ybf.next()
                qt, qs = ysq.next()
                P.op("pool", lambda e, yt=yt, c=c: e.tensor_copy(out=yt[:], in_=y_t[:, c, :]), reads=[y_cs[c]], writes=[ys])
                P.op("act", lambda e, qt=qt, c=c: e.activation(out=qt[:], in_=y_t[:, c, :], func=AF.Square), reads=[y_cs[c]], writes=[qs])

                def stats(yt=yt, ys=ys, qt=qt, qs=qs, c=c):
                    P.op("pe", lambda e: e.matmul(S1[:, :], ot[:], yt[:], start=(c == 0), stop=(c == NC_ - 1)), reads=[osl, ys], writes=[S1s])
                    P.op("pe", lambda e: e.matmul(S2[:, :], ot[:], qt[:], start=(c == 0), stop=(c == NC_ - 1)), reads=[osl, qs], writes=[S2s])
                if pend[0] is not None:
                    pend[0]()
                pend[0] = stats
            pend[0]()
            P.op("dve", lambda e: e.tensor_scalar(out=mean_t[:], in0=S1[:, :], scalar1=1.0 / D, scalar2=None, op0=ALU.mult), reads=[S1s], writes=[mean_s])
            m2, m2s = tmp.next()
            P.op("dve", lambda e: e.tensor_tensor(out=m2[:], in0=mean_t[:], in1=mean_t[:], op=ALU.mult), reads=[mean_s], writes=[m2s])
            P.op("dve", lambda e: e.scalar_tensor_tensor(out=rstd_t[:], in0=S2[:, :], scalar=1.0 / D, in1=m2[:], op0=ALU.mult, op1=ALU.subtract),
                 reads=[S2s, m2s], writes=[rstd_s])
            P.op("dve", lambda e: e.tensor_scalar(out=rstd_t[:], in0=rstd_t[:], scalar1=eps, scalar2=None, op0=ALU.add), reads=[rstd_s], writes=[rstd_s])
            P.op("act", lambda e: e.activation(out=rstd_t[:], in_=rstd_t[:], func=AF.Sqrt), reads=[rstd_s], writes=[rstd_s])
            P.op("dve", lambda e: e.reciprocal(out=rstd_t[:], in_=rstd_t[:]), reads=[rstd_s], writes=[rstd_s])
            for c in range(NC_):
                tt_, ts_ = tmp.next()
                P.op("dve", lambda e, tt_=tt_, c=c: e.tensor_tensor(out=tt_[:], in0=y_t[:, c, :], in1=mean_t[:], op=ALU.subtract),
                     reads=[y_cs[c], mean_s], writes=[ts_])
                P.op("dve", lambda e, tt_=tt_: e.tensor_tensor(out=tt_[:], in0=tt_[:], in1=rstd_t[:], op=ALU.mult), reads=[ts_, rstd_s], writes=[ts_])
                g_ap = lnvec(l, 2 * which_ln, c)
                b_ap = lnvec(l, 2 * which_ln + 1, c)
                P.op("act", lambda e, tt_=tt_, c=c, g_ap=g_ap, b_ap=b_ap: e.activation(
                    out=y_t[:, c, :], in_=tt_[:], func=AF.Identity, bias=b_ap, scale=g_ap), reads=[ts_, lsl], writes=[y_cs[c]])
                xn_t, xn_s = b["xnb"].next()
                P.op("pool", lambda e, c=c, xn_t=xn_t: e.tensor_copy(out=xn_t[:], in_=y_t[:, c, :]), reads=[y_cs[c]], writes=[xn_s])
                P.dma("pool", f"xnst{b['xnb'].i}", lambda e, c=c, xn_t=xn_t: e.dma_start(out=XB[t][:, c, :], in_=xn_t[:]), reads=[xn_s], writes=[xb_slots[t]])
            P.dma("pool", "yst", lambda e: e.dma_start(out=XF[t], in_=y_t[:]), reads=[y_s] + y_cs, writes=[xf_slots[t]])

        def ffn_stage(l, f, which_ln):
            ph = contextlib.ExitStack()
            xT = Buf(P, ph, "xT", [128, NC_, TT], BF16, n=2)
            aT = Buf(P, ph, "aT", [128, NFC, TT], BF16)
            wib = Buf(P, ph, "wib", [128, 2, NC_, 128], BF16, n=4)
            wob = Buf(P, ph, "wob", [128, NFC, 128], BF16, n=2)
            sg = Buf(P, ph, "sg", [128, TT], F32, n=2)
            b = ln_bufs(ph)
            aT_t, aT_s = aT.next()
            aT_cs = [P.slot(f"aT{j}") for j in range(NFC)]
            for t in range(ntile):
                xt, xs = xT.next()
                P.dma("pool", f"xT{xT.i}", lambda e, xt=xt, t=t: e.dma_start(out=xt[:], in_=XB[t]), reads=[xb_slots[t]], writes=[xs])
                for j in range(NFC):
                    wt, ws = wib.next()
                    P.dma("sp", f"wib{wib.i}", lambda e, wt=wt, j=j: e.dma_start(out=wt[:], in_=WIN[l, f][j]),
                          reads=[wslot["in", l, f, j]], writes=[ws])
                    gb, gs = bk[(j % 2) * 2]
                    ub, us = bk[(j % 2) * 2 + 1]
                    for h, (pb, ps) in enumerate(((gb, gs), (ub, us))):
                        for k in range(NC_):
                            P.op("pe", lambda e, pb=pb, wt=wt, xt=xt, k=k, h=h: e.matmul(
                                pb[:, :], wt[:, h, k, :], xt[:, k, :], start=(k == 0), stop=(k == NC_ - 1)),
                                reads=[ws, xs], writes=[ps], signal=(k == NC_ - 1))
                    st_, ss_ = sg.next()
                    P.op("act", lambda e, st_=st_, gb=gb: e.activation(out=st_[:], in_=gb[:, :], func=AF.Silu), reads=[gs], writes=[ss_])
                    P.op("dve", lambda e, st_=st_, ub=ub, j=j: e.tensor_tensor(out=aT_t[:, j, :], in0=ub[:, :], in1=st_[:], op=ALU.mult),
                         reads=[us, ss_], writes=[aT_cs[j]])

                def wload(c):
                    wt, ws = wob.next()
                    P.dma("sp", f"wob{wob.i}", lambda e: e.dma_start(out=wt[:], in_=WOUT[l, f][c]), reads=[wslot["out", l, f, c]], writes=[ws])
                    return wt, ws
                out_ln(l, which_ln, t, NFC, lambda j: (aT_t[:, j, :], aT_cs[j]), wload, 0.5 / ALPHA, b)
            P.fence()
            ph.close()

        AQ, AK, AV, DQ, DK, DV, DZ = 0, 8, 16, 24, 32, 40, 48
        NB = SQ // 128
        NQ = SQ // 512
        AX = mybir.AxisListType.X
        if "mix" in stages:
            def cload(name, shape, src, dt=F32):
                bf = Buf(P, stack, name, shape, dt)
                t_, s_ = bf.next()
                P.dma("pool", "const", lambda e: e.dma_start(out=t_[:], in_=src), writes=[s_])
                return t_, s_
            mk_t, mk_s = cload("masks", [128, 640], masks_in[:, :])
            cw_t, cw_s = cload("convw", [128, depth * 96], convw_in[:, :])
            hp_t, hp_s = cload("hp8", [128, depth * 2], hp8_in[:, :])
            lv_t, lv_s = cload("lamv", [128, depth * 256], lamv_in[:, :])
            ng_t, ng_s = cload("normg", [128, depth * 256], normg_in[:, :])
            rc_t, rc_s = cload("ropec", [128, 2], ropec_in[:, :])
            TRIf, LSTR, UINC, CIND, BLKM = mk_t[:, 0:128], mk_t[:, 128:256], mk_t[:, 256:384], mk_t[:, 384:386], mk_t[:, 512:640]
            tribf_t, tribf_s = Buf(P, stack, "tribf", [128, 128], BF16).next()
            P.op("dve", lambda e: e.tensor_copy(out=tribf_t[:], in_=TRIf), reads=[mk_s], writes=[tribf_s])
            zer_t, zer_s = Buf(P, stack, "zerf", [128, 128], F32).next()
            P.op("pool", lambda e: e.memset(zer_t[:], 0.0), writes=[zer_s])
            onf_t, onf_s = Buf(P, stack, "onesf", [128, 128], F32).next()
            P.op("pool", lambda e: e.memset(onf_t[:], 1.0), writes=[onf_s])
            lam_t, lam_s = Buf(P, stack, "lamt", [128, depth * 2], F32).next()
            nA_t, nA_s = Buf(P, stack, "negA", [128, depth], F32).next()
            sc_t, sc_s = Buf(P, stack, "lamsc", [128, 64], F32).next()
            sc2_t, sc2_s = Buf(P, stack, "lamsc2", [128, 4], F32).next()
            for l in range(depth):
                lam_init = 0.8 - 0.6 * math.exp(-0.3 * l)
                for z in range(2):
                    o_ = l * 256 + z * 128
                    P.op("dve", lambda e, o_=o_: e.tensor_tensor(out=sc_t[:], in0=lv_t[:, o_:o_ + 64], in1=lv_t[:, o_ + 64:o_ + 128], op=ALU.mult),
                         reads=[lv_s], writes=[sc_s])
                    P.op("dve", lambda e, z=z: e.reduce_sum(out=sc2_t[:, z:z + 1], in_=sc_t[:], axis=AX), reads=[sc_s], writes=[sc2_s])
                P.op("act", lambda e: e.activation(out=sc2_t[:, 0:2], in_=sc2_t[:, 0:2], func=AF.Exp), reads=[sc2_s], writes=[sc2_s])
                P.op("dve", lambda e: e.tensor_tensor(out=sc2_t[:, 2:3], in0=sc2_t[:, 0:1], in1=sc2_t[:, 1:2], op=ALU.subtract), reads=[sc2_s], writes=[sc2_s])
                P.op("dve", lambda e, l=l, lam_init=lam_init: e.tensor_scalar(out=lam_t[:, 2 * l:2 * l + 1], in0=sc2_t[:, 2:3], scalar1=lam_init, scalar2=None, op0=ALU.add),
                     reads=[sc2_s], writes=[lam_s])
                P.op("dve", lambda e, l=l: e.tensor_scalar(out=lam_t[:, 2 * l + 1:2 * l + 2], in0=lam_t[:, 2 * l:2 * l + 1], scalar1=-1.0, scalar2=None, op0=ALU.mult),
                     reads=[lam_s], writes=[lam_s])
                P.op("dve", lambda e, l=l, lam_init=lam_init: e.tensor_scalar(out=ng_t[:, l * 256:l * 256 + 128], in0=ng_t[:, l * 256:l * 256 + 128],
                                                                              scalar1=1.0 - lam_init, scalar2=None, op0=ALU.mult), reads=[ng_s], writes=[ng_s])
                P.op("act", lambda e, l=l: e.activation(out=nA_t[:, l:l + 1], in_=hp_t[:, 2 * l:2 * l + 1], func=AF.Exp), reads=[hp_s], writes=[nA_s])
                P.op("dve", lambda e, l=l: e.tensor_scalar(out=nA_t[:, l:l + 1], in0=nA_t[:, l:l + 1], scalar1=-1.0, scalar2=None, op0=ALU.mult), reads=[nA_s], writes=[nA_s])

        def rope_tables():
            P.fence()
            ph = contextlib.ExitStack()
            posi, posi_s = Buf(P, ph, "posi", [128, SQ], I32).next()
            pf, pf_s = Buf(P, ph, "posf", [128, SQ], F32).next()
            ti, ti_s = Buf(P, ph, "rti", [128, SQ], I32).next()
            tf, tf_s = Buf(P, ph, "rtf", [128, SQ], F32).next()
            u, u_s = Buf(P, ph, "ru", [128, SQ], F32).next()
            tabs = Buf(P, ph, "rtab", [128, SQ], F32, n=2)
            for q in range(nseq):
                P.dma("pool", "posld", lambda e, q=q: e.dma_start(out=posi[:], in_=pos_in[:, q * SQ:(q + 1) * SQ]), writes=[posi_s])
                P.op("dve", lambda e: e.tensor_copy(out=pf[:], in_=posi[:]), reads=[posi_s], writes=[pf_s])
                P.op("dve", lambda e: e.tensor_scalar(out=pf[:], in0=pf[:], scalar1=rc_t[:, 0:1], scalar2=None, op0=ALU.mult), reads=[pf_s, rc_s], writes=[pf_s])
                for idx, off in ((1, 0.0), (0, 0.25)):
                    P.op("dve", lambda e, off=off: e.tensor_scalar(out=u[:], in0=pf[:], scalar1=off, scalar2=None, op0=ALU.add), reads=[pf_s], writes=[u_s])
                    P.op("dve", lambda e: e.tensor_copy(out=ti[:], in_=u[:]), reads=[u_s], writes=[ti_s])
                    P.op("dve", lambda e: e.tensor_copy(out=tf[:], in_=ti[:]), reads=[ti_s], writes=[tf_s])
                    P.op("dve", lambda e: e.tensor_tensor(out=u[:], in0=u[:], in1=tf[:], op=ALU.subtract), reads=[u_s, tf_s], writes=[u_s])
                    P.op("dve", lambda e: e.tensor_single_scalar(out=tf[:], in_=u[:], scalar=0.5, op=ALU.is_gt), reads=[u_s], writes=[tf_s])
                    P.op("dve", lambda e: e.tensor_tensor(out=u[:], in0=u[:], in1=tf[:], op=ALU.subtract), reads=[u_s, tf_s], writes=[u_s])
                    P.op("dve", lambda e: e.tensor_single_scalar(out=tf[:], in_=u[:], scalar=-0.5, op=ALU.is_lt), reads=[u_s], writes=[tf_s])
                    P.op("dve", lambda e: e.tensor_tensor(out=u[:], in0=u[:], in1=tf[:], op=ALU.add), reads=[u_s, tf_s], writes=[u_s])
                    tb, tbs = tabs.next()
                    P.op("act", lambda e, tb=tb: e.activation(out=tb[:], in_=u[:], func=AF.Sin, scale=2.0 * math.pi), reads=[u_s], writes=[tbs])
                    if idx == 1:
                        P.op("dve", lambda e, tb=tb: e.tensor_scalar(out=tb[:], in0=tb[:], scalar1=rc_t[:, 1:2], scalar2=None, op0=ALU.mult), reads=[tbs, rc_s], writes=[tbs])
                    P.dma("pool", f"ropest{tabs.i}", lambda e, tb=tb, q=q, idx=idx: e.dma_start(out=ROPE[q, idx], in_=tb[:]), reads=[tbs], writes=[rope_slots[q]])
            P.fence()
            ph.close()

        def mixer_stage(l, q, parts=("diff", "gdn")):
            tiles = list(range(q * NQ, (q + 1) * NQ))
            phm = contextlib.ExitStack()
            xTs, xTs_s = Buf(P, phm, "xTs", [128, NC_, SQ], BF16).next()
            for iq, t in enumerate(tiles):
                P.dma("pool", "xTsld", lambda e, iq=iq, t=t: e.dma_start(out=xTs[:, :, iq * 512:(iq + 1) * 512], in_=XB[t]), reads=[xb_slots[t]], writes=[xTs_s])
            wpt = Buf(P, phm, "wpt", [128, NC_, 128], BF16, n=3)

            def wget(kind, m):
                wt, ws = wpt.next()
                src = {"pr": WPR, "sw": WSW}[kind][l][m]
                P.dma("sp", f"wpt{wpt.i}", lambda e: e.dma_start(out=wt[:], in_=src), reads=[wslot[kind, l, m]], writes=[ws])
                return wt, ws

            def proj_fm(wt, ws, iq, pb, ps, mlo=0, mhi=128):
                for k in range(NC_):
                    P.op("pe", lambda e, k=k: e.matmul(pb[0:mhi - mlo, :], wt[:, k, mlo:mhi], xTs[:, k, iq * 512:(iq + 1) * 512],
                                                      start=(k == 0), stop=(k == NC_ - 1)),
                         reads=[ws, xTs_s], writes=[ps], signal=(k == NC_ - 1))

            def store_mix(oT, oT_s, j):
                for iq, t in enumerate(tiles):
                    P.dma("pool", "mixst", lambda e, iq=iq, t=t: e.dma_start(out=MIX[t][:, j, :], in_=oT[:, iq * 512:(iq + 1) * 512]),
                          reads=[oT_s], writes=[mix_slots[t][j]])

            if "diff" in parts:
                ph = contextlib.ExitStack()
                Ct, Ct_s = Buf(P, ph, "Ct", [128, SQ], F32).next()
                St, St_s = Buf(P, ph, "St", [128, SQ], F32).next()
                P.dma("pool", "ropeld", lambda e: e.dma_start(out=Ct[:], in_=ROPE[q, 0]), reads=[rope_slots[q]], writes=[Ct_s])
                P.dma("pool", "ropeld", lambda e: e.dma_start(out=St[:], in_=ROPE[q, 1]), reads=[rope_slots[q]], writes=[St_s])
                qTb = Buf(P, ph, "qT", [128, SQ], BF16, n=2)
                kTb = Buf(P, ph, "kT", [128, SQ], BF16, n=2)
                vab = Buf(P, ph, "vaug", [128, NB, 132], BF16, n=2)
                for va_, va_s_ in zip(vab.t, vab.s):
                    P.op("pool", lambda e, va_=va_: e.memset(va_[:, :, 128:132], 1.0), writes=[va_s_])
                Ea, _ = Buf(P, ph, "Eall", [128, NB, 512], BF16).next()
                Es = [P.slot() for _ in range(NB)]
                r1 = Buf(P, ph, "r1", [128, 512], F32, n=2)
                r2 = Buf(P, ph, "r2", [128, 512], F32, n=2)
                o1, o1_s = Buf(P, ph, "o1", [128, 4, 128], F32).next()
                ofb = Buf(P, ph, "of", [128, 128], F32, n=2)
                osq = Buf(P, ph, "osq", [128, 128], F32, n=2)
                onb = Buf(P, ph, "onb", [128, 128], F32, n=3)
                sm = Buf(P, ph, "smd", [128, 4], F32, n=4)
                oTb = Buf(P, ph, "oTd", [128, SQ], BF16, n=2)
                nlam = lam_t[:, 2 * l + 1:2 * l + 2]
                gdr = ng_t[:, l * 256:l * 256 + 128]
                dres = {}

                def thrDP(h):
                    qT, qT_s = qTb.next()
                    kT, kT_s = kTb.next()
                    va, va_s = vab.next()
                    dres[h] = (qT, qT_s, kT, kT_s, va, va_s)
                    for (ma, mb, dst, dst_s) in ((AQ + h, h, qT, qT_s), (AK + h, 8 + h, kT, kT_s)):
                        wa, was = wget("pr", ma)
                        wb, wbs = wget("sw", mb)
                        for iq in range(NQ):
                            tok = slice(iq * 512, (iq + 1) * 512)
                            pa, pas = bk[4]
                            pb_, pbs = bk[5]
                            proj_fm(wa, was, iq, pa, pas)
                            proj_fm(wb, wbs, iq, pb_, pbs)
                            t1, t1s = r1.next()
                            t2, t2s = r2.next()
                            P.op("dve", lambda e, t1=t1, tok=tok, pb_=pb_: e.tensor_tensor(out=t1[:], in0=pb_[:, :], in1=St[:, tok], op=ALU.mult), reads=[St_s], writes=[t1s, pbs])
                            P.op("dve", lambda e, t2=t2, tok=tok, pa=pa: e.tensor_tensor(out=t2[:], in0=pa[:, :], in1=Ct[:, tok], op=ALU.mult), reads=[Ct_s], writes=[t2s, pas])
                            P.op("pool", lambda e, t1=t1, t2=t2, dst=dst, tok=tok: e.tensor_tensor(out=dst[:, tok], in0=t1[:], in1=t2[:], op=ALU.add),
                                 reads=[t1s, t2s], writes=[dst_s])
                            yield
                    wv, wvs = wget("pr", AV + h)
                    for g4 in range(NQ):
                        pb, ps = bk[7]
                        for i4 in range(4):
                            tt = g4 * 4 + i4
                            for k in range(NC_):
                                P.op("pe", lambda e, pb=pb, i4=i4, tt=tt, k=k, wv=wv: e.matmul(pb[:, i4 * 128:(i4 + 1) * 128], xTs[:, k, tt * 128:(tt + 1) * 128], wv[:, k, :],
                                                                                      start=(k == 0), stop=(k == NC_ - 1)),
                                     reads=[wvs, xTs_s], writes=[ps], signal=(k == NC_ - 1))
                            yield
                        P.op("act", lambda e, pb=pb, g4=g4, va=va: e.copy(out=va[:, g4 * 4:(g4 + 1) * 4, 0:128], in_=pb[:, :].rearrange("p (a b) -> p a b", a=4)),
                             reads=[], writes=[va_s, ps])
                        yield

                def thrDA(h):
                    qT, qT_s, kT, kT_s, va, va_s = dres[h]
                    oT, oT_s = oTb.next()
                    for J in range(NQ):
                        for c in range(2):
                            cs = slice(c * 64, (c + 1) * 64)
                            for i in range(4 * J + 4):
                                sb_, ss_ = bk[i % 2]
                                P.op("pe", lambda e, sb_=sb_, i=i, cs=cs, J=J: e.matmul(sb_[:, :], kT[cs, i * 128:(i + 1) * 128], qT[cs, J * 512:(J + 1) * 512], start=True, stop=True),
                                     reads=[kT_s, qT_s], writes=[ss_])
                                P.op("act", lambda e, sb_=sb_, i=i: e.activation(out=Ea[:, i, :], in_=sb_[:, :], func=AF.Exp, scale=0.125), reads=[], writes=[Es[i], ss_])
                                r = i - 4 * J
                                if r >= 0:
                                    P.op("pool", lambda e, i=i, r=r: e.tensor_tensor(out=Ea[:, i, r * 128:(r + 1) * 128], in0=Ea[:, i, r * 128:(r + 1) * 128], in1=tribf_t[:], op=ALU.mult),
                                         reads=[tribf_s], writes=[Es[i]])
                                if i % 2 == 1:
                                    yield
                            pend = None
                            for u in range(4):
                                ob, obs = bk[2 + u % 2]
                                last = 4 * J + u
                                for i in range(last + 1):
                                    P.op("pe", lambda e, ob=ob, i=i, u=u, last=last: e.matmul(ob[:, 0:129], Ea[:, i, u * 128:(u + 1) * 128], va[:, i, 0:129], start=(i == 0), stop=(i == last)),
                                         reads=[Es[i], va_s], writes=[obs], signal=(i == last))
                                if pend is not None:
                                    pend()
                                    pend = None
                                s4, s4s = sm.next()
                                P.op("dve", lambda e, ob=ob, s4=s4: e.reciprocal(out=s4[:, 0:1], in_=ob[:, 128:129]), reads=[], writes=[s4s, obs])
                                if c == 0:
                                    P.op("act", lambda e, ob=ob, s4=s4, u=u: e.activation(out=o1[:, u, :], in_=ob[:, 0:128], func=AF.Identity, scale=s4[:, 0:1]),
                                         reads=[s4s], writes=[o1_s, obs])
                                else:
                                    P.op("dve", lambda e, s4=s4: e.tensor_scalar(out=s4[:, 1:2], in0=s4[:, 0:1], scalar1=nlam, scalar2=None, op0=ALU.mult), reads=[lam_s], writes=[s4s])
                                    of, ofs = ofb.next()
                                    P.op("dve", lambda e, ob=ob, s4=s4, u=u, of=of: e.scalar_tensor_tensor(out=of[:], in0=ob[:, 0:128], scalar=s4[:, 1:2], in1=o1[:, u, :],
                                                                                                              op0=ALU.mult, op1=ALU.add), reads=[s4s, o1_s], writes=[ofs, obs])
                                    sq, sqs = osq.next()
                                    P.op("dve", lambda e, sq=sq, of=of: e.tensor_tensor(out=sq[:], in0=of[:], in1=of[:], op=ALU.mult), reads=[ofs], writes=[sqs])
                                    P.op("dve", lambda e, sq=sq, s4=s4: e.reduce_sum(out=s4[:, 2:3], in_=sq[:], axis=AX), reads=[sqs], writes=[s4s])
                                    P.op("dve", lambda e, s4=s4: e.tensor_scalar(out=s4[:, 2:3], in0=s4[:, 2:3], scalar1=1.0 / 128, scalar2=1e-5, op0=ALU.mult, op1=ALU.add), reads=[s4s], writes=[s4s])
                                    P.op("act", lambda e, s4=s4: e.activation(out=s4[:, 2:3], in_=s4[:, 2:3], func=AF.Sqrt), reads=[s4s], writes=[s4s])
                                    P.op("dve", lambda e, s4=s4: e.reciprocal(out=s4[:, 3:4], in_=s4[:, 2:3]), reads=[s4s], writes=[s4s])
                                    on, ons = onb.next()
                                    P.op("dve", lambda e, on=on, of=of, s4=s4: e.scalar_tensor_tensor(out=on[:], in0=of[:], scalar=s4[:, 3:4], in1=gdr, op0=ALU.mult, op1=ALU.mult),
                                         reads=[ofs, s4s, ng_s], writes=[ons])

                                    def pend(on=on, ons=ons, u=u):
                                        tb, tbs = bk[6]
                                        P.op("pe", lambda e: e.transpose(tb[:, u * 128:(u + 1) * 128], on[:], it[:]), reads=[ons, isl], writes=[tbs])
                                yield
                            if c == 1:
                                pend()
                                tb, tbs = bk[6]
                                P.op("act", lambda e, tb=tb, J=J, oT=oT: e.copy(out=oT[:, J * 512:(J + 1) * 512], in_=tb[:, :]), reads=[], writes=[oT_s, tbs])
                    store_mix(oT, oT_s, h)

                def drain_(g):
                    for _ in g:
                        pass
                drain_(thrDP(0))
                for h in range(8):
                    gp = thrDP(h + 1) if h + 1 < 8 else iter(())
                    cnt_ = 0
                    for _ in thrDA(h):
                        cnt_ += 1
                        if cnt_ % 3 == 0:
                            next(gp, None)
                    drain_(gp)
                P.fence()
                ph.close()

            if "gdn" in parts:
                ph = contextlib.ExitStack()

                def tmb(name):
                    return Buf(P, ph, name, [128, NB, 8], F32).next()
                btm, btm_s = tmb("btm")
                nbtm, nbtm_s = tmb("nbtm")
                gtm, gtm_s = tmb("gtm")
                gctm, gctm_s = tmb("gctm")
                gltm, gltm_s = tmb("gltm")
                egtm, egtm_s = tmb("egtm")
                kdtm, kdtm_s = tmb("kdtm")
                bkeg, bkeg_s = tmb("bkeg")
                ph2 = contextlib.ExitStack()
                wba, wba_s = Buf(P, ph2, "wba", [128, NC_, 16], BF16).next()
                P.dma("sp", "wba", lambda e: e.dma_start(out=wba[:], in_=WBA[l]), reads=[wslot["ba", l]], writes=[wba_s])
                bfm, bfm_s = Buf(P, ph2, "bfm", [8, SQ], F32).next()
                gfm, gfm_s = Buf(P, ph2, "gfm", [8, SQ], F32).next()
                for iq in range(NQ):
                    tok = slice(iq * 512, (iq + 1) * 512)
                    pa, pas = bk[0]
                    pb_, pbs = bk[1]
                    proj_fm(wba, wba_s, iq, pa, pas, 0, 8)
                    proj_fm(wba, wba_s, iq, pb_, pbs, 8, 16)
                    P.op("act", lambda e, tok=tok: e.activation(out=bfm[:, tok], in_=pa[0:8, :], func=AF.Sigmoid), reads=[], writes=[bfm_s, pas])
                    P.op("act", lambda e, tok=tok: e.activation(out=gfm[:, tok], in_=pb_[0:8, :], func=AF.Exp, bias=hp_t[0:8, 2 * l + 1:2 * l + 2], scale=1.0),
                         reads=[hp_s], writes=[gfm_s, pbs])
                P.op("dve", lambda e: e.tensor_scalar(out=gfm[:, :], in0=gfm[:, :], scalar1=1.0, scalar2=None, op0=ALU.add), reads=[gfm_s], writes=[gfm_s])
                P.op("act", lambda e: e.activation(out=gfm[:, :], in_=gfm[:, :], func=AF.Ln), reads=[gfm_s], writes=[gfm_s])
                P.op("dve", lambda e: e.tensor_scalar(out=gfm[:, :], in0=gfm[:, :], scalar1=nA_t[0:8, l:l + 1], scalar2=None, op0=ALU.mult), reads=[gfm_s, nA_s], writes=[gfm_s])
                for (src, src_s, dst, dst_s, bki) in ((bfm, bfm_s, btm, btm_s, 2), (gfm, gfm_s, gtm, gtm_s, 3)):
                    pb, ps = bk[bki]
                    for b in range(NB):
                        P.op("pe", lambda e, pb=pb, src=src, b=b: e.transpose(pb[:, b * 8:(b + 1) * 8], src[0:8, b * 128:(b + 1) * 128], it[0:8, 0:8]),
                             reads=[src_s, isl], writes=[ps], signal=(b == NB - 1))
                    P.op("dve", lambda e, pb=pb, dst=dst: e.tensor_copy(out=dst[:, :, :].rearrange("p a b -> p (a b)"), in_=pb[:, 0:NB * 8]), reads=[], writes=[dst_s, ps])
                for (msk, dst, dst_s, bki) in ((UINC, gctm, gctm_s, 4), (BLKM, gltm, gltm_s, 5)):
                    pb, ps = bk[bki]
                    for b in range(NB):
                        P.op("pe", lambda e, pb=pb, msk=msk, b=b: e.matmul(pb[:, b * 8:(b + 1) * 8], msk, gtm[:, b, :], start=True, stop=True),
                             reads=[mk_s, gtm_s], writes=[ps], signal=(b == NB - 1))
                    P.op("dve", lambda e, pb=pb, dst=dst: e.tensor_copy(out=dst[:, :, :].rearrange("p a b -> p (a b)"), in_=pb[:, 0:NB * 8]), reads=[], writes=[dst_s, ps])
                fl = lambda t_: t_[:, :, :].rearrange("p a b -> p (a b)")
                P.op("act", lambda e: e.activation(out=fl(egtm), in_=fl(gctm), func=AF.Exp), reads=[gctm_s], writes=[egtm_s])
                P.op("dve", lambda e: e.tensor_tensor(out=fl(kdtm), in0=fl(gltm), in1=fl(gctm), op=ALU.subtract), reads=[gltm_s, gctm_s], writes=[kdtm_s])
                P.op("act", lambda e: e.activation(out=fl(kdtm), in_=fl(kdtm), func=AF.Exp), reads=[kdtm_s], writes=[kdtm_s])
                P.op("dve", lambda e: e.tensor_scalar(out=fl(nbtm), in0=fl(btm), scalar1=-1.0, scalar2=None, op0=ALU.mult), reads=[btm_s], writes=[nbtm_s])
                P.op("dve", lambda e: e.tensor_tensor(out=fl(bkeg), in0=fl(btm), in1=fl(egtm), op=ALU.mult), reads=[btm_s, egtm_s], writes=[bkeg_s])
                P.fence()
                ph2.close()
                upad, upad_s = Buf(P, ph, "upad", [128, SQ + 4], F32).next()
                cv, cv_s = Buf(P, ph, "cv", [128, SQ], F32).next()
                gqTb = Buf(P, ph, "gqT", [128, SQ], BF16, n=2)
                gkTb = Buf(P, ph, "gkT", [128, SQ], BF16, n=2)
                ktmb = Buf(P, ph, "ktm", [128, NB, 128], BF16, n=2)
                vtmb = Buf(P, ph, "vtm", [128, NB, 128], BF16, n=2)
                qdTb = Buf(P, ph, "qdT", [128, SQ], BF16, n=2)
                qdT_qs = [[P.slot() for _ in range(NB // 4)] for _ in range(2)]
                zs, zs_s = Buf(P, ph, "zsb", [128, SQ], BF16).next()
                otm, otm_s = Buf(P, ph, "otm", [128, NB, 128], F32).next()
                goTb = Buf(P, ph, "oTg", [128, SQ], BF16, n=1)
                sqb = Buf(P, ph, "sqb", [128, 512], BF16, n=2)
                gbt = Buf(P, ph, "gbt", [128, 128], F32, n=2)
                EGr, EGr_s = Buf(P, ph, "EGr", [128, 512], F32).next()
                cdqb = Buf(P, ph, "cdq", [128, 8], F32, n=2)
                Dt, Dt_s = Buf(P, ph, "Dt", [128, 4, 128], F32).next()
                DTt, DTt_s = Buf(P, ph, "DTt", [128, 4, 128], F32).next()
                Mf, Mf_s = Buf(P, ph, "Mf", [128, 4, 128], F32).next()
                Yf, Yf_s = Buf(P, ph, "Yf", [128, 4, 128], F32).next()
                Pbs = Buf(P, ph, "Pb", [128, 4, 128], BF16, n=2)
                Qbs = Buf(P, ph, "Qb", [128, 4, 128], BF16, n=2)
                Ybb = Buf(P, ph, "Yb", [128, 4, 128], BF16, n=2)
                aTtb = Buf(P, ph, "attnT", [128, 4, 128], BF16, n=2)
                rvb = Buf(P, ph, "rhsv", [128, 4, 128], BF16, n=2)
                rk, rk_s = Buf(P, ph, "rhsk", [128, 4, 128], BF16).next()
                kdb = Buf(P, ph, "kdec", [128, 4, 128], BF16, n=2)
                nwb = Buf(P, ph, "nwT", [128, 4, 128], BF16, n=2)
                vn, vn_s = Buf(P, ph, "vnew", [128, 128], BF16).next()
                Sf, Sf_s = Buf(P, ph, "Sf", [128, 128], F32).next()
                Sb, Sb_s = Buf(P, ph, "Sb", [128, 128], BF16).next()
                gonb = Buf(P, ph, "gonb", [128, 128], F32, n=2)
                gosq = Buf(P, ph, "gosq", [128, 128], F32, n=2)
                gsm = Buf(P, ph, "smg", [128, 4], F32, n=4)
                gdl = ng_t[:, l * 256 + 128:l * 256 + 256]
                flq = lambda t_: t_[:, :, :].rearrange("p a b -> p (a b)")
                NQD = NB // 4
                hd = {}
                qres = {}

                def conv_silu(m, ch):
                    wt, ws = wget("pr", m)
                    P.op("pool", lambda e: e.memset(upad[:, 0:3], 0.0), writes=[upad_s])
                    for iq in range(NQ):
                        pb, ps = bk[0]
                        proj_fm(wt, ws, iq, pb, ps)
                        P.op("act", lambda e, pb=pb, iq=iq: e.copy(out=upad[:, 3 + iq * 512:3 + (iq + 1) * 512], in_=pb[:, :]), reads=[], writes=[upad_s, ps])
                        yield
                    base = (l * 24 + ch) * 4
                    P.op("dve", lambda e: e.tensor_scalar(out=cv[:], in0=upad[:, 0:SQ], scalar1=cw_t[:, base:base + 1], scalar2=None, op0=ALU.mult),
                         reads=[upad_s, cw_s], writes=[cv_s])
                    for j in range(1, 4):
                        P.op("dve", lambda e, j=j: e.scalar_tensor_tensor(out=cv[:], in0=upad[:, j:j + SQ], scalar=cw_t[:, base + j:base + j + 1], in1=cv[:],
                                                                          op0=ALU.mult, op1=ALU.add), reads=[upad_s, cw_s, cv_s], writes=[cv_s])
                    P.op("act", lambda e: e.activation(out=cv[:], in_=cv[:], func=AF.Silu), reads=[cv_s], writes=[cv_s])
                    yield

                def l2n():
                    for iq in range(NQ):
                        tok = slice(iq * 512, (iq + 1) * 512)
                        sq, sqs = sqb.next()
                        P.op("act", lambda e, sq=sq, tok=tok: e.activation(out=sq[:], in_=cv[:, tok], func=AF.Square), reads=[cv_s], writes=[sqs])
                        pb, ps = bk[0]
                        P.op("pe", lambda e, pb=pb, sq=sq: e.matmul(pb[:, :], ot[:], sq[:], start=True, stop=True), reads=[osl, sqs], writes=[ps])
                        P.op("dve", lambda e, pb=pb, tok=tok: e.tensor_scalar(out=upad[:, tok], in0=pb[:, :], scalar1=1e-6, scalar2=None, op0=ALU.add), reads=[], writes=[upad_s, ps])
                    P.op("act", lambda e: e.activation(out=upad[:, 0:SQ], in_=upad[:, 0:SQ], func=AF.Sqrt), reads=[upad_s], writes=[upad_s])
                    P.op("dve", lambda e: e.reciprocal(out=upad[:, 0:SQ], in_=upad[:, 0:SQ]), reads=[upad_s], writes=[upad_s])
                    yield

                def to_tm(dst, dst_s):
                    for b4 in range(NB // 4):
                        pb, ps = bk[1]
                        for i4 in range(4):
                            b = b4 * 4 + i4
                            P.op("pe", lambda e, pb=pb, i4=i4, b=b: e.transpose(pb[:, i4 * 128:(i4 + 1) * 128], cv[:, b * 128:(b + 1) * 128], it[:]),
                                 reads=[cv_s, isl], writes=[ps], signal=(i4 == 3))
                        P.op("act", lambda e, pb=pb, b4=b4: e.copy(out=dst[:, b4 * 4:(b4 + 1) * 4, :], in_=pb[:, :].rearrange("p (a b) -> p a b", a=4)),
                             reads=[], writes=[dst_s, ps])
                        yield

                def thrA(h):
                    r = {}
                    r["gqT"], r["gqT_s"] = gqTb.next()
                    r["gkT"], r["gkT_s"] = gkTb.next()
                    r["ktm"], r["ktm_s"] = ktmb.next()
                    r["vtm"], r["vtm_s"] = vtmb.next()
                    r["qdT"], _ = qdTb.next()
                    r["qdT_qs"] = qdT_qs[qdTb.i]
                    hd[h] = r
                    gqT, gqT_s, gkT, gkT_s = r["gqT"], r["gqT_s"], r["gkT"], r["gkT_s"]
                    yield from conv_silu(DQ + h, h)
                    yield from l2n()
                    P.op("dve", lambda e: e.scalar_tensor_tensor(out=gqT[:], in0=cv[:], scalar=128 ** -0.5, in1=upad[:, 0:SQ], op0=ALU.mult, op1=ALU.mult),
                         reads=[cv_s, upad_s], writes=[gqT_s])
                    yield
                    yield from conv_silu(DK + h, 8 + h)
                    yield from l2n()
                    P.op("dve", lambda e: e.tensor_tensor(out=cv[:], in0=cv[:], in1=upad[:, 0:SQ], op=ALU.mult), reads=[cv_s, upad_s], writes=[cv_s])
                    P.op("pool", lambda e: e.tensor_copy(out=gkT[:], in_=cv[:]), reads=[cv_s], writes=[gkT_s])
                    yield from to_tm(r["ktm"], r["ktm_s"])
                    yield from conv_silu(DV + h, 16 + h)
                    yield from to_tm(r["vtm"], r["vtm_s"])

                def thrB(h, qd):
                    r = hd[h]
                    gqT, gqT_s, gkT, gkT_s, ktm, ktm_s, vtm, vtm_s = r["gqT"], r["gqT_s"], r["gkT"], r["gkT_s"], r["ktm"], r["ktm_s"], r["vtm"], r["vtm_s"]
                    qdT, qdT_s = r["qdT"], r["qdT_qs"][qd]
                    hs = slice(h, h + 1)
                    Yb, Yb_s = Ybb.next()
                    aTt, aTt_s = aTtb.next()
                    rv, rv_s = rvb.next()
                    kd, kd_s = kdb.next()
                    nw, nw_s = nwb.next()
                    cdq, cdq_s = cdqb.next()
                    qres[h, qd] = dict(Yb=Yb, Yb_s=Yb_s, aTt=aTt, aTt_s=aTt_s, rv=rv, rv_s=rv_s, kd=kd, kd_s=kd_s, nw=nw, nw_s=nw_s, cdq=cdq, cdq_s=cdq_s)
                    qtok = slice(qd * 512, (qd + 1) * 512)
                    g0, g0s = bk[1]
                    g1, g1s = bk[2]
                    for j4 in range(4):
                        b = qd * 4 + j4
                        gb, gbs = gbt.next()
                        P.op("dve", lambda e, gb=gb, b=b: e.tensor_scalar(out=gb[:], in0=onf_t[:], scalar1=gtm[:, b, hs], scalar2=None, op0=ALU.mult),
                             reads=[onf_s, gtm_s], writes=[gbs])
                        P.op("pe", lambda e, gb=gb, j4=j4: e.matmul(g0[:, j4 * 128:(j4 + 1) * 128], gb[:], UINC, start=True, stop=True), reads=[gbs, mk_s], writes=[g0s])
                        P.op("pe", lambda e, gb=gb, j4=j4: e.matmul(g1[:, j4 * 2:(j4 + 1) * 2], gb[:], CIND, start=True, stop=True), reads=[gbs, mk_s], writes=[g1s])
                    yield
                    P.op("act", lambda e: e.activation(out=EGr[:], in_=g0[:, :], func=AF.Exp), reads=[], writes=[EGr_s, g0s])
                    P.op("act", lambda e: e.activation(out=cdq[:], in_=g1[:, 0:8], func=AF.Exp), reads=[], writes=[cdq_s, g1s])
                    P.op("dve", lambda e: e.tensor_tensor(out=qdT[:, qtok], in0=gqT[:, qtok], in1=EGr[:], op=ALU.mult), reads=[gqT_s, EGr_s], writes=[qdT_s])
                    for j4 in range(4):
                        b = qd * 4 + j4
                        gsl = slice(j4 * 128, (j4 + 1) * 128)
                        P.op("dve", lambda e, j4=j4, b=b, gsl=gsl: e.scalar_tensor_tensor(out=Dt[:, j4, :], in0=g0[:, gsl], scalar=gctm[:, b, hs], in1=zer_t[:],
                                                                                     op0=ALU.subtract, op1=ALU.max), reads=[gctm_s, zer_s], writes=[Dt_s, g0s])
                        P.op("dve", lambda e, j4=j4, b=b, gsl=gsl: e.scalar_tensor_tensor(out=DTt[:, j4, :], in0=g0[:, gsl], scalar=gctm[:, b, hs], in1=zer_t[:],
                                                                                     op0=ALU.subtract, op1=ALU.min), reads=[gctm_s, zer_s], writes=[DTt_s, g0s])
                    yield
                    P.op("act", lambda e: e.activation(out=flq(Dt), in_=flq(Dt), func=AF.Exp, scale=-1.0), reads=[Dt_s], writes=[Dt_s])
                    P.op("act", lambda e: e.activation(out=flq(DTt), in_=flq(DTt), func=AF.Exp), reads=[DTt_s], writes=[DTt_s])
                    for j4 in range(4):
                        P.op("pool", lambda e, j4=j4: e.tensor_tensor(out=Dt[:, j4, :], in0=Dt[:, j4, :], in1=LSTR, op=ALU.mult), reads=[mk_s], writes=[Dt_s])
                        P.op("pool", lambda e, j4=j4: e.tensor_tensor(out=DTt[:, j4, :], in0=DTt[:, j4, :], in1=UINC, op=ALU.mult), reads=[mk_s], writes=[DTt_s])
                    yield
                    a2, a2s = bk[3]
                    a3, a3s = bk[1]
                    Pb, Pb_s = Pbs.next()
                    Qb, Qb_s = Qbs.next()
                    for j4 in range(4):
                        blk = slice((qd * 4 + j4) * 128, (qd * 4 + j4 + 1) * 128)
                        P.op("pe", lambda e, j4=j4, blk=blk: e.matmul(a2[:, j4 * 128:(j4 + 1) * 128], gkT[:, blk], gkT[:, blk], start=True, stop=True),
                             reads=[gkT_s], writes=[a2s], signal=(j4 == 3))
                    for j4 in range(4):
                        b = qd * 4 + j4
                        P.op("dve", lambda e, j4=j4, b=b: e.scalar_tensor_tensor(out=Mf[:, j4, :], in0=a2[:, j4 * 128:(j4 + 1) * 128], scalar=nbtm[:, b, hs], in1=Dt[:, j4, :],
                                                                                 op0=ALU.mult, op1=ALU.mult), reads=[nbtm_s, Dt_s], writes=[Mf_s, a2s])
                    P.op("pool", lambda e, Pb=Pb: e.tensor_copy(out=flq(Pb), in_=flq(Mf)), reads=[Mf_s], writes=[Pb_s])
                    yield
                    for j4 in range(4):
                        P.op("pe", lambda e, j4=j4: e.transpose(a3[:, j4 * 128:(j4 + 1) * 128], Mf[:, j4, :], it[:]), reads=[Mf_s, isl], writes=[a3s], signal=(j4 == 3))
                    P.op("act", lambda e, Qb=Qb: e.copy(out=flq(Qb), in_=a3[:, :]), reads=[], writes=[Qb_s, a3s])
                    for j4 in range(4):
                        P.op("dve", lambda e, j4=j4: e.tensor_tensor(out=Yf[:, j4, :], in0=a3[:, j4 * 128:(j4 + 1) * 128], in1=it[:], op=ALU.add), reads=[isl], writes=[Yf_s, a3s])
                    P.op("pool", lambda e: e.tensor_copy(out=flq(Yb), in_=flq(Yf)), reads=[Yf_s], writes=[Yb_s])
                    yield
                    for n in range(5):
                        p4, p4s = bk[2]
                        p5, p5s = bk[3]
                        p6, p6s = bk[1]
                        Pn, Pn_s = Pbs.next()
                        for j4 in range(4):
                            P.op("pe", lambda e, j4=j4, Qb=Qb, Pb=Pb: e.matmul(p4[:, j4 * 128:(j4 + 1) * 128], Qb[:, j4, :], Pb[:, j4, :], start=True, stop=True),
                                 reads=[Qb_s, Pb_s], writes=[p4s], signal=(j4 == 3))
                        if n < 4:
                            Qn, Qn_s = Qbs.next()
                            for j4 in range(4):
                                P.op("pe", lambda e, j4=j4, Qb=Qb, Pb=Pb: e.matmul(p5[:, j4 * 128:(j4 + 1) * 128], Pb[:, j4, :], Qb[:, j4, :], start=True, stop=True),
                                     reads=[Qb_s, Pb_s], writes=[p5s], signal=(j4 == 3))
                        P.op("dve", lambda e, Pn=Pn: e.tensor_copy(out=flq(Pn), in_=p4[:, :]), reads=[], writes=[Pn_s, p4s])
                        if n < 4:
                            P.op("act", lambda e, Qn=Qn: e.copy(out=flq(Qn), in_=p5[:, :]), reads=[], writes=[Qn_s, p5s])
                        yield
                        for j4 in range(4):
                            P.op("pe", lambda e, j4=j4, Pn=Pn: e.matmul(p6[:, j4 * 128:(j4 + 1) * 128], Pn[:, j4, :], Yb[:, j4, :], start=True, stop=True),
                                 reads=[Pn_s, Yb_s], writes=[p6s], signal=(j4 == 3))
                        P.op("dve", lambda e: e.tensor_tensor(out=flq(Yf), in0=p6[:, :], in1=flq(Yf), op=ALU.add), reads=[Yf_s], writes=[Yf_s, p6s])
                        P.op("act", lambda e: e.copy(out=flq(Yb), in_=flq(Yf)), reads=[Yf_s], writes=[Yb_s])
                        Pb, Pb_s = Pn, Pn_s
                        if n < 4:
                            Qb, Qb_s = Qn, Qn_s
                        yield
                    aq2, aq2s = bk[2]
                    aw3, aw3s = bk[3]
                    for j4 in range(4):
                        blk = slice((qd * 4 + j4) * 128, (qd * 4 + j4 + 1) * 128)
                        P.op("pe", lambda e, j4=j4, blk=blk: e.matmul(aq2[:, j4 * 128:(j4 + 1) * 128], gkT[:, blk], gqT[:, blk], start=True, stop=True),
                             reads=[gkT_s, gqT_s], writes=[aq2s], signal=(j4 == 3))
                    P.op("dve", lambda e: e.tensor_tensor(out=flq(aTt), in0=aq2[:, :], in1=flq(DTt), op=ALU.mult), reads=[DTt_s], writes=[aTt_s, aq2s])
                    for j4 in range(4):
                        b = qd * 4 + j4
                        P.op("act", lambda e, j4=j4, b=b: e.activation(out=rv[:, j4, :], in_=vtm[:, b, :], func=AF.Copy, scale=btm[:, b, hs]),
                             reads=[vtm_s, btm_s], writes=[rv_s])
                        P.op("act", lambda e, j4=j4, b=b: e.activation(out=rk[:, j4, :], in_=ktm[:, b, :], func=AF.Copy, scale=bkeg[:, b, hs]),
                             reads=[ktm_s, bkeg_s], writes=[rk_s])
                        P.op("act", lambda e, j4=j4, b=b: e.activation(out=kd[:, j4, :], in_=ktm[:, b, :], func=AF.Copy, scale=kdtm[:, b, hs]),
                             reads=[ktm_s, kdtm_s], writes=[kd_s])
                    yield
                    for j4 in range(4):
                        P.op("pe", lambda e, j4=j4: e.matmul(aw3[:, j4 * 128:(j4 + 1) * 128], rk[:, j4, :], Yb[:, j4, :], start=True, stop=True),
                             reads=[rk_s, Yb_s], writes=[aw3s], signal=(j4 == 3))
                    P.op("act", lambda e: e.mul(out=flq(nw), in_=aw3[:, :], mul=-1.0), reads=[], writes=[nw_s, aw3s])
                    yield

                def thrC(h, qd):
                    r = hd[h]
                    qdT, qdT_s = r["qdT"], r["qdT_qs"][qd]
                    z = qres[h, qd]
                    Yb, Yb_s, aTt, aTt_s, rv, rv_s, kd, kd_s, nw, nw_s, cdq, cdq_s = (z["Yb"], z["Yb_s"], z["aTt"], z["aTt_s"], z["rv"], z["rv_s"],
                                                                                      z["kd"], z["kd_s"], z["nw"], z["nw_s"], z["cdq"], z["cdq_s"])
                    if qd == 0:
                        P.op("pool", lambda e: e.memset(Sf[:], 0.0), writes=[Sf_s])
                        P.op("pool", lambda e: e.memset(Sb[:], 0.0), writes=[Sb_s])
                    for j4 in range(4):
                        b = qd * 4 + j4
                        blk = slice(b * 128, (b + 1) * 128)
                        for x in range(2):
                            R = slice(64 * x, 64 * x + 64)
                            pv, pvs = bk[7]
                            po, pos_ = bk[4 + x]
                            pst, psts = bk[6]
                            P.op("pe", lambda e, j4=j4: e.matmul(pv[:, 0:128], Yb[:, j4, :], rv[:, j4, :], start=True, stop=False), reads=[Yb_s, rv_s], writes=[pvs], signal=False)
                            P.op("pe", lambda e, j4=j4: e.matmul(pv[:, 0:128], nw[:, j4, :], Sb[:], start=False, stop=True), reads=[nw_s, Sb_s], writes=[pvs])
                            P.op("dve", lambda e, R=R: e.tensor_copy(out=vn[R, :], in_=pv[R, 0:128]), reads=[], writes=[vn_s, pvs])
                            P.op("pe", lambda e, blk=blk, po=po: e.matmul(po[:, 0:128], qdT[:, blk], Sb[:], start=True, stop=False), reads=[qdT_s, Sb_s], writes=[pos_], signal=False)
                            P.op("pe", lambda e, j4=j4, R=R, po=po: e.matmul(po[:, 0:128], aTt[R, j4, :], vn[R, :], start=False, stop=True), reads=[aTt_s, vn_s], writes=[pos_])
                            P.op("act", lambda e, R=R, b=b, po=po: e.copy(out=otm[R, b, :], in_=po[R, 0:128]), reads=[], writes=[otm_s, pos_])
                            P.op("pe", lambda e, j4=j4, R=R: e.matmul(pst[:, 0:128], kd[R, j4, :], vn[R, :], start=True, stop=True), reads=[kd_s, vn_s], writes=[psts])
                            ci_ = 2 * j4 + x
                            P.op("dve", lambda e, ci_=ci_: e.scalar_tensor_tensor(out=Sb[:], in0=Sf[:], scalar=cdq[:, ci_:ci_ + 1], in1=pst[:, 0:128], op0=ALU.mult, op1=ALU.add),
                                 reads=[cdq_s, Sf_s], writes=[Sb_s, psts])
                            P.op("dve", lambda e, ci_=ci_: e.scalar_tensor_tensor(out=Sf[:], in0=Sf[:], scalar=cdq[:, ci_:ci_ + 1], in1=pst[:, 0:128], op0=ALU.mult, op1=ALU.add),
                                 reads=[cdq_s], writes=[Sf_s, psts])
                            yield

                def thrG(h):
                    wz, wzs = wget("pr", DZ + h)
                    for iq in range(NQ):
                        pb, ps = bk[4 + iq % 2]
                        proj_fm(wz, wzs, iq, pb, ps)
                        P.op("act", lambda e, pb=pb, iq=iq: e.activation(out=zs[:, iq * 512:(iq + 1) * 512], in_=pb[:, :], func=AF.Silu), reads=[], writes=[zs_s, ps])
                        yield
                    goT, goT_s = goTb.next()
                    for b4 in range(NB // 4):
                        tb, tbs = bk[4 + b4 % 2]
                        for i4 in range(4):
                            b = b4 * 4 + i4
                            sq, sqs = gosq.next()
                            s4, s4s = gsm.next()
                            P.op("dve", lambda e, sq=sq, b=b: e.tensor_tensor(out=sq[:], in0=otm[:, b, :], in1=otm[:, b, :], op=ALU.mult), reads=[otm_s], writes=[sqs])
                            P.op("dve", lambda e, sq=sq, s4=s4: e.reduce_sum(out=s4[:, 0:1], in_=sq[:], axis=AX), reads=[sqs], writes=[s4s])
                            P.op("dve", lambda e, s4=s4: e.tensor_scalar(out=s4[:, 0:1], in0=s4[:, 0:1], scalar1=1.0 / 128, scalar2=1e-6, op0=ALU.mult, op1=ALU.add), reads=[s4s], writes=[s4s])
                            P.op("act", lambda e, s4=s4: e.activation(out=s4[:, 0:1], in_=s4[:, 0:1], func=AF.Sqrt), reads=[s4s], writes=[s4s])
                            P.op("dve", lambda e, s4=s4: e.reciprocal(out=s4[:, 1:2], in_=s4[:, 0:1]), reads=[s4s], writes=[s4s])
                            on, ons = gonb.next()
                            P.op("dve", lambda e, on=on, b=b, s4=s4: e.scalar_tensor_tensor(out=on[:], in0=otm[:, b, :], scalar=s4[:, 1:2], in1=gdl, op0=ALU.mult, op1=ALU.mult),
                                 reads=[otm_s, s4s, ng_s], writes=[ons])
                            P.op("pe", lambda e, tb=tb, on=on, i4=i4: e.transpose(tb[:, i4 * 128:(i4 + 1) * 128], on[:], it[:]), reads=[ons, isl], writes=[tbs])
                        P.op("dve", lambda e, tb=tb, b4=b4, goT=goT: e.tensor_tensor(out=goT[:, b4 * 512:(b4 + 1) * 512], in0=tb[:, :], in1=zs[:, b4 * 512:(b4 + 1) * 512], op=ALU.mult),
                             reads=[zs_s], writes=[goT_s, tbs])
                        yield
                    store_mix(goT, goT_s, 8 + h)

                def drain(g):
                    for _ in g:
                        pass

                def merge(primary, others):
                    for _ in primary:
                        for g, w in others:
                            for _i in range(w):
                                next(g, None)

                NH = 8
                gA = thrA(0)
                drain(gA)
                gB = thrB(0, 0)
                drain(gB)
                for h in range(NH):
                    gA = thrA(h + 1) if h + 1 < NH else iter(())
                    for qd in range(NQD):
                        if qd + 1 < NQD:
                            gB = thrB(h, qd + 1)
                        elif h + 1 < NH:
                            drain(gA)
                            gB = thrB(h + 1, 0)
                        else:
                            gB = iter(())
                        merge(thrC(h, qd), [(gB, 2), (gA, 2)])
                        drain(gB)
                    drain(gA)
                    drain(thrG(h))
                P.fence()
                ph.close()
            if len(parts) < 2:
                ph = contextlib.ExitStack()
                zT, zT_s = Buf(P, ph, "zT", [128, SQ], BF16).next()
                P.op("pool", lambda e: e.memset(zT[:], 0.0), writes=[zT_s])
                for j in (range(8, 16) if "gdn" not in parts else range(0, 8)):
                    store_mix(zT, zT_s, j)
                P.fence()
                ph.close()
            P.fence()
            phm.close()
            ph = contextlib.ExitStack()
            b_ = ln_bufs(ph, 2)
            mtb = Buf(P, ph, "mixt", [128, NC_, TT], BF16, n=2)
            wmb = Buf(P, ph, "wmo", [128, NC_, 128], BF16, n=NC_)
            wres = []
            for c in range(NC_):
                wt, ws = wmb.next()
                P.dma("sp", f"wmo{c % 4}", lambda e, wt=wt, c=c: e.dma_start(out=wt[:], in_=WMO[l][c]), reads=[wslot["mo", l, c]], writes=[ws])
                wres.append((wt, ws))
            for t in tiles:
                mt, mts = mtb.next()
                P.dma("pool", f"mixld{mtb.i}", lambda e, mt=mt, t=t: e.dma_start(out=mt[:], in_=MIX[t]), reads=mix_slots[t], writes=[mts])

                def wload(c):
                    return wres[c]
                out_ln(l, 1, t, NC_, lambda j, mt=mt, mts=mts: (mt[:, j, :], mts), wload, 1.0 / ALPHA, b_)
            P.fence()
            ph.close()

        if "mix" in stages:
            rope_tables()
        for l in range(depth):
            if "ffn1" in stages:
                ffn_stage(l, 1, 0)
            if "mix" in stages:
                for q in range(nseq):
                    mixer_stage(l, q, cfg.get("parts", ("diff", "gdn")))
            if "ffn2" in stages:
                ffn_stage(l, 2, 2)

        outs = []
        ph = contextlib.ExitStack()
        xfst = Buf(P, ph, "xfst", [128, NC_, TT], F32, n=2)
        ytok = Buf(P, ph, "ytok", [128, D], F32, n=2)
        for t in range(ntile):
            ft, fs = xfst.next()
            P.dma("pool", f"xfld{xfst.i}", lambda e, ft=ft, t=t: e.dma_start(out=ft[:], in_=XF[t]),
                  reads=[xf_slots[t]], writes=[fs])
            for q4 in range(4):
                tt = t * 4 + q4
                yt, ys = ytok.next()
                for g in range(4):
                    pb, ps = bk[g % 2]
                    for i in range(4):
                        c = 4 * g + i
                        P.op("pe", lambda e, pb=pb, ft=ft, c=c, i=i, q4=q4: e.transpose(pb[:, i * 128:(i + 1) * 128], ft[:, c, q4 * 128:(q4 + 1) * 128], it[:]),
                             reads=[fs, isl], writes=[ps], signal=(i == 3))
                    if g % 2 == 0:
                        P.op("act", lambda e, yt=yt, pb=pb, g=g: e.copy(out=yt[:, g * 512:(g + 1) * 512], in_=pb[:, :]), reads=[ps], writes=[ys])
                    else:
                        P.op("dve", lambda e, yt=yt, pb=pb, g=g: e.tensor_copy(out=yt[:, g * 512:(g + 1) * 512], in_=pb[:, :]), reads=[ps], writes=[ys])
                osl_ = P.slot()
                outs.append(osl_)
                P.dma("pool", f"ytok{ytok.i}", lambda e, yt=yt, tt=tt: e.dma_start(out=y_out[tt * 128:(tt + 1) * 128, :], in_=yt[:]),
                      reads=[ys], writes=[osl_])
        P.wait_all("pool", outs)
        ph.close()
        P.emit(stack)
    return nc


def host_consts(depth, ln):
    a = np.stack(ln, axis=1)
    a = a.reshape(depth, 6, NC_, 128)
    a = np.transpose(a, (3, 0, 1, 2)).reshape(128, depth * 6 * NC_)
    return np.ascontiguousarray(a.astype(np.float32))


def host_mix_inputs(inp, depth, pos_core):
    f32 = np.float32
    w_in = np.asarray(inp["w_in"])[:depth]
    perm = np.arange(2048)
    d = perm % 64
    perm = np.where(d < 8, perm + 8, np.where(d < 16, perm - 8, perm))
    w_sw = np.ascontiguousarray(w_in[:, :, perm])
    conv_w = np.asarray(inp["conv_w"])[:depth]
    cw = conv_w.reshape(depth, 4, 24, 128).transpose(3, 0, 2, 1).reshape(128, depth * 96)
    hp8 = np.zeros((128, depth * 2), f32)
    hp8[:8, 0::2] = np.asarray(inp["a_log"])[:depth].T
    hp8[:8, 1::2] = np.asarray(inp["dt_bias"])[:depth].T
    lam = np.concatenate([np.asarray(inp[k])[:depth] for k in ("lam_q1", "lam_k1", "lam_q2", "lam_k2")], axis=1)
    lamv = np.broadcast_to(lam.reshape(1, depth * 256), (128, depth * 256))
    ng = np.concatenate([np.asarray(inp["diff_norm_g"])[:depth], np.asarray(inp["delta_norm_g"])[:depth]], axis=1)
    normg = np.broadcast_to(ng.reshape(1, depth * 256), (128, depth * 256))
    pos = np.broadcast_to(np.asarray(pos_core).reshape(1, -1).astype(np.int32), (128, pos_core.size))
    return {"w_in": np.ascontiguousarray(w_in), "w_in_sw": w_sw, "w_out": np.ascontiguousarray(np.asarray(inp["w_out"])[:depth]),
            "convw": np.ascontiguousarray(cw.astype(f32)), "hp8": hp8, "lamv": np.ascontiguousarray(lamv.astype(f32)),
            "normg": np.ascontiguousarray(normg.astype(f32)), "pos": np.ascontiguousarray(pos)}


def host_static_consts():
    f32 = np.float32
    p = np.arange(128)
    d = p % 64
    inv = 500000.0 ** (-(d % 8) / 8.0)
    ropec = np.zeros((128, 2), f32)
    ropec[:, 0] = np.where(d < 16, inv / (2 * np.pi), 0.0)
    ropec[:, 1] = np.where(d < 8, -1.0, np.where(d < 16, 1.0, 0.0))
    i = p[:, None]
    j = p[None, :]
    same = (i // 64) == (j // 64)
    masks = np.zeros((128, 640), f32)
    masks[:, 0:128] = (i <= j)
    masks[:, 128:256] = same & (i > j)
    masks[:, 256:384] = same & (i <= j)
    masks[:, 384] = (p < 64)
    masks[:, 385] = (p >= 64)
    masks[:, 512:640] = same
    return {"ropec": ropec, "masks": masks, "ident": np.eye(128, dtype=f32)}


def make_in_maps(inputs, n_cores, depth, stages=("ffn1", "mix", "ffn2")):
    x = np.asarray(inputs["x"])
    B, S, _ = x.shape
    per = B // n_cores
    pos = np.asarray(inputs["positions"])
    lnp = host_consts(depth, [np.asarray(inputs[k])[:depth] for k in ("ln1_g", "ln1_b", "ln2_g", "ln2_b", "ln3_g", "ln3_b")])
    st = host_static_consts()
    shared = {"lnp": lnp, "ident": st["ident"]}
    if "ffn1" in stages:
        shared["ffn1_w_in"] = np.ascontiguousarray(np.asarray(inputs["ffn1_w_in"])[:depth])
        shared["ffn1_w_out"] = np.ascontiguousarray(np.asarray(inputs["ffn1_w_out"])[:depth])
    if "ffn2" in stages:
        shared["ffn2_w_in"] = np.ascontiguousarray(np.asarray(inputs["ffn2_w_in"])[:depth])
        shared["ffn2_w_out"] = np.ascontiguousarray(np.asarray(inputs["ffn2_w_out"])[:depth])
    in_maps = []
    for c in range(n_cores):
        m = dict(shared)
        m["x"] = np.ascontiguousarray(x[c * per:(c + 1) * per].reshape(per * S, D))
        if "mix" in stages:
            mm = host_mix_inputs(inputs, depth, pos[c * per:(c + 1) * per].reshape(-1))
            if c > 0:
                for k in ("w_in", "w_in_sw", "w_out", "convw", "hp8", "lamv", "normg"):
                    mm[k] = in_maps[0][k]
            m.update(mm)
            m["ropec"] = st["ropec"]
            m["masks"] = st["masks"]
        in_maps.append(m)
    return in_maps, per, S


def kernel(**inputs):
    n = 8
    in_maps, per, S = make_in_maps(inputs, n, DEPTH)
    nc = build_program(dict(ntok=per * S, depth=DEPTH))
    res = run_bass_kernel_spmd(nc, in_maps, core_ids=list(range(n)))
    out = np.concatenate([np.asarray(r["y"]).reshape(per, S, D) for r in res.results], axis=0)
    return out.astype(np.float32)
```

```python
import contextlib
import math
import numpy as np
import concourse.bass as bass
import concourse.mybir as mybir
from concourse.bass_utils import run_bass_kernel_spmd

F32 = mybir.dt.float32
BF16 = mybir.dt.bfloat16
I32 = mybir.dt.int32
AF = mybir.ActivationFunctionType
ALU = mybir.AluOpType

D = 2048
NC_ = 16
DFF = 5632
NFC = 44
DEPTH = 4
SEQ = 2048
ALPHA = (2 * DEPTH) ** 0.25
LN_EPS = 1e-5
IN_WIDTH = 7184


class Slot:
    __slots__ = ("name", "w", "r")

    def __init__(self, name):
        self.name = name
        self.w = None
        self.r = {}


class Prog:
    def __init__(self, nc):
        self.nc = nc
        self.q = {"pe": [], "act": [], "dve": [], "pool": [], "sp": []}
        self.cnt = {}
        self.waited = {e: {} for e in self.q}
        self.nslots = 0
        self.qmap = {}

    def slot(self, name=None):
        self.nslots += 1
        return Slot(name or f"s{self.nslots}")

    def _deps(self, eng, reads, writes, extra=()):
        deps = {}

        def add(ev):
            if ev is None:
                return
            k, v = ev
            if deps.get(k, 0) < v:
                deps[k] = v

        for s in reads:
            add(s.w)
        for s in writes:
            add(s.w)
            for k, v in s.r.items():
                add((k, v))
        for ev in extra:
            add(ev)
        waits = []
        wd = self.waited[eng]
        for k, v in deps.items():
            if k == "pe" and eng == "pe":
                continue
            if wd.get(k, 0) >= v:
                continue
            wd[k] = v
            waits.append((k, v))
        return waits

    def _mark(self, ev, reads, writes):
        k, v = ev
        for s in reads:
            if s.r.get(k, 0) < v:
                s.r[k] = v
        for s in writes:
            s.w = ev
            s.r = {}

    def op(self, eng, fn, reads=(), writes=(), signal=True):
        waits = self._deps(eng, reads, writes)
        c = self.cnt.get(eng, 0)
        if signal:
            c += 1
            self.cnt[eng] = c
            ev = (eng, c)
            inc = (eng, 1)
        else:
            ev = (eng, c + 1)
            inc = None
        self._mark(ev, reads, writes)
        self.q[eng].append((waits, fn, inc))

    def dma(self, qeng, chan, fn, reads=(), writes=()):
        if qeng == "pool" and not chan.startswith(("yld", "yst", "xnst", "xT0", "xT1")):
            qeng = "sp"
        key = "d:" + chan
        c = self.cnt.get(key, 0)
        waits = self._deps(qeng, reads, writes, extra=[(key, c)] if c else [])
        c += 16
        self.cnt[key] = c
        self._mark((key, c), reads, writes)
        self.q[qeng].append((waits, fn, (key, 16)))

    def fence(self):
        for e in self.q:
            waits = []
            wd = self.waited[e]
            for k, v in self.cnt.items():
                if k == "pe" and e == "pe":
                    continue
                if wd.get(k, 0) >= v:
                    continue
                wd[k] = v
                waits.append((k, v))
            if waits:
                self.q[e].append((waits, None, None))

    def wait_all(self, eng, slots):
        waits = self._deps(eng, slots, ())
        self.q[eng].append((waits, None, None))

    def emit(self, stack):
        nc = self.nc
        sems = {}
        for k in self.cnt:
            sems[k] = stack.enter_context(nc.semaphore("sem_" + k.replace(":", "_")))
        engs = {"pe": "tensor", "act": "scalar", "dve": "vector", "pool": "gpsimd", "sp": "sync"}
        q = self.q

        def run(e, lst):
            for waits, fn, inc in lst:
                for k, v in waits:
                    e.wait_ge(sems[k], v)
                if fn is not None:
                    ins = fn(e)
                    if inc is not None:
                        ins.then_inc(sems[inc[0]], inc[1])

        with nc.Block() as block:
            for name, attr in engs.items():
                if not q[name]:
                    continue

                def mk(lst):
                    def f(e):
                        run(e, lst)
                    return f

                getattr(block, attr)(mk(q[name]))


class Buf:
    uid = 0

    def __init__(self, P, stack, name, shape, dtype, n=1, psum=False):
        self.n = n
        self.t = []
        self.s = []
        for i in range(n):
            Buf.uid += 1
            nm = f"{name}{i}_{Buf.uid}"
            if psum:
                t = stack.enter_context(P.nc.psum_tensor(nm, shape, dtype))
            else:
                t = stack.enter_context(P.nc.sbuf_tensor(nm, shape, dtype))
            self.t.append(t)
            self.s.append(P.slot(nm))
        self.i = -1

    def next(self):
        self.i = (self.i + 1) % self.n
        return self.t[self.i], self.s[self.i]

    def cur(self):
        return self.t[self.i], self.s[self.i]


def build_program(cfg):
    NT = cfg["ntok"]
    depth = cfg["depth"]
    stages = cfg.get("stages", ("ffn1", "mix", "ffn2"))
    TT = 512
    ntile = NT // TT
    n128 = NT // 128

    nc = bass.Bass("TRN2", target_bir_lowering=False)
    P = Prog(nc)
    stack = contextlib.ExitStack()

    def din(name, shape, dt=F32):
        return nc.dram_tensor(name, list(shape), dt, kind="ExternalInput").ap()

    x_in = din("x", [NT, D])
    w1i = din("ffn1_w_in", [depth, D, 2 * DFF]) if "ffn1" in stages else None
    w1o = din("ffn1_w_out", [depth, DFF, D]) if "ffn1" in stages else None
    w2i = din("ffn2_w_in", [depth, D, 2 * DFF]) if "ffn2" in stages else None
    w2o = din("ffn2_w_out", [depth, DFF, D]) if "ffn2" in stages else None
    if "mix" in stages:
        w_pr = din("w_in", [depth, D, IN_WIDTH])
        w_sw = din("w_in_sw", [depth, D, 2048])
        w_mo = din("w_out", [depth, D, D])
        convw_in = din("convw", [128, depth * 24 * 4])
        hp8_in = din("hp8", [128, depth * 2])
        lamv_in = din("lamv", [128, depth * 256])
        normg_in = din("normg", [128, depth * 256])
        pos_in = din("pos", [128, NT], I32)
        ropec_in = din("ropec", [128, 2])
        masks_in = din("masks", [128, 640])
    lnp = din("lnp", [128, depth * 6 * NC_])
    ident_in = din("ident", [128, 128])
    y_out = nc.dram_tensor("y", [NT, D], F32, kind="ExternalOutput").ap()

    def dscr(name, shape, dt):
        return nc.dram_tensor(name, list(shape), dt, kind="Internal").ap()

    XF = dscr("XF", [ntile, 128, NC_, TT], F32)
    XB = dscr("XB", [ntile, 128, NC_, TT], BF16)
    WIN = {}
    WOUT = {}
    for l in range(depth):
        for f in (1, 2):
            WIN[l, f] = dscr(f"WIN{l}_{f}", [NFC, 128, 2, NC_, 128], BF16)
            WOUT[l, f] = dscr(f"WOUT{l}_{f}", [NC_, 128, NFC, 128], BF16)
    nseq = NT // SEQ if NT >= SEQ else 1
    SQ = min(SEQ, NT)
    if "mix" in stages:
        WPR = {l: dscr(f"WPR{l}", [56, 128, NC_, 128], BF16) for l in range(depth)}
        WSW = {l: dscr(f"WSW{l}", [16, 128, NC_, 128], BF16) for l in range(depth)}
        WBA = {l: dscr(f"WBA{l}", [128, NC_, 16], BF16) for l in range(depth)}
        WMO = {l: dscr(f"WMO{l}", [NC_, 128, NC_, 128], BF16) for l in range(depth)}
        ROPE = dscr("ROPE", [nseq, 2, 128, SQ], F32)
        MIX = dscr("MIX", [ntile, 128, NC_, TT], BF16)
        mix_slots = [[P.slot(f"MIX{t}_{j}") for j in range(NC_)] for t in range(ntile)]
        rope_slots = [P.slot(f"ROPE{q}") for q in range(nseq)]
    xf_slots = [P.slot(f"XF{t}") for t in range(ntile)]
    xb_slots = [P.slot(f"XB{t}") for t in range(ntile)]
    wslot = {}

    with stack:
        ident = Buf(P, stack, "ident", [128, 128], F32)
        ones_bf = Buf(P, stack, "ones_bf", [128, 128], BF16)
        lnp_sb = Buf(P, stack, "lnp_sb", [128, depth * 6 * NC_], F32)
        it, isl = ident.next()
        P.dma("pool", "const", lambda e: e.dma_start(out=it[:], in_=ident_in[:, :]), writes=[isl])
        ot, osl = ones_bf.next()
        P.op("pool", lambda e: e.memset(ot[:], 1.0), writes=[osl])
        lt, lsl = lnp_sb.next()
        if not (cfg.get("dbg", 0) & 2):
            P.dma("pool", "const", lambda e: e.dma_start(out=lt[:], in_=lnp[:, :]), writes=[lsl])

        def lnvec(l, which, c):
            o = (l * 6 + which) * NC_ + c
            return lt[:, o:o + 1]

        banks = Buf(P, stack, "bank", [128, 512], F32, n=8, psum=True)
        bk = list(zip(banks.t, banks.s))

        ph = contextlib.ExitStack()
        NSTG = 3
        stg = Buf(P, ph, "stg", [128, NC_ * 512], F32, n=NSTG)
        stgb = Buf(P, ph, "stgb", [128, NC_ * 512], BF16, n=NSTG)
        ci = [0]

        def precast_span(src2d, row0, nk, col0, ncc, dsts, slots):
            st, ss = stg.next()
            sb, sbs = stgb.next()
            k_ = ci[0]
            ci[0] += 1
            i_ = stg.i
            qe = "sp" if k_ % 2 == 0 else "act"
            W = ncc * 128
            v = st[:, :nk * W].rearrange("p (k w) -> p k w", k=nk)
            srcap = src2d[row0:row0 + nk * 128, col0:col0 + W].rearrange("(k p) w -> p k w", p=128)
            P.dma(qe, f"pcl{i_}", lambda e: e.dma_start(out=v, in_=srcap), writes=[ss])
            ce = ("dve", "act", "dve", "act", "pool")[k_ % 5]
            ov = sb[:, :nk * W].rearrange("p (h k c) -> p h k c", h=ncc, k=nk)
            iv = st[:, :nk * W].rearrange("p (k h c) -> p h k c", k=nk, h=ncc)
            if ce == "act":
                P.op("act", lambda e: e.copy(out=ov, in_=iv), reads=[ss], writes=[sbs])
            else:
                P.op(ce, lambda e: e.tensor_copy(out=ov, in_=iv), reads=[ss], writes=[sbs])
            for h in range(ncc):
                P.dma(qe, f"pcs{i_}", lambda e, h=h: e.dma_start(out=dsts[h], in_=sb[:, h * nk * 128:(h + 1) * nk * 128]),
                      reads=[sbs], writes=[slots[h]])

        def precast_ffn(l, f, wi, wo):
            for j in range(NFC):
                wslot["in", l, f, j] = P.slot()
            for c in range(NC_):
                wslot["out", l, f, c] = P.slot()
            for J in range(NFC // 4):
                for h in range(2):
                    precast_span(wi[l], 0, NC_, h * DFF + J * 512, 4,
                                 [WIN[l, f][4 * J + cc][:, h].rearrange("p k c -> p (k c)") for cc in range(4)],
                                 [wslot["in", l, f, 4 * J + cc] for cc in range(4)])
            for C in range(NC_ // 4):
                for (k0, nk) in ((0, 16), (16, 16), (32, 12)):
                    precast_span(wo[l], k0 * 128, nk, C * 512, 4,
                                 [WOUT[l, f][4 * C + cc][:, k0:k0 + nk, :].rearrange("p k c -> p (k c)") for cc in range(4)],
                                 [wslot["out", l, f, 4 * C + cc] for cc in range(4)])

        def precast_mix(l):
            for (kind, W_, src, n) in (("pr", WPR, w_pr, 56), ("sw", WSW, w_sw, 16), ("mo", WMO, w_mo, 16)):
                for m in range(n):
                    wslot[kind, l, m] = P.slot()
                for s4 in range(n // 4):
                    precast_span(src[l], 0, NC_, s4 * 512, 4,
                                 [W_[l][4 * s4 + cc].rearrange("p k c -> p (k c)") for cc in range(4)],
                                 [wslot[kind, l, 4 * s4 + cc] for cc in range(4)])
            st, ss = stg.next()
            sb, sbs = stgb.next()
            ci[0] += 1
            i_ = stg.i
            v = st[:, :NC_ * 16].rearrange("p (k w) -> p k w", k=NC_)
            srcap = w_pr[l][:, 7168:7184].rearrange("(k p) w -> p k w", p=128)
            P.dma("sp", f"pcl{i_}", lambda e: e.dma_start(out=v, in_=srcap), writes=[ss])
            P.op("dve", lambda e: e.tensor_copy(out=sb[:, :NC_ * 16], in_=st[:, :NC_ * 16]), reads=[ss], writes=[sbs])
            sl = P.slot()
            wslot["ba", l] = sl
            P.dma("sp", f"pcs{i_}", lambda e: e.dma_start(out=WBA[l].rearrange("p k c -> p (k c)"), in_=sb[:, :NC_ * 16]),
                  reads=[sbs], writes=[sl])

        for l in range(depth):
            if "mix" in stages:
                precast_mix(l)
            if "ffn1" in stages:
                precast_ffn(l, 1, w1i, w1o)
            if "ffn2" in stages:
                precast_ffn(l, 2, w2i, w2o)

        P.fence()
        ph.close()
        ph = contextlib.ExitStack()
        xtok = Buf(P, ph, "xtok", [128, D], F32, n=2)
        xfst = Buf(P, ph, "xfst", [128, NC_, TT], F32, n=2)
        xbst = Buf(P, ph, "xbst", [128, NC_, TT], BF16, n=2)
        for t in range(ntile):
            ft, fs = xfst.next()
            bt, bs = xbst.next()
            for q4 in range(4):
                tt = t * 4 + q4
                xt, xs = xtok.next()
                P.dma("pool", f"xtok{xtok.i}", lambda e, xt=xt, tt=tt: e.dma_start(out=xt[:], in_=x_in[tt * 128:(tt + 1) * 128, :]),
                      writes=[xs])
                for g in range(4):
                    pb, ps = bk[g % 2]
                    for i in range(4):
                        c = 4 * g + i
                        P.op("pe", lambda e, pb=pb, xt=xt, c=c, i=i: e.transpose(pb[:, i * 128:(i + 1) * 128], xt[:, c * 128:(c + 1) * 128], it[:]),
                             reads=[xs, isl], writes=[ps], signal=(i == 3))
                    pv = pb[:, :].rearrange("p (a b) -> p a b", a=4)
                    P.op("act", lambda e, ft=ft, pv=pv, g=g, q4=q4: e.copy(out=ft[:, 4 * g:4 * g + 4, q4 * 128:(q4 + 1) * 128], in_=pv),
                         reads=[ps], writes=[fs])
                    P.op("dve", lambda e, bt=bt, ft=ft, g=g, q4=q4: e.tensor_copy(out=bt[:, 4 * g:4 * g + 4, q4 * 128:(q4 + 1) * 128],
                                                                                 in_=ft[:, 4 * g:4 * g + 4, q4 * 128:(q4 + 1) * 128]),
                         reads=[fs], writes=[bs])
            tok = slice(t * TT, (t + 1) * TT)
            P.dma("pool", f"xfst{xfst.i}", lambda e, ft=ft, t=t: e.dma_start(out=XF[t], in_=ft[:]),
                  reads=[fs], writes=[xf_slots[t]])
            P.dma("pool", f"xbst{xbst.i}", lambda e, bt=bt, t=t: e.dma_start(out=XB[t], in_=bt[:]),
                  reads=[bs], writes=[xb_slots[t]])

        P.fence()
        ph.close()

        def ln_bufs(ph, b_ny=1):
            b = {}
            b["yb"] = Buf(P, ph, "ybuf", [128, NC_, TT], F32, n=b_ny)
            b["xnb"] = Buf(P, ph, "xnb", [128, TT], BF16, n=3)
            b["ybf"] = Buf(P, ph, "ybf", [128, TT], BF16, n=3)
            b["ysq"] = Buf(P, ph, "ysq", [128, TT], BF16, n=3)
            b["mean"] = Buf(P, ph, "mean", [128, TT], F32, n=2)
            b["rstd"] = Buf(P, ph, "rstd", [128, TT], F32, n=2)
            b["tmp"] = Buf(P, ph, "tmpn", [128, TT], F32, n=2)
            b["y_cs_all"] = [[P.slot() for c in range(NC_)] for _ in range(b_ny)]
            return b

        def out_ln(l, which_ln, t, nk, rhs_fn, wload, s_res, b):
            eps = LN_EPS / (ALPHA * ALPHA)
            y_t, y_s = b["yb"].next()
            y_cs = b["y_cs_all"][b["yb"].i]
            mean_t, mean_s = b["mean"].next()
            rstd_t, rstd_s = b["rstd"].next()
            tmp, ybf, ysq = b["tmp"], b["ybf"], b["ysq"]
            S1, S1s = bk[6]
            S2, S2s = bk[7]
            P.dma("pool", "yld", lambda e: e.dma_start(out=y_t[:], in_=XF[t]), reads=[xf_slots[t]], writes=[y_s] + y_cs)
            pend = [None]
            for c in range(NC_):
                wt, ws = wload(c)
                pb, ps = bk[4 + (c % 2)]
                for j in range(nk):
                    ra, rs = rhs_fn(j)
                    P.op("pe", lambda e, pb=pb, wt=wt, j=j, ra=ra: e.matmul(pb[:, :], wt[:, j, :], ra, start=(j == 0), stop=(j == nk - 1)),
                         reads=[ws, rs], writes=[ps], signal=(j == nk - 1))
                P.op("dve", lambda e, pb=pb, c=c: e.scalar_tensor_tensor(
                    out=y_t[:, c, :], in0=pb[:, :], scalar=s_res, in1=y_t[:, c, :], op0=ALU.mult, op1=ALU.add),
                    reads=[ps, y_cs[c]], writes=[y_cs[c]])
                yt, ys = ybf.next()
                qt, qs = ysq.next()
                P.op("pool", lambda e, yt=yt, c=c: e.tensor_copy(out=yt[:], in_=y_t[:, c, :]), reads=[y_cs[c]], writes=[ys])
                P.op("act", lambda e, qt=qt, c=c: e.activation(out=qt[:], in_=y_t[:, c, :], func=AF.Square), reads=[y_cs[c]], writes=[qs])

                def stats(yt=yt, ys=ys, qt=qt, qs=qs, c=c):
                    P.op("pe", lambda e: e.matmul(S1[:, :], ot[:], yt[:], start=(c == 0), stop=(c == NC_ - 1)), reads=[osl, ys], writes=[S1s])
                    P.op("pe", lambda e: e.matmul(S2[:, :], ot[:], qt[:], start=(c == 0), stop=(c == NC_ - 1)), reads=[osl, qs], writes=[S2s])
                if pend[0] is not None:
                    pend[0]()
                pend[0] = stats
            pend[0]()
            P.op("dve", lambda e: e.tensor_scalar(out=mean_t[:], in0=S1[:, :], scalar1=1.0 / D, scalar2=None, op0=ALU.mult), reads=[S1s], writes=[mean_s])
            m2, m2s = tmp.next()
            P.op("dve", lambda e: e.tensor_tensor(out=m2[:], in0=mean_t[:], in1=mean_t[:], op=ALU.mult), reads=[mean_s], writes=[m2s])
            P.op("dve", lambda e: e.scalar_tensor_tensor(out=rstd_t[:], in0=S2[:, :], scalar=1.0 / D, in1=m2[:], op0=ALU.mult, op1=ALU.subtract),
                 reads=[S2s, m2s], writes=[rstd_s])
            P.op("dve", lambda e: e.tensor_scalar(out=rstd_t[:], in0=rstd_t[:], scalar1=eps, scalar2=None, op0=ALU.add), reads=[rstd_s], writes=[rstd_s])
            P.op("act", lambda e: e.activation(out=rstd_t[:], in_=rstd_t[:], func=AF.Sqrt), reads=[rstd_s], writes=[rstd_s])
            P.op("dve", lambda e: e.reciprocal(out=rstd_t[:], in_=rstd_t[:]), reads=[rstd_s], writes=[rstd_s])
            return ln_norm(l, which_ln, t, b, y_t, y_s, y_cs, mean_t, mean_s, rstd_t, rstd_s, b.get('engs', ('pool',)))

        def ln_norm(l, which_ln, t, b, y_t, y_s, y_cs, mean_t, mean_s, rstd_t, rstd_s, engs=("pool",)):
            for c in range(NC_):
                en = engs[c % len(engs)]
                g_ap = lnvec(l, 2 * which_ln, c)
                b_ap = lnvec(l, 2 * which_ln + 1, c)
                P.op(en, lambda e, c=c: e.tensor_tensor(out=y_t[:, c, :], in0=y_t[:, c, :], in1=mean_t[:], op=ALU.subtract),
                     reads=[mean_s], writes=[y_cs[c]])
                P.op(en, lambda e, c=c: e.tensor_tensor(out=y_t[:, c, :], in0=y_t[:, c, :], in1=rstd_t[:], op=ALU.mult),
                     reads=[rstd_s], writes=[y_cs[c]])
                P.op(en, lambda e, c=c, g_ap=g_ap, b_ap=b_ap: e.tensor_scalar(out=y_t[:, c, :], in0=y_t[:, c, :], scalar1=g_ap, scalar2=b_ap, op0=ALU.mult, op1=ALU.add),
                     reads=[lsl], writes=[y_cs[c]])
                xn_t, xn_s = b["xnb"].next()
                P.op(en, lambda e, c=c, xn_t=xn_t: e.tensor_copy(out=xn_t[:], in_=y_t[:, c, :]), reads=[y_cs[c]], writes=[xn_s])
                P.dma("pool", f"xnst{b['xnb'].i}", lambda e, c=c, xn_t=xn_t: e.dma_start(out=XB[t][:, c, :], in_=xn_t[:]), reads=[xn_s], writes=[xb_slots[t]])
                yield
            P.dma("pool", "yst", lambda e: e.dma_start(out=XF[t], in_=y_t[:]), reads=[y_s] + y_cs, writes=[xf_slots[t]])
            yield

        def ffn_stage(l, f, which_ln):
            ph = contextlib.ExitStack()
            xT = Buf(P, ph, "xT", [128, NC_, TT], BF16, n=2)
            aT = Buf(P, ph, "aT", [128, NFC, TT], BF16)
            wib = Buf(P, ph, "wib", [128, 2, NC_, 128], BF16, n=4)
            wob = Buf(P, ph, "wob", [128, NFC, 128], BF16, n=2)
            sg = Buf(P, ph, "sg", [128, TT], F32, n=2)
            b = ln_bufs(ph)
            aT_t, aT_s = aT.next()
            aT_cs = [P.slot(f"aT{j}") for j in range(NFC)]
            ep = iter(())
            nxt = None
            for t in range(ntile):
                if nxt is None:
                    xt, xs = xT.next()
                    P.dma("pool", f"xT{xT.i}", lambda e, xt=xt, t=t: e.dma_start(out=xt[:], in_=XB[t]), reads=[xb_slots[t]], writes=[xs])
                else:
                    xt, xs = nxt
                for j in range(NFC):
                    wt, ws = wib.next()
                    P.dma("sp", f"wib{wib.i}", lambda e, wt=wt, j=j: e.dma_start(out=wt[:], in_=WIN[l, f][j]),
                          reads=[wslot["in", l, f, j]], writes=[ws])
                    gb, gs = bk[(j % 2) * 2]
                    ub, us = bk[(j % 2) * 2 + 1]
                    for h, (pb, ps) in enumerate(((gb, gs), (ub, us))):
                        for k in range(NC_):
                            P.op("pe", lambda e, pb=pb, wt=wt, xt=xt, k=k, h=h: e.matmul(
                                pb[:, :], wt[:, h, k, :], xt[:, k, :], start=(k == 0), stop=(k == NC_ - 1)),
                                reads=[ws, xs], writes=[ps], signal=(k == NC_ - 1))
                    st_, ss_ = sg.next()
                    P.op("act", lambda e, st_=st_, gb=gb: e.activation(out=st_[:], in_=gb[:, :], func=AF.Silu), reads=[gs], writes=[ss_])
                    P.op("dve", lambda e, st_=st_, ub=ub, j=j: e.tensor_tensor(out=aT_t[:, j, :], in0=ub[:, :], in1=st_[:], op=ALU.mult),
                         reads=[us, ss_], writes=[aT_cs[j]])

                def wload(c):
                    wt, ws = wob.next()
                    P.dma("sp", f"wob{wob.i}", lambda e: e.dma_start(out=wt[:], in_=WOUT[l, f][c]), reads=[wslot["out", l, f, c]], writes=[ws])
                    return wt, ws
                if t + 1 < ntile:
                    nxt = xT.next()
                    P.dma("pool", f"xT{xT.i}", lambda e, xt2=nxt[0], t=t: e.dma_start(out=xt2[:], in_=XB[t + 1]), reads=[xb_slots[t + 1]], writes=[nxt[1]])
                for _ in out_ln(l, which_ln, t, NFC, lambda j: (aT_t[:, j, :], aT_cs[j]), wload, 0.5 / ALPHA, b):
                    pass
            P.fence()
            ph.close()

        AQ, AK, AV, DQ, DK, DV, DZ = 0, 8, 16, 24, 32, 40, 48
        NB = SQ // 128
        NQ = SQ // 512
        AX = mybir.AxisListType.X
        if "mix" in stages:
            def cload(name, shape, src, dt=F32):
                bf = Buf(P, stack, name, shape, dt)
                t_, s_ = bf.next()
                P.dma("pool", "const", lambda e: e.dma_start(out=t_[:], in_=src), writes=[s_])
                return t_, s_
            mk_t, mk_s = cload("masks", [128, 640], masks_in[:, :])
            cw_t, cw_s = cload("convw", [128, depth * 96], convw_in[:, :])
            hp_t, hp_s = cload("hp8", [128, depth * 2], hp8_in[:, :])
            lv_t, lv_s = cload("lamv", [128, depth * 256], lamv_in[:, :])
            ng_t, ng_s = cload("normg", [128, depth * 256], normg_in[:, :])
            rc_t, rc_s = cload("ropec", [128, 2], ropec_in[:, :])
            TRIf, LSTR, UINC, CIND, BLKM = mk_t[:, 0:128], mk_t[:, 128:256], mk_t[:, 256:384], mk_t[:, 384:386], mk_t[:, 512:640]
            tribf_t, tribf_s = Buf(P, stack, "tribf", [128, 128], BF16).next()
            P.op("dve", lambda e: e.tensor_copy(out=tribf_t[:], in_=TRIf), reads=[mk_s], writes=[tribf_s])
            zer_t, zer_s = Buf(P, stack, "zerf", [128, 128], F32).next()
            P.op("pool", lambda e: e.memset(zer_t[:], 0.0), writes=[zer_s])
            onf_t, onf_s = Buf(P, stack, "onesf", [128, 128], F32).next()
            P.op("pool", lambda e: e.memset(onf_t[:], 1.0), writes=[onf_s])
            lam_t, lam_s = Buf(P, stack, "lamt", [128, depth * 2], F32).next()
            nA_t, nA_s = Buf(P, stack, "negA", [128, depth], F32).next()
            sc_t, sc_s = Buf(P, stack, "lamsc", [128, 64], F32).next()
            sc2_t, sc2_s = Buf(P, stack, "lamsc2", [128, 4], F32).next()
            for l in range(depth):
                lam_init = 0.8 - 0.6 * math.exp(-0.3 * l)
                for z in range(2):
                    o_ = l * 256 + z * 128
                    P.op("dve", lambda e, o_=o_: e.tensor_tensor(out=sc_t[:], in0=lv_t[:, o_:o_ + 64], in1=lv_t[:, o_ + 64:o_ + 128], op=ALU.mult),
                         reads=[lv_s], writes=[sc_s])
                    P.op("dve", lambda e, z=z: e.reduce_sum(out=sc2_t[:, z:z + 1], in_=sc_t[:], axis=AX), reads=[sc_s], writes=[sc2_s])
                P.op("act", lambda e: e.activation(out=sc2_t[:, 0:2], in_=sc2_t[:, 0:2], func=AF.Exp), reads=[sc2_s], writes=[sc2_s])
                P.op("dve", lambda e: e.tensor_tensor(out=sc2_t[:, 2:3], in0=sc2_t[:, 0:1], in1=sc2_t[:, 1:2], op=ALU.subtract), reads=[sc2_s], writes=[sc2_s])
                P.op("dve", lambda e, l=l, lam_init=lam_init: e.tensor_scalar(out=lam_t[:, 2 * l:2 * l + 1], in0=sc2_t[:, 2:3], scalar1=lam_init, scalar2=None, op0=ALU.add),
                     reads=[sc2_s], writes=[lam_s])
                P.op("dve", lambda e, l=l: e.tensor_scalar(out=lam_t[:, 2 * l + 1:2 * l + 2], in0=lam_t[:, 2 * l:2 * l + 1], scalar1=-1.0, scalar2=None, op0=ALU.mult),
                     reads=[lam_s], writes=[lam_s])
                P.op("dve", lambda e, l=l, lam_init=lam_init: e.tensor_scalar(out=ng_t[:, l * 256:l * 256 + 128], in0=ng_t[:, l * 256:l * 256 + 128],
                                                                              scalar1=1.0 - lam_init, scalar2=None, op0=ALU.mult), reads=[ng_s], writes=[ng_s])
                P.op("act", lambda e, l=l: e.activation(out=nA_t[:, l:l + 1], in_=hp_t[:, 2 * l:2 * l + 1], func=AF.Exp), reads=[hp_s], writes=[nA_s])
                P.op("dve", lambda e, l=l: e.tensor_scalar(out=nA_t[:, l:l + 1], in0=nA_t[:, l:l + 1], scalar1=-1.0, scalar2=None, op0=ALU.mult), reads=[nA_s], writes=[nA_s])

        def rope_tables():
            P.fence()
            ph = contextlib.ExitStack()
            posi, posi_s = Buf(P, ph, "posi", [128, SQ], I32).next()
            pf, pf_s = Buf(P, ph, "posf", [128, SQ], F32).next()
            ti, ti_s = Buf(P, ph, "rti", [128, SQ], I32).next()
            tf, tf_s = Buf(P, ph, "rtf", [128, SQ], F32).next()
            u, u_s = Buf(P, ph, "ru", [128, SQ], F32).next()
            tabs = Buf(P, ph, "rtab", [128, SQ], F32, n=2)
            for q in range(nseq):
                P.dma("pool", "posld", lambda e, q=q: e.dma_start(out=posi[:], in_=pos_in[:, q * SQ:(q + 1) * SQ]), writes=[posi_s])
                P.op("dve", lambda e: e.tensor_copy(out=pf[:], in_=posi[:]), reads=[posi_s], writes=[pf_s])
                P.op("dve", lambda e: e.tensor_scalar(out=pf[:], in0=pf[:], scalar1=rc_t[:, 0:1], scalar2=None, op0=ALU.mult), reads=[pf_s, rc_s], writes=[pf_s])
                for idx, off in ((1, 0.0), (0, 0.25)):
                    P.op("dve", lambda e, off=off: e.tensor_scalar(out=u[:], in0=pf[:], scalar1=off, scalar2=None, op0=ALU.add), reads=[pf_s], writes=[u_s])
                    P.op("dve", lambda e: e.tensor_copy(out=ti[:], in_=u[:]), reads=[u_s], writes=[ti_s])
                    P.op("dve", lambda e: e.tensor_copy(out=tf[:], in_=ti[:]), reads=[ti_s], writes=[tf_s])
                    P.op("dve", lambda e: e.tensor_tensor(out=u[:], in0=u[:], in1=tf[:], op=ALU.subtract), reads=[u_s, tf_s], writes=[u_s])
                    P.op("dve", lambda e: e.tensor_single_scalar(out=tf[:], in_=u[:], scalar=0.5, op=ALU.is_gt), reads=[u_s], writes=[tf_s])
                    P.op("dve", lambda e: e.tensor_tensor(out=u[:], in0=u[:], in1=tf[:], op=ALU.subtract), reads=[u_s, tf_s], writes=[u_s])
                    P.op("dve", lambda e: e.tensor_single_scalar(out=tf[:], in_=u[:], scalar=-0.5, op=ALU.is_lt), reads=[u_s], writes=[tf_s])
                    P.op("dve", lambda e: e.tensor_tensor(out=u[:], in0=u[:], in1=tf[:], op=ALU.add), reads=[u_s, tf_s], writes=[u_s])
                    tb, tbs = tabs.next()
                    P.op("act", lambda e, tb=tb: e.activation(out=tb[:], in_=u[:], func=AF.Sin, scale=2.0 * math.pi), reads=[u_s], writes=[tbs])
                    if idx == 1:
                        P.op("dve", lambda e, tb=tb: e.tensor_scalar(out=tb[:], in0=tb[:], scalar1=rc_t[:, 1:2], scalar2=None, op0=ALU.mult), reads=[tbs, rc_s], writes=[tbs])
                    P.dma("pool", f"ropest{tabs.i}", lambda e, tb=tb, q=q, idx=idx: e.dma_start(out=ROPE[q, idx], in_=tb[:]), reads=[tbs], writes=[rope_slots[q]])
            P.fence()
            ph.close()

        def mixer_stage(l, q, parts=("diff", "gdn")):
            tiles = list(range(q * NQ, (q + 1) * NQ))
            phm = contextlib.ExitStack()
            xTs, xTs_s = Buf(P, phm, "xTs", [128, NC_, SQ], BF16).next()
            for iq, t in enumerate(tiles):
                P.dma("pool", "xTsld", lambda e, iq=iq, t=t: e.dma_start(out=xTs[:, :, iq * 512:(iq + 1) * 512], in_=XB[t]), reads=[xb_slots[t]], writes=[xTs_s])
            wpt = Buf(P, phm, "wpt", [128, NC_, 128], BF16, n=3)

            def wget(kind, m):
                wt, ws = wpt.next()
                src = {"pr": WPR, "sw": WSW}[kind][l][m]
                P.dma("sp", f"wpt{wpt.i}", lambda e: e.dma_start(out=wt[:], in_=src), reads=[wslot[kind, l, m]], writes=[ws])
                return wt, ws

            def proj_fm(wt, ws, iq, pb, ps, mlo=0, mhi=128):
                for k in range(NC_):
                    P.op("pe", lambda e, k=k: e.matmul(pb[0:mhi - mlo, :], wt[:, k, mlo:mhi], xTs[:, k, iq * 512:(iq + 1) * 512],
                                                      start=(k == 0), stop=(k == NC_ - 1)),
                         reads=[ws, xTs_s], writes=[ps], signal=(k == NC_ - 1))

            def store_mix(oT, oT_s, j):
                for iq, t in enumerate(tiles):
                    P.dma("pool", "mixst", lambda e, iq=iq, t=t: e.dma_start(out=MIX[t][:, j, :], in_=oT[:, iq * 512:(iq + 1) * 512]),
                          reads=[oT_s], writes=[mix_slots[t][j]])

            if "diff" in parts:
                ph = contextlib.ExitStack()
                Ct, Ct_s = Buf(P, ph, "Ct", [128, SQ], F32).next()
                St, St_s = Buf(P, ph, "St", [128, SQ], F32).next()
                P.dma("pool", "ropeld", lambda e: e.dma_start(out=Ct[:], in_=ROPE[q, 0]), reads=[rope_slots[q]], writes=[Ct_s])
                P.dma("pool", "ropeld", lambda e: e.dma_start(out=St[:], in_=ROPE[q, 1]), reads=[rope_slots[q]], writes=[St_s])
                qTb = Buf(P, ph, "qT", [128, SQ], BF16, n=2)
                kTb = Buf(P, ph, "kT", [128, SQ], BF16, n=2)
                vab = Buf(P, ph, "vaug", [128, NB, 132], BF16, n=2)
                for va_, va_s_ in zip(vab.t, vab.s):
                    P.op("pool", lambda e, va_=va_: e.memset(va_[:, :, 128:132], 1.0), writes=[va_s_])
                Ea, _ = Buf(P, ph, "Eall", [128, NB, 512], BF16).next()
                Es = [P.slot() for _ in range(NB)]
                r1 = Buf(P, ph, "r1", [128, 512], F32, n=2)
                r2 = Buf(P, ph, "r2", [128, 512], F32, n=2)
                o1, o1_s = Buf(P, ph, "o1", [128, 4, 128], F32).next()
                ofb = Buf(P, ph, "of", [128, 128], F32, n=2)
                osq = Buf(P, ph, "osq", [128, 128], F32, n=2)
                onb = Buf(P, ph, "onb", [128, 128], F32, n=3)
                sm = Buf(P, ph, "smd", [128, 4], F32, n=4)
                oTb = Buf(P, ph, "oTd", [128, SQ], BF16, n=2)
                nlam = lam_t[:, 2 * l + 1:2 * l + 2]
                gdr = ng_t[:, l * 256:l * 256 + 128]
                dres = {}

                def thrDP(h):
                    qT, qT_s = qTb.next()
                    kT, kT_s = kTb.next()
                    va, va_s = vab.next()
                    dres[h] = (qT, qT_s, kT, kT_s, va, va_s)
                    for (ma, mb, dst, dst_s) in ((AQ + h, h, qT, qT_s), (AK + h, 8 + h, kT, kT_s)):
                        wa, was = wget("pr", ma)
                        wb, wbs = wget("sw", mb)
                        for iq in range(NQ):
                            tok = slice(iq * 512, (iq + 1) * 512)
                            pa, pas = bk[4]
                            pb_, pbs = bk[5]
                            proj_fm(wa, was, iq, pa, pas)
                            proj_fm(wb, wbs, iq, pb_, pbs)
                            t1, t1s = r1.next()
                            t2, t2s = r2.next()
                            P.op("dve", lambda e, t1=t1, tok=tok, pb_=pb_: e.tensor_tensor(out=t1[:], in0=pb_[:, :], in1=St[:, tok], op=ALU.mult), reads=[St_s], writes=[t1s, pbs])
                            P.op("dve", lambda e, t2=t2, tok=tok, pa=pa: e.tensor_tensor(out=t2[:], in0=pa[:, :], in1=Ct[:, tok], op=ALU.mult), reads=[Ct_s], writes=[t2s, pas])
                            P.op("pool", lambda e, t1=t1, t2=t2, dst=dst, tok=tok: e.tensor_tensor(out=dst[:, tok], in0=t1[:], in1=t2[:], op=ALU.add),
                                 reads=[t1s, t2s], writes=[dst_s])
                            yield
                    wv, wvs = wget("pr", AV + h)
                    for g4 in range(NQ):
                        pb, ps = bk[7]
                        for i4 in range(4):
                            tt = g4 * 4 + i4
                            for k in range(NC_):
                                P.op("pe", lambda e, pb=pb, i4=i4, tt=tt, k=k, wv=wv: e.matmul(pb[:, i4 * 128:(i4 + 1) * 128], xTs[:, k, tt * 128:(tt + 1) * 128], wv[:, k, :],
                                                                                      start=(k == 0), stop=(k == NC_ - 1)),
                                     reads=[wvs, xTs_s], writes=[ps], signal=(k == NC_ - 1))
                            yield
                        P.op("act", lambda e, pb=pb, g4=g4, va=va: e.copy(out=va[:, g4 * 4:(g4 + 1) * 4, 0:128], in_=pb[:, :].rearrange("p (a b) -> p a b", a=4)),
                             reads=[], writes=[va_s, ps])
                        yield

                def thrDA(h):
                    qT, qT_s, kT, kT_s, va, va_s = dres[h]
                    oT, oT_s = oTb.next()
                    for J in range(NQ):
                        for c in range(2):
                            cs = slice(c * 64, (c + 1) * 64)
                            for i in range(4 * J + 4):
                                sb_, ss_ = bk[i % 2]
                                P.op("pe", lambda e, sb_=sb_, i=i, cs=cs, J=J: e.matmul(sb_[:, :], kT[cs, i * 128:(i + 1) * 128], qT[cs, J * 512:(J + 1) * 512], start=True, stop=True),
                                     reads=[kT_s, qT_s], writes=[ss_])
                                P.op("act", lambda e, sb_=sb_, i=i: e.activation(out=Ea[:, i, :], in_=sb_[:, :], func=AF.Exp, scale=0.125), reads=[], writes=[Es[i], ss_])
                                r = i - 4 * J
                                if r >= 0:
                                    P.op("pool", lambda e, i=i, r=r: e.tensor_tensor(out=Ea[:, i, r * 128:(r + 1) * 128], in0=Ea[:, i, r * 128:(r + 1) * 128], in1=tribf_t[:], op=ALU.mult),
                                         reads=[tribf_s], writes=[Es[i]])
                                if i % 2 == 1:
                                    yield
                            pend = None
                            for u in range(4):
                                ob, obs = bk[2 + u % 2]
                                last = 4 * J + u
                                for i in range(last + 1):
                                    P.op("pe", lambda e, ob=ob, i=i, u=u, last=last: e.matmul(ob[:, 0:129], Ea[:, i, u * 128:(u + 1) * 128], va[:, i, 0:129], start=(i == 0), stop=(i == last)),
                                         reads=[Es[i], va_s], writes=[obs], signal=(i == last))
                                if pend is not None:
                                    pend()
                                    pend = None
                                s4, s4s = sm.next()
                                P.op("dve", lambda e, ob=ob, s4=s4: e.reciprocal(out=s4[:, 0:1], in_=ob[:, 128:129]), reads=[], writes=[s4s, obs])
                                if c == 0:
                                    P.op("act", lambda e, ob=ob, s4=s4, u=u: e.activation(out=o1[:, u, :], in_=ob[:, 0:128], func=AF.Identity, scale=s4[:, 0:1]),
                                         reads=[s4s], writes=[o1_s, obs])
                                else:
                                    P.op("dve", lambda e, s4=s4: e.tensor_scalar(out=s4[:, 1:2], in0=s4[:, 0:1], scalar1=nlam, scalar2=None, op0=ALU.mult), reads=[lam_s], writes=[s4s])
                                    of, ofs = ofb.next()
                                    P.op("dve", lambda e, ob=ob, s4=s4, u=u, of=of: e.scalar_tensor_tensor(out=of[:], in0=ob[:, 0:128], scalar=s4[:, 1:2], in1=o1[:, u, :],
                                                                                                              op0=ALU.mult, op1=ALU.add), reads=[s4s, o1_s], writes=[ofs, obs])
                                    sq, sqs = osq.next()
                                    P.op("dve", lambda e, sq=sq, of=of: e.tensor_tensor(out=sq[:], in0=of[:], in1=of[:], op=ALU.mult), reads=[ofs], writes=[sqs])
                                    P.op("dve", lambda e, sq=sq, s4=s4: e.reduce_sum(out=s4[:, 2:3], in_=sq[:], axis=AX), reads=[sqs], writes=[s4s])
                                    P.op("dve", lambda e, s4=s4: e.tensor_scalar(out=s4[:, 2:3], in0=s4[:, 2:3], scalar1=1.0 / 128, scalar2=1e-5, op0=ALU.mult, op1=ALU.add), reads=[s4s], writes=[s4s])
                                    P.op("act", lambda e, s4=s4: e.activation(out=s4[:, 2:3], in_=s4[:, 2:3], func=AF.Sqrt), reads=[s4s], writes=[s4s])
                                    P.op("dve", lambda e, s4=s4: e.reciprocal(out=s4[:, 3:4], in_=s4[:, 2:3]), reads=[s4s], writes=[s4s])
                                    on, ons = onb.next()
                                    P.op("dve", lambda e, on=on, of=of, s4=s4: e.scalar_tensor_tensor(out=on[:], in0=of[:], scalar=s4[:, 3:4], in1=gdr, op0=ALU.mult, op1=ALU.mult),
                                         reads=[ofs, s4s, ng_s], writes=[ons])

                                    def pend(on=on, ons=ons, u=u):
                                        tb, tbs = bk[6]
                                        P.op("pe", lambda e: e.transpose(tb[:, u * 128:(u + 1) * 128], on[:], it[:]), reads=[ons, isl], writes=[tbs])
                                yield
                            if c == 1:
                                pend()
                                tb, tbs = bk[6]
                                P.op("act", lambda e, tb=tb, J=J, oT=oT: e.copy(out=oT[:, J * 512:(J + 1) * 512], in_=tb[:, :]), reads=[], writes=[oT_s, tbs])
                    store_mix(oT, oT_s, h)

                def drain_(g):
                    for _ in g:
                        pass
                drain_(thrDP(0))
                for h in range(8):
                    gp = thrDP(h + 1) if h + 1 < 8 else iter(())
                    cnt_ = 0
                    for _ in thrDA(h):
                        cnt_ += 1
                        if cnt_ % 3 == 0:
                            next(gp, None)
                    drain_(gp)
                P.fence()
                ph.close()

            if "gdn" in parts:
                ph = contextlib.ExitStack()

                def tmb(name):
                    return Buf(P, ph, name, [128, NB, 8], F32).next()
                btm, btm_s = tmb("btm")
                nbtm, nbtm_s = tmb("nbtm")
                gtm, gtm_s = tmb("gtm")
                gctm, gctm_s = tmb("gctm")
                gltm, gltm_s = tmb("gltm")
                egtm, egtm_s = tmb("egtm")
                kdtm, kdtm_s = tmb("kdtm")
                bkeg, bkeg_s = tmb("bkeg")
                ph2 = contextlib.ExitStack()
                wba, wba_s = Buf(P, ph2, "wba", [128, NC_, 16], BF16).next()
                P.dma("sp", "wba", lambda e: e.dma_start(out=wba[:], in_=WBA[l]), reads=[wslot["ba", l]], writes=[wba_s])
                bfm, bfm_s = Buf(P, ph2, "bfm", [8, SQ], F32).next()
                gfm, gfm_s = Buf(P, ph2, "gfm", [8, SQ], F32).next()
                for iq in range(NQ):
                    tok = slice(iq * 512, (iq + 1) * 512)
                    pa, pas = bk[0]
                    pb_, pbs = bk[1]
                    proj_fm(wba, wba_s, iq, pa, pas, 0, 8)
                    proj_fm(wba, wba_s, iq, pb_, pbs, 8, 16)
                    P.op("act", lambda e, tok=tok: e.activation(out=bfm[:, tok], in_=pa[0:8, :], func=AF.Sigmoid), reads=[], writes=[bfm_s, pas])
                    P.op("act", lambda e, tok=tok: e.activation(out=gfm[:, tok], in_=pb_[0:8, :], func=AF.Exp, bias=hp_t[0:8, 2 * l + 1:2 * l + 2], scale=1.0),
                         reads=[hp_s], writes=[gfm_s, pbs])
                P.op("dve", lambda e: e.tensor_scalar(out=gfm[:, :], in0=gfm[:, :], scalar1=1.0, scalar2=None, op0=ALU.add), reads=[gfm_s], writes=[gfm_s])
                P.op("act", lambda e: e.activation(out=gfm[:, :], in_=gfm[:, :], func=AF.Ln), reads=[gfm_s], writes=[gfm_s])
                P.op("dve", lambda e: e.tensor_scalar(out=gfm[:, :], in0=gfm[:, :], scalar1=nA_t[0:8, l:l + 1], scalar2=None, op0=ALU.mult), reads=[gfm_s, nA_s], writes=[gfm_s])
                for (src, src_s, dst, dst_s, bki) in ((bfm, bfm_s, btm, btm_s, 2), (gfm, gfm_s, gtm, gtm_s, 3)):
                    pb, ps = bk[bki]
                    for b in range(NB):
                        P.op("pe", lambda e, pb=pb, src=src, b=b: e.transpose(pb[:, b * 8:(b + 1) * 8], src[0:8, b * 128:(b + 1) * 128], it[0:8, 0:8]),
                             reads=[src_s, isl], writes=[ps], signal=(b == NB - 1))
                    P.op("dve", lambda e, pb=pb, dst=dst: e.tensor_copy(out=dst[:, :, :].rearrange("p a b -> p (a b)"), in_=pb[:, 0:NB * 8]), reads=[], writes=[dst_s, ps])
                for (msk, dst, dst_s, bki) in ((UINC, gctm, gctm_s, 4), (BLKM, gltm, gltm_s, 5)):
                    pb, ps = bk[bki]
                    for b in range(NB):
                        P.op("pe", lambda e, pb=pb, msk=msk, b=b: e.matmul(pb[:, b * 8:(b + 1) * 8], msk, gtm[:, b, :], start=True, stop=True),
                             reads=[mk_s, gtm_s], writes=[ps], signal=(b == NB - 1))
                    P.op("dve", lambda e, pb=pb, dst=dst: e.tensor_copy(out=dst[:, :, :].rearrange("p a b -> p (a b)"), in_=pb[:, 0:NB * 8]), reads=[], writes=[dst_s, ps])
                fl = lambda t_: t_[:, :, :].rearrange("p a b -> p (a b)")
                P.op("act", lambda e: e.activation(out=fl(egtm), in_=fl(gctm), func=AF.Exp), reads=[gctm_s], writes=[egtm_s])
                P.op("dve", lambda e: e.tensor_tensor(out=fl(kdtm), in0=fl(gltm), in1=fl(gctm), op=ALU.subtract), reads=[gltm_s, gctm_s], writes=[kdtm_s])
                P.op("act", lambda e: e.activation(out=fl(kdtm), in_=fl(kdtm), func=AF.Exp), reads=[kdtm_s], writes=[kdtm_s])
                P.op("dve", lambda e: e.tensor_scalar(out=fl(nbtm), in0=fl(btm), scalar1=-1.0, scalar2=None, op0=ALU.mult), reads=[btm_s], writes=[nbtm_s])
                P.op("dve", lambda e: e.tensor_tensor(out=fl(bkeg), in0=fl(btm), in1=fl(egtm), op=ALU.mult), reads=[btm_s, egtm_s], writes=[bkeg_s])
                P.fence()
                ph2.close()
                upad, upad_s = Buf(P, ph, "upad", [128, SQ + 4], F32).next()
                cv, cv_s = Buf(P, ph, "cv", [128, SQ], F32).next()
                gqTb = Buf(P, ph, "gqT", [128, SQ], BF16, n=2)
                gkTb = Buf(P, ph, "gkT", [128, SQ], BF16, n=2)
                ktmb = Buf(P, ph, "ktm", [128, NB, 128], BF16, n=2)
                vtmb = Buf(P, ph, "vtm", [128, NB, 128], BF16, n=2)
                qdTb = Buf(P, ph, "qdT", [128, SQ], BF16, n=2)
                qdT_qs = [[P.slot() for _ in range(NB // 4)] for _ in range(2)]
                zs, zs_s = Buf(P, ph, "zsb", [128, SQ], BF16).next()
                otm, otm_s = Buf(P, ph, "otm", [128, NB, 128], F32).next()
                goTb = Buf(P, ph, "oTg", [128, SQ], BF16, n=1)
                sqb = Buf(P, ph, "sqb", [128, 512], BF16, n=2)
                gbt = Buf(P, ph, "gbt", [128, 128], F32, n=3)
                EGr, EGr_s = Buf(P, ph, "EGr", [128, 512], F32).next()
                cdqb = Buf(P, ph, "cdq", [128, 8], F32, n=2)
                Dt, Dt_s = Buf(P, ph, "Dt", [128, 4, 128], F32).next()
                DTt, DTt_s = Buf(P, ph, "DTt", [128, 4, 128], F32).next()
                Mf, Mf_s = Buf(P, ph, "Mf", [128, 4, 128], F32).next()
                Yf, Yf_s = Buf(P, ph, "Yf", [128, 4, 128], F32).next()
                Pbs = Buf(P, ph, "Pb", [128, 4, 128], BF16, n=2)
                Qbs = Buf(P, ph, "Qb", [128, 4, 128], BF16, n=2)
                Ybb = Buf(P, ph, "Yb", [128, 4, 128], BF16, n=2)
                aTtb = Buf(P, ph, "attnT", [128, 4, 128], BF16, n=2)
                rvb = Buf(P, ph, "rhsv", [128, 4, 128], BF16, n=2)
                rk, rk_s = Buf(P, ph, "rhsk", [128, 4, 128], BF16).next()
                kdb = Buf(P, ph, "kdec", [128, 4, 128], BF16, n=2)
                nwb = Buf(P, ph, "nwT", [128, 4, 128], BF16, n=2)
                vn, vn_s = Buf(P, ph, "vnew", [128, 128], BF16).next()
                Sf, Sf_s = Buf(P, ph, "Sf", [128, 128], F32).next()
                Sb, Sb_s = Buf(P, ph, "Sb", [128, 128], BF16).next()
                gonb = Buf(P, ph, "gonb", [128, 128], F32, n=3)
                gsq_all, gsq_s = Buf(P, ph, "gsqall", [128, NB, 128], F32).next()
                grs, grs_s = Buf(P, ph, "grs", [128, 2 * NB], F32).next()
                gdl = ng_t[:, l * 256 + 128:l * 256 + 256]
                flq = lambda t_: t_[:, :, :].rearrange("p a b -> p (a b)")
                NQD = NB // 4
                hd = {}
                qres = {}

                def conv_silu(m, ch):
                    wt, ws = wget("pr", m)
                    P.op("pool", lambda e: e.memset(upad[:, 0:3], 0.0), writes=[upad_s])
                    for iq in range(NQ):
                        pb, ps = bk[0]
                        proj_fm(wt, ws, iq, pb, ps)
                        P.op("act", lambda e, pb=pb, iq=iq: e.copy(out=upad[:, 3 + iq * 512:3 + (iq + 1) * 512], in_=pb[:, :]), reads=[], writes=[upad_s, ps])
                        yield
                    base = (l * 24 + ch) * 4
                    P.op("dve", lambda e: e.tensor_scalar(out=cv[:], in0=upad[:, 0:SQ], scalar1=cw_t[:, base:base + 1], scalar2=None, op0=ALU.mult),
                         reads=[upad_s, cw_s], writes=[cv_s])
                    for j in range(1, 4):
                        P.op("dve", lambda e, j=j: e.scalar_tensor_tensor(out=cv[:], in0=upad[:, j:j + SQ], scalar=cw_t[:, base + j:base + j + 1], in1=cv[:],
                                                                          op0=ALU.mult, op1=ALU.add), reads=[upad_s, cw_s, cv_s], writes=[cv_s])
                    P.op("act", lambda e: e.activation(out=cv[:], in_=cv[:], func=AF.Silu), reads=[cv_s], writes=[cv_s])
                    yield

                def l2n():
                    for iq in range(NQ):
                        tok = slice(iq * 512, (iq + 1) * 512)
                        sq, sqs = sqb.next()
                        P.op("act", lambda e, sq=sq, tok=tok: e.activation(out=sq[:], in_=cv[:, tok], func=AF.Square), reads=[cv_s], writes=[sqs])
                        pb, ps = bk[0]
                        P.op("pe", lambda e, pb=pb, sq=sq: e.matmul(pb[:, :], ot[:], sq[:], start=True, stop=True), reads=[osl, sqs], writes=[ps])
                        P.op("dve", lambda e, pb=pb, tok=tok: e.tensor_scalar(out=upad[:, tok], in0=pb[:, :], scalar1=1e-6, scalar2=None, op0=ALU.add), reads=[], writes=[upad_s, ps])
                    P.op("act", lambda e: e.activation(out=upad[:, 0:SQ], in_=upad[:, 0:SQ], func=AF.Sqrt), reads=[upad_s], writes=[upad_s])
                    P.op("dve", lambda e: e.reciprocal(out=upad[:, 0:SQ], in_=upad[:, 0:SQ]), reads=[upad_s], writes=[upad_s])
                    yield

                def to_tm(dst, dst_s):
                    for b4 in range(NB // 4):
                        pb, ps = bk[1]
                        for i4 in range(4):
                            b = b4 * 4 + i4
                            P.op("pe", lambda e, pb=pb, i4=i4, b=b: e.transpose(pb[:, i4 * 128:(i4 + 1) * 128], cv[:, b * 128:(b + 1) * 128], it[:]),
                                 reads=[cv_s, isl], writes=[ps], signal=(i4 == 3))
                        P.op("act", lambda e, pb=pb, b4=b4: e.copy(out=dst[:, b4 * 4:(b4 + 1) * 4, :], in_=pb[:, :].rearrange("p (a b) -> p a b", a=4)),
                             reads=[], writes=[dst_s, ps])
                        yield

                def thrA(h):
                    r = {}
                    r["gqT"], r["gqT_s"] = gqTb.next()
                    r["gkT"], r["gkT_s"] = gkTb.next()
                    r["ktm"], r["ktm_s"] = ktmb.next()
                    r["vtm"], r["vtm_s"] = vtmb.next()
                    r["qdT"], _ = qdTb.next()
                    r["qdT_qs"] = qdT_qs[qdTb.i]
                    hd[h] = r
                    gqT, gqT_s, gkT, gkT_s = r["gqT"], r["gqT_s"], r["gkT"], r["gkT_s"]
                    yield from conv_silu(DQ + h, h)
                    yield from l2n()
                    P.op("dve", lambda e: e.scalar_tensor_tensor(out=gqT[:], in0=cv[:], scalar=128 ** -0.5, in1=upad[:, 0:SQ], op0=ALU.mult, op1=ALU.mult),
                         reads=[cv_s, upad_s], writes=[gqT_s])
                    yield
                    yield from conv_silu(DK + h, 8 + h)
                    yield from l2n()
                    P.op("dve", lambda e: e.tensor_tensor(out=cv[:], in0=cv[:], in1=upad[:, 0:SQ], op=ALU.mult), reads=[cv_s, upad_s], writes=[cv_s])
                    P.op("pool", lambda e: e.tensor_copy(out=gkT[:], in_=cv[:]), reads=[cv_s], writes=[gkT_s])
                    yield from to_tm(r["ktm"], r["ktm_s"])
                    yield from conv_silu(DV + h, 16 + h)
                    yield from to_tm(r["vtm"], r["vtm_s"])

                def thrB(h, qd):
                    r = hd[h]
                    gqT, gqT_s, gkT, gkT_s, ktm, ktm_s, vtm, vtm_s = r["gqT"], r["gqT_s"], r["gkT"], r["gkT_s"], r["ktm"], r["ktm_s"], r["vtm"], r["vtm_s"]
                    qdT, qdT_s = r["qdT"], r["qdT_qs"][qd]
                    hs = slice(h, h + 1)
                    Yb, Yb_s = Ybb.next()
                    aTt, aTt_s = aTtb.next()
                    rv, rv_s = rvb.next()
                    kd, kd_s = kdb.next()
                    nw, nw_s = nwb.next()
                    cdq, cdq_s = cdqb.next()
                    qres[h, qd] = dict(Yb=Yb, Yb_s=Yb_s, aTt=aTt, aTt_s=aTt_s, rv=rv, rv_s=rv_s, kd=kd, kd_s=kd_s, nw=nw, nw_s=nw_s, cdq=cdq, cdq_s=cdq_s)
                    qtok = slice(qd * 512, (qd + 1) * 512)
                    g0, g0s = bk[1]
                    g1, g1s = bk[2]
                    for j4 in range(4):
                        b = qd * 4 + j4
                        gb, gbs = gbt.next()
                        P.op("pool", lambda e, gb=gb, b=b: e.tensor_scalar(out=gb[:], in0=onf_t[:], scalar1=gtm[:, b, hs], scalar2=None, op0=ALU.mult),
                             reads=[onf_s, gtm_s], writes=[gbs])
                        P.op("pe", lambda e, gb=gb, j4=j4: e.matmul(g0[:, j4 * 128:(j4 + 1) * 128], gb[:], UINC, start=True, stop=True), reads=[gbs, mk_s], writes=[g0s])
                        P.op("pe", lambda e, gb=gb, j4=j4: e.matmul(g1[:, j4 * 2:(j4 + 1) * 2], gb[:], CIND, start=True, stop=True), reads=[gbs, mk_s], writes=[g1s])
                    yield
                    P.op("act", lambda e: e.activation(out=EGr[:], in_=g0[:, :], func=AF.Exp), reads=[], writes=[EGr_s, g0s])
                    P.op("act", lambda e: e.activation(out=cdq[:], in_=g1[:, 0:8], func=AF.Exp), reads=[], writes=[cdq_s, g1s])
                    P.op("dve", lambda e: e.tensor_tensor(out=qdT[:, qtok], in0=gqT[:, qtok], in1=EGr[:], op=ALU.mult), reads=[gqT_s, EGr_s], writes=[qdT_s])
                    for j4 in range(4):
                        b = qd * 4 + j4
                        gsl = slice(j4 * 128, (j4 + 1) * 128)
                        P.op("dve", lambda e, j4=j4, b=b, gsl=gsl: e.scalar_tensor_tensor(out=Dt[:, j4, :], in0=g0[:, gsl], scalar=gctm[:, b, hs], in1=zer_t[:],
                                                                                     op0=ALU.subtract, op1=ALU.max), reads=[gctm_s, zer_s], writes=[Dt_s, g0s])
                        P.op("dve", lambda e, j4=j4, b=b, gsl=gsl: e.scalar_tensor_tensor(out=DTt[:, j4, :], in0=g0[:, gsl], scalar=gctm[:, b, hs], in1=zer_t[:],
                                                                                     op0=ALU.subtract, op1=ALU.min), reads=[gctm_s, zer_s], writes=[DTt_s, g0s])
                    yield
                    P.op("act", lambda e: e.activation(out=flq(Dt), in_=flq(Dt), func=AF.Exp, scale=-1.0), reads=[Dt_s], writes=[Dt_s])
                    P.op("act", lambda e: e.activation(out=flq(DTt), in_=flq(DTt), func=AF.Exp), reads=[DTt_s], writes=[DTt_s])
                    for j4 in range(4):
                        P.op("pool", lambda e, j4=j4: e.tensor_tensor(out=Dt[:, j4, :], in0=Dt[:, j4, :], in1=LSTR, op=ALU.mult), reads=[mk_s], writes=[Dt_s])
                        P.op("pool", lambda e, j4=j4: e.tensor_tensor(out=DTt[:, j4, :], in0=DTt[:, j4, :], in1=UINC, op=ALU.mult), reads=[mk_s], writes=[DTt_s])
                    yield
                    a2, a2s = bk[3]
                    a3, a3s = bk[1]
                    Pb, Pb_s = Pbs.next()
                    Qb, Qb_s = Qbs.next()
                    for j4 in range(4):
                        blk = slice((qd * 4 + j4) * 128, (qd * 4 + j4 + 1) * 128)
                        P.op("pe", lambda e, j4=j4, blk=blk: e.matmul(a2[:, j4 * 128:(j4 + 1) * 128], gkT[:, blk], gkT[:, blk], start=True, stop=True),
                             reads=[gkT_s], writes=[a2s], signal=(j4 == 3))
                    for j4 in range(4):
                        b = qd * 4 + j4
                        P.op("dve", lambda e, j4=j4, b=b: e.scalar_tensor_tensor(out=Mf[:, j4, :], in0=a2[:, j4 * 128:(j4 + 1) * 128], scalar=nbtm[:, b, hs], in1=Dt[:, j4, :],
                                                                                 op0=ALU.mult, op1=ALU.mult), reads=[nbtm_s, Dt_s], writes=[Mf_s, a2s])
                    P.op("pool", lambda e, Pb=Pb: e.tensor_copy(out=flq(Pb), in_=flq(Mf)), reads=[Mf_s], writes=[Pb_s])
                    yield
                    for j4 in range(4):
                        P.op("pe", lambda e, j4=j4: e.transpose(a3[:, j4 * 128:(j4 + 1) * 128], Mf[:, j4, :], it[:]), reads=[Mf_s, isl], writes=[a3s], signal=(j4 == 3))
                    P.op("act", lambda e, Qb=Qb: e.copy(out=flq(Qb), in_=a3[:, :]), reads=[], writes=[Qb_s, a3s])
                    for j4 in range(4):
                        P.op("dve", lambda e, j4=j4: e.tensor_tensor(out=Yf[:, j4, :], in0=a3[:, j4 * 128:(j4 + 1) * 128], in1=it[:], op=ALU.add), reads=[isl], writes=[Yf_s, a3s])
                    P.op("pool", lambda e: e.tensor_copy(out=flq(Yb), in_=flq(Yf)), reads=[Yf_s], writes=[Yb_s])
                    yield
                    for n in range(5):
                        p4, p4s = bk[2]
                        p5, p5s = bk[3]
                        p6, p6s = bk[1]
                        Pn, Pn_s = Pbs.next()
                        for j4 in range(4):
                            P.op("pe", lambda e, j4=j4, Qb=Qb, Pb=Pb: e.matmul(p4[:, j4 * 128:(j4 + 1) * 128], Qb[:, j4, :], Pb[:, j4, :], start=True, stop=True),
                                 reads=[Qb_s, Pb_s], writes=[p4s], signal=(j4 == 3))
                        if n < 4:
                            Qn, Qn_s = Qbs.next()
                            for j4 in range(4):
                                P.op("pe", lambda e, j4=j4, Qb=Qb, Pb=Pb: e.matmul(p5[:, j4 * 128:(j4 + 1) * 128], Pb[:, j4, :], Qb[:, j4, :], start=True, stop=True),
                                     reads=[Qb_s, Pb_s], writes=[p5s], signal=(j4 == 3))
                        P.op("dve", lambda e, Pn=Pn: e.tensor_copy(out=flq(Pn), in_=p4[:, :]), reads=[], writes=[Pn_s, p4s])
                        if n < 4:
                            P.op("act", lambda e, Qn=Qn: e.copy(out=flq(Qn), in_=p5[:, :]), reads=[], writes=[Qn_s, p5s])
                        yield
                        for j4 in range(4):
                            P.op("pe", lambda e, j4=j4, Pn=Pn: e.matmul(p6[:, j4 * 128:(j4 + 1) * 128], Pn[:, j4, :], Yb[:, j4, :], start=True, stop=True),
                                 reads=[Pn_s, Yb_s], writes=[p6s], signal=(j4 == 3))
                        P.op("dve", lambda e: e.tensor_tensor(out=flq(Yf), in0=p6[:, :], in1=flq(Yf), op=ALU.add), reads=[Yf_s], writes=[Yf_s, p6s])
                        P.op("act", lambda e: e.copy(out=flq(Yb), in_=flq(Yf)), reads=[Yf_s], writes=[Yb_s])
                        Pb, Pb_s = Pn, Pn_s
                        if n < 4:
                            Qb, Qb_s = Qn, Qn_s
                        yield
                    aq2, aq2s = bk[2]
                    aw3, aw3s = bk[3]
                    for j4 in range(4):
                        blk = slice((qd * 4 + j4) * 128, (qd * 4 + j4 + 1) * 128)
                        P.op("pe", lambda e, j4=j4, blk=blk: e.matmul(aq2[:, j4 * 128:(j4 + 1) * 128], gkT[:, blk], gqT[:, blk], start=True, stop=True),
                             reads=[gkT_s, gqT_s], writes=[aq2s], signal=(j4 == 3))
                    P.op("dve", lambda e: e.tensor_tensor(out=flq(aTt), in0=aq2[:, :], in1=flq(DTt), op=ALU.mult), reads=[DTt_s], writes=[aTt_s, aq2s])
                    for j4 in range(4):
                        b = qd * 4 + j4
                        P.op("act", lambda e, j4=j4, b=b: e.activation(out=rv[:, j4, :], in_=vtm[:, b, :], func=AF.Copy, scale=btm[:, b, hs]),
                             reads=[vtm_s, btm_s], writes=[rv_s])
                        P.op("act", lambda e, j4=j4, b=b: e.activation(out=rk[:, j4, :], in_=ktm[:, b, :], func=AF.Copy, scale=bkeg[:, b, hs]),
                             reads=[ktm_s, bkeg_s], writes=[rk_s])
                        P.op("act", lambda e, j4=j4, b=b: e.activation(out=kd[:, j4, :], in_=ktm[:, b, :], func=AF.Copy, scale=kdtm[:, b, hs]),
                             reads=[ktm_s, kdtm_s], writes=[kd_s])
                    yield
                    for j4 in range(4):
                        P.op("pe", lambda e, j4=j4: e.matmul(aw3[:, j4 * 128:(j4 + 1) * 128], rk[:, j4, :], Yb[:, j4, :], start=True, stop=True),
                             reads=[rk_s, Yb_s], writes=[aw3s], signal=(j4 == 3))
                    P.op("act", lambda e: e.mul(out=flq(nw), in_=aw3[:, :], mul=-1.0), reads=[], writes=[nw_s, aw3s])
                    yield

                def thrC(h, qd):
                    r = hd[h]
                    qdT, qdT_s = r["qdT"], r["qdT_qs"][qd]
                    z = qres[h, qd]
                    Yb, Yb_s, aTt, aTt_s, rv, rv_s, kd, kd_s, nw, nw_s, cdq, cdq_s = (z["Yb"], z["Yb_s"], z["aTt"], z["aTt_s"], z["rv"], z["rv_s"],
                                                                                      z["kd"], z["kd_s"], z["nw"], z["nw_s"], z["cdq"], z["cdq_s"])
                    if qd == 0:
                        P.op("pool", lambda e: e.memset(Sf[:], 0.0), writes=[Sf_s])
                        P.op("pool", lambda e: e.memset(Sb[:], 0.0), writes=[Sb_s])
                    for j4 in range(4):
                        b = qd * 4 + j4
                        blk = slice(b * 128, (b + 1) * 128)
                        for x in range(2):
                            R = slice(64 * x, 64 * x + 64)
                            pv, pvs = bk[7]
                            po, pos_ = bk[4 + x]
                            pst, psts = bk[6]
                            P.op("pe", lambda e, j4=j4: e.matmul(pv[:, 0:128], Yb[:, j4, :], rv[:, j4, :], start=True, stop=False), reads=[Yb_s, rv_s], writes=[pvs], signal=False)
                            P.op("pe", lambda e, j4=j4: e.matmul(pv[:, 0:128], nw[:, j4, :], Sb[:], start=False, stop=True), reads=[nw_s, Sb_s], writes=[pvs])
                            P.op("act", lambda e, R=R: e.copy(out=vn[R, :], in_=pv[R, 0:128]), reads=[], writes=[vn_s, pvs])
                            P.op("pe", lambda e, blk=blk, po=po: e.matmul(po[:, 0:128], qdT[:, blk], Sb[:], start=True, stop=False), reads=[qdT_s, Sb_s], writes=[pos_], signal=False)
                            P.op("pe", lambda e, j4=j4, R=R, po=po: e.matmul(po[:, 0:128], aTt[R, j4, :], vn[R, :], start=False, stop=True), reads=[aTt_s, vn_s], writes=[pos_])
                            P.op("act", lambda e, R=R, b=b, po=po: e.copy(out=otm[R, b, :], in_=po[R, 0:128]), reads=[], writes=[otm_s, pos_])
                            P.op("pe", lambda e, j4=j4, R=R: e.matmul(pst[:, 0:128], kd[R, j4, :], vn[R, :], start=True, stop=True), reads=[kd_s, vn_s], writes=[psts])
                            ci_ = 2 * j4 + x
                            P.op("dve", lambda e, ci_=ci_: e.scalar_tensor_tensor(out=Sb[:], in0=Sf[:], scalar=cdq[:, ci_:ci_ + 1], in1=pst[:, 0:128], op0=ALU.mult, op1=ALU.add),
                                 reads=[cdq_s, Sf_s], writes=[Sb_s, psts])
                            P.op("dve", lambda e, ci_=ci_: e.scalar_tensor_tensor(out=Sf[:], in0=Sf[:], scalar=cdq[:, ci_:ci_ + 1], in1=pst[:, 0:128], op0=ALU.mult, op1=ALU.add),
                                 reads=[cdq_s], writes=[Sf_s, psts])
                            yield

                def thrG(h):
                    wz, wzs = wget("pr", DZ + h)
                    for iq in range(NQ):
                        pb, ps = bk[4 + iq % 2]
                        proj_fm(wz, wzs, iq, pb, ps)
                        P.op("act", lambda e, pb=pb, iq=iq: e.activation(out=zs[:, iq * 512:(iq + 1) * 512], in_=pb[:, :], func=AF.Silu), reads=[], writes=[zs_s, ps])
                        yield
                    goT, goT_s = goTb.next()
                    P.op("dve", lambda e: e.tensor_tensor(out=flq(gsq_all), in0=flq(otm), in1=flq(otm), op=ALU.mult), reads=[otm_s], writes=[gsq_s])
                    P.op("dve", lambda e: e.reduce_sum(out=grs[:, 0:NB], in_=gsq_all[:, :, :], axis=AX), reads=[gsq_s], writes=[grs_s])
                    P.op("dve", lambda e: e.tensor_scalar(out=grs[:, 0:NB], in0=grs[:, 0:NB], scalar1=1.0 / 128, scalar2=1e-6, op0=ALU.mult, op1=ALU.add), reads=[grs_s], writes=[grs_s])
                    P.op("act", lambda e: e.activation(out=grs[:, 0:NB], in_=grs[:, 0:NB], func=AF.Sqrt), reads=[grs_s], writes=[grs_s])
                    P.op("dve", lambda e: e.reciprocal(out=grs[:, NB:2 * NB], in_=grs[:, 0:NB]), reads=[grs_s], writes=[grs_s])
                    yield
                    for b4 in range(NB // 4):
                        tb, tbs = bk[4 + b4 % 2]
                        for i4 in range(4):
                            b = b4 * 4 + i4
                            on, ons = gonb.next()
                            P.op("act", lambda e, on=on, b=b: e.activation(out=on[:], in_=otm[:, b, :], func=AF.Copy, scale=grs[:, NB + b:NB + b + 1]),
                                 reads=[otm_s, grs_s], writes=[ons])
                            P.op("pool", lambda e, on=on: e.tensor_tensor(out=on[:], in0=on[:], in1=gdl, op=ALU.mult), reads=[ng_s], writes=[ons])
                            P.op("pe", lambda e, tb=tb, on=on, i4=i4: e.transpose(tb[:, i4 * 128:(i4 + 1) * 128], on[:], it[:]), reads=[ons, isl], writes=[tbs])
                        P.op("dve", lambda e, tb=tb, b4=b4, goT=goT: e.tensor_tensor(out=goT[:, b4 * 512:(b4 + 1) * 512], in0=tb[:, :], in1=zs[:, b4 * 512:(b4 + 1) * 512], op=ALU.mult),
                             reads=[zs_s], writes=[goT_s, tbs])
                        yield
                    store_mix(goT, goT_s, 8 + h)

                def drain(g):
                    for _ in g:
                        pass

                def merge(primary, others):
                    for _ in primary:
                        for g, w in others:
                            for _i in range(w):
                                next(g, None)

                NH = 8
                gA = thrA(0)
                drain(gA)
                gB = thrB(0, 0)
                drain(gB)
                for h in range(NH):
                    gA = thrA(h + 1) if h + 1 < NH else iter(())
                    for qd in range(NQD):
                        if qd + 1 < NQD:
                            gB = thrB(h, qd + 1)
                        elif h + 1 < NH:
                            drain(gA)
                            gB = thrB(h + 1, 0)
                        else:
                            gB = iter(())
                        merge(thrC(h, qd), [(gB, 2), (gA, 2)])
                        drain(gB)
                    drain(gA)
                    drain(thrG(h))
                P.fence()
                ph.close()
            if len(parts) < 2:
                ph = contextlib.ExitStack()
                zT, zT_s = Buf(P, ph, "zT", [128, SQ], BF16).next()
                P.op("pool", lambda e: e.memset(zT[:], 0.0), writes=[zT_s])
                for j in (range(8, 16) if "gdn" not in parts else range(0, 8)):
                    store_mix(zT, zT_s, j)
                P.fence()
                ph.close()
            P.fence()
            phm.close()
            ph = contextlib.ExitStack()
            b_ = ln_bufs(ph, 2)
            b_["engs"] = ("dve", "pool", "dve")
            mtb = Buf(P, ph, "mixt", [128, NC_, TT], BF16, n=2)
            wmb = Buf(P, ph, "wmo", [128, NC_, 128], BF16, n=NC_)
            wres = []
            for c in range(NC_):
                wt, ws = wmb.next()
                P.dma("sp", f"wmo{c % 4}", lambda e, wt=wt, c=c: e.dma_start(out=wt[:], in_=WMO[l][c]), reads=[wslot["mo", l, c]], writes=[ws])
                wres.append((wt, ws))
            epc = iter(())
            for t in tiles:
                mt, mts = mtb.next()
                P.dma("pool", f"mixld{mtb.i}", lambda e, mt=mt, t=t: e.dma_start(out=mt[:], in_=MIX[t]), reads=mix_slots[t], writes=[mts])

                def wload(c):
                    return wres[c]
                ep_new = out_ln(l, 1, t, NC_, lambda j, mt=mt, mts=mts: (mt[:, j, :], mts), wload, 1.0 / ALPHA, b_)
                for _ in epc:
                    pass
                epc = ep_new
            for _ in epc:
                pass
            P.fence()
            ph.close()

        if "mix" in stages:
            rope_tables()
        for l in range(depth):
            if "ffn1" in stages:
                ffn_stage(l, 1, 0)
            if "mix" in stages:
                for q in range(nseq):
                    mixer_stage(l, q, cfg.get("parts", ("diff", "gdn")))
            if "ffn2" in stages:
                ffn_stage(l, 2, 2)

        outs = []
        ph = contextlib.ExitStack()
        xfst = Buf(P, ph, "xfst", [128, NC_, TT], F32, n=2)
        ytok = Buf(P, ph, "ytok", [128, D], F32, n=2)
        for t in range(ntile):
            ft, fs = xfst.next()
            P.dma("pool", f"xfld{xfst.i}", lambda e, ft=ft, t=t: e.dma_start(out=ft[:], in_=XF[t]),
                  reads=[xf_slots[t]], writes=[fs])
            for q4 in range(4):
                tt = t * 4 + q4
                yt, ys = ytok.next()
                for g in range(4):
                    pb, ps = bk[g % 2]
                    for i in range(4):
                        c = 4 * g + i
                        P.op("pe", lambda e, pb=pb, ft=ft, c=c, i=i, q4=q4: e.transpose(pb[:, i * 128:(i + 1) * 128], ft[:, c, q4 * 128:(q4 + 1) * 128], it[:]),
                             reads=[fs, isl], writes=[ps], signal=(i == 3))
                    if g % 2 == 0:
                        P.op("act", lambda e, yt=yt, pb=pb, g=g: e.copy(out=yt[:, g * 512:(g + 1) * 512], in_=pb[:, :]), reads=[ps], writes=[ys])
                    else:
                        P.op("dve", lambda e, yt=yt, pb=pb, g=g: e.tensor_copy(out=yt[:, g * 512:(g + 1) * 512], in_=pb[:, :]), reads=[ps], writes=[ys])
                osl_ = P.slot()
                outs.append(osl_)
                P.dma("pool", f"ytok{ytok.i}", lambda e, yt=yt, tt=tt: e.dma_start(out=y_out[tt * 128:(tt + 1) * 128, :], in_=yt[:]),
                      reads=[ys], writes=[osl_])
        P.wait_all("pool", outs)
        ph.close()
        P.emit(stack)
    return nc


def host_consts(depth, ln):
    a = np.stack(ln, axis=1)
    a = a.reshape(depth, 6, NC_, 128)
    a = np.transpose(a, (3, 0, 1, 2)).reshape(128, depth * 6 * NC_)
    return np.ascontiguousarray(a.astype(np.float32))


def host_mix_inputs(inp, depth, pos_core):
    f32 = np.float32
    w_in = np.asarray(inp["w_in"])[:depth]
    perm = np.arange(2048)
    d = perm % 64
    perm = np.where(d < 8, perm + 8, np.where(d < 16, perm - 8, perm))
    w_sw = np.ascontiguousarray(w_in[:, :, perm])
    conv_w = np.asarray(inp["conv_w"])[:depth]
    cw = conv_w.reshape(depth, 4, 24, 128).transpose(3, 0, 2, 1).reshape(128, depth * 96)
    hp8 = np.zeros((128, depth * 2), f32)
    hp8[:8, 0::2] = np.asarray(inp["a_log"])[:depth].T
    hp8[:8, 1::2] = np.asarray(inp["dt_bias"])[:depth].T
    lam = np.concatenate([np.asarray(inp[k])[:depth] for k in ("lam_q1", "lam_k1", "lam_q2", "lam_k2")], axis=1)
    lamv = np.broadcast_to(lam.reshape(1, depth * 256), (128, depth * 256))
    ng = np.concatenate([np.asarray(inp["diff_norm_g"])[:depth], np.asarray(inp["delta_norm_g"])[:depth]], axis=1)
    normg = np.broadcast_to(ng.reshape(1, depth * 256), (128, depth * 256))
    pos = np.broadcast_to(np.asarray(pos_core).reshape(1, -1).astype(np.int32), (128, pos_core.size))
    return {"w_in": np.ascontiguousarray(w_in), "w_in_sw": w_sw, "w_out": np.ascontiguousarray(np.asarray(inp["w_out"])[:depth]),
            "convw": np.ascontiguousarray(cw.astype(f32)), "hp8": hp8, "lamv": np.ascontiguousarray(lamv.astype(f32)),
            "normg": np.ascontiguousarray(normg.astype(f32)), "pos": np.ascontiguousarray(pos)}


def host_static_consts():
    f32 = np.float32
    p = np.arange(128)
    d = p % 64
    inv = 500000.0 ** (-(d % 8) / 8.0)
    ropec = np.zeros((128, 2), f32)
    ropec[:, 0] = np.where(d < 16, inv / (2 * np.pi), 0.0)
    ropec[:, 1] = np.where(d < 8, -1.0, np.where(d < 16, 1.0, 0.0))
    i = p[:, None]
    j = p[None, :]
    same = (i // 64) == (j // 64)
    masks = np.zeros((128, 640), f32)
    masks[:, 0:128] = (i <= j)
    masks[:, 128:256] = same & (i > j)
    masks[:, 256:384] = same & (i <= j)
    masks[:, 384] = (p < 64)
    masks[:, 385] = (p >= 64)
    masks[:, 512:640] = same
    return {"ropec": ropec, "masks": masks, "ident": np.eye(128, dtype=f32)}


def make_in_maps(inputs, n_cores, depth, stages=("ffn1", "mix", "ffn2")):
    x = np.asarray(inputs["x"])
    B, S, _ = x.shape
    per = B // n_cores
    pos = np.asarray(inputs["positions"])
    lnp = host_consts(depth, [np.asarray(inputs[k])[:depth] for k in ("ln1_g", "ln1_b", "ln2_g", "ln2_b", "ln3_g", "ln3_b")])
    st = host_static_consts()
    shared = {"lnp": lnp, "ident": st["ident"]}
    if "ffn1" in stages:
        shared["ffn1_w_in"] = np.ascontiguousarray(np.asarray(inputs["ffn1_w_in"])[:depth])
        shared["ffn1_w_out"] = np.ascontiguousarray(np.asarray(inputs["ffn1_w_out"])[:depth])
    if "ffn2" in stages:
        shared["ffn2_w_in"] = np.ascontiguousarray(np.asarray(inputs["ffn2_w_in"])[:depth])
        shared["ffn2_w_out"] = np.ascontiguousarray(np.asarray(inputs["ffn2_w_out"])[:depth])
    in_maps = []
    for c in range(n_cores):
        m = dict(shared)
        m["x"] = np.ascontiguousarray(x[c * per:(c + 1) * per].reshape(per * S, D))
        if "mix" in stages:
            mm = host_mix_inputs(inputs, depth, pos[c * per:(c + 1) * per].reshape(-1))
            if c > 0:
                for k in ("w_in", "w_in_sw", "w_out", "convw", "hp8", "lamv", "normg"):
                    mm[k] = in_maps[0][k]
            m.update(mm)
            m["ropec"] = st["ropec"]
            m["masks"] = st["masks"]
        in_maps.append(m)
    return in_maps, per, S


def kernel(**inputs):
    n = 8
    in_maps, per, S = make_in_maps(inputs, n, DEPTH)
    nc = build_program(dict(ntok=per * S, depth=DEPTH))
    res = run_bass_kernel_spmd(nc, in_maps, core_ids=list(range(n)))
    out = np.concatenate([np.asarray(r["y"]).reshape(per, S, D) for r in res.results], axis=0)
    return out.astype(np.float32)
```

```python
import contextlib
import math
import numpy as np
import concourse.bass as bass
import concourse.mybir as mybir
from concourse.bass_utils import run_bass_kernel_spmd

F32 = mybir.dt.float32
BF16 = mybir.dt.bfloat16
I32 = mybir.dt.int32
AF = mybir.ActivationFunctionType
ALU = mybir.AluOpType

D = 2048
NC_ = 16
DFF = 5632
NFC = 44
DEPTH = 4
SEQ = 2048
ALPHA = (2 * DEPTH) ** 0.25
LN_EPS = 1e-5
IN_WIDTH = 7184


class Slot:
    __slots__ = ("name", "w", "r")

    def __init__(self, name):
        self.name = name
        self.w = None
        self.r = {}


class Prog:
    def __init__(self, nc):
        self.nc = nc
        self.q = {"pe": [], "act": [], "dve": [], "pool": [], "sp": []}
        self.cnt = {}
        self.waited = {e: {} for e in self.q}
        self.nslots = 0
        self.qmap = {}

    def slot(self, name=None):
        self.nslots += 1
        return Slot(name or f"s{self.nslots}")

    def _deps(self, eng, reads, writes, extra=()):
        deps = {}

        def add(ev):
            if ev is None:
                return
            k, v = ev
            if deps.get(k, 0) < v:
                deps[k] = v

        for s in reads:
            add(s.w)
        for s in writes:
            add(s.w)
            for k, v in s.r.items():
                add((k, v))
        for ev in extra:
            add(ev)
        waits = []
        wd = self.waited[eng]
        for k, v in deps.items():
            if k == "pe" and eng == "pe":
                continue
            if wd.get(k, 0) >= v:
                continue
            wd[k] = v
            waits.append((k, v))
        return waits

    def _mark(self, ev, reads, writes):
        k, v = ev
        for s in reads:
            if s.r.get(k, 0) < v:
                s.r[k] = v
        for s in writes:
            s.w = ev
            s.r = {}

    def op(self, eng, fn, reads=(), writes=(), signal=True):
        waits = self._deps(eng, reads, writes)
        c = self.cnt.get(eng, 0)
        if signal:
            c += 1
            self.cnt[eng] = c
            ev = (eng, c)
            inc = (eng, 1)
        else:
            ev = (eng, c + 1)
            inc = None
        self._mark(ev, reads, writes)
        self.q[eng].append((waits, fn, inc))

    def dma(self, qeng, chan, fn, reads=(), writes=()):
        if qeng == "pool" and not chan.startswith(("yld", "yst", "xnst", "xT0", "xT1")):
            qeng = "sp"
        key = "d:" + chan
        c = self.cnt.get(key, 0)
        waits = self._deps(qeng, reads, writes, extra=[(key, c)] if c else [])
        c += 16
        self.cnt[key] = c
        self._mark((key, c), reads, writes)
        self.q[qeng].append((waits, fn, (key, 16)))

    def fence(self):
        for e in self.q:
            waits = []
            wd = self.waited[e]
            for k, v in self.cnt.items():
                if k == "pe" and e == "pe":
                    continue
                if wd.get(k, 0) >= v:
                    continue
                wd[k] = v
                waits.append((k, v))
            if waits:
                self.q[e].append((waits, None, None))

    def wait_all(self, eng, slots):
        waits = self._deps(eng, slots, ())
        self.q[eng].append((waits, None, None))

    def emit(self, stack):
        nc = self.nc
        sems = {}
        for k in self.cnt:
            sems[k] = stack.enter_context(nc.semaphore("sem_" + k.replace(":", "_")))
        engs = {"pe": "tensor", "act": "scalar", "dve": "vector", "pool": "gpsimd", "sp": "sync"}
        q = self.q

        def run(e, lst):
            for waits, fn, inc in lst:
                for k, v in waits:
                    e.wait_ge(sems[k], v)
                if fn is not None:
                    ins = fn(e)
                    if inc is not None:
                        ins.then_inc(sems[inc[0]], inc[1])

        with nc.Block() as block:
            for name, attr in engs.items():
                if not q[name]:
                    continue

                def mk(lst):
                    def f(e):
                        run(e, lst)
                    return f

                getattr(block, attr)(mk(q[name]))


class Buf:
    uid = 0

    def __init__(self, P, stack, name, shape, dtype, n=1, psum=False):
        self.n = n
        self.t = []
        self.s = []
        for i in range(n):
            Buf.uid += 1
            nm = f"{name}{i}_{Buf.uid}"
            if psum:
                t = stack.enter_context(P.nc.psum_tensor(nm, shape, dtype))
            else:
                t = stack.enter_context(P.nc.sbuf_tensor(nm, shape, dtype))
            self.t.append(t)
            self.s.append(P.slot(nm))
        self.i = -1

    def next(self):
        self.i = (self.i + 1) % self.n
        return self.t[self.i], self.s[self.i]

    def cur(self):
        return self.t[self.i], self.s[self.i]


def build_program(cfg):
    NT = cfg["ntok"]
    depth = cfg["depth"]
    stages = cfg.get("stages", ("ffn1", "mix", "ffn2"))
    TT = 512
    ntile = NT // TT
    n128 = NT // 128

    nc = bass.Bass("TRN2", target_bir_lowering=False)
    P = Prog(nc)
    stack = contextlib.ExitStack()

    def din(name, shape, dt=F32):
        return nc.dram_tensor(name, list(shape), dt, kind="ExternalInput").ap()

    x_in = din("x", [NT, D])
    w1i = din("ffn1_w_in", [depth, D, 2 * DFF]) if "ffn1" in stages else None
    w1o = din("ffn1_w_out", [depth, DFF, D]) if "ffn1" in stages else None
    w2i = din("ffn2_w_in", [depth, D, 2 * DFF]) if "ffn2" in stages else None
    w2o = din("ffn2_w_out", [depth, DFF, D]) if "ffn2" in stages else None
    if "mix" in stages:
        w_pr = din("w_in", [depth, D, IN_WIDTH])
        w_sw = din("w_in_sw", [depth, D, 2048])
        w_mo = din("w_out", [depth, D, D])
        convw_in = din("convw", [128, depth * 24 * 4])
        hp8_in = din("hp8", [128, depth * 2])
        lamv_in = din("lamv", [128, depth * 256])
        normg_in = din("normg", [128, depth * 256])
        pos_in = din("pos", [128, NT], I32)
        ropec_in = din("ropec", [128, 2])
        masks_in = din("masks", [128, 640])
    lnp = din("lnp", [128, depth * 6 * NC_])
    ident_in = din("ident", [128, 128])
    y_out = nc.dram_tensor("y", [NT, D], F32, kind="ExternalOutput").ap()

    def dscr(name, shape, dt):
        return nc.dram_tensor(name, list(shape), dt, kind="Internal").ap()

    XF = dscr("XF", [ntile, 128, NC_, TT], F32)
    XB = dscr("XB", [ntile, 128, NC_, TT], BF16)
    WIN = {}
    WOUT = {}
    for l in range(depth):
        for f in (1, 2):
            WIN[l, f] = dscr(f"WIN{l}_{f}", [NFC, 128, 2, NC_, 128], BF16)
            WOUT[l, f] = dscr(f"WOUT{l}_{f}", [NC_, 128, NFC, 128], BF16)
    nseq = NT // SEQ if NT >= SEQ else 1
    SQ = min(SEQ, NT)
    if "mix" in stages:
        WPR = {l: dscr(f"WPR{l}", [56, 128, NC_, 128], BF16) for l in range(depth)}
        WSW = {l: dscr(f"WSW{l}", [16, 128, NC_, 128], BF16) for l in range(depth)}
        WBA = {l: dscr(f"WBA{l}", [128, NC_, 16], BF16) for l in range(depth)}
        WMO = {l: dscr(f"WMO{l}", [NC_, 128, NC_, 128], BF16) for l in range(depth)}
        ROPE = dscr("ROPE", [nseq, 2, 128, SQ], F32)
        MIX = dscr("MIX", [ntile, 128, NC_, TT], BF16)
        mix_slots = [[P.slot(f"MIX{t}_{j}") for j in range(NC_)] for t in range(ntile)]
        rope_slots = [P.slot(f"ROPE{q}") for q in range(nseq)]
    xf_slots = [P.slot(f"XF{t}") for t in range(ntile)]
    xb_slots = [P.slot(f"XB{t}") for t in range(ntile)]
    wslot = {}

    with stack:
        ident = Buf(P, stack, "ident", [128, 128], F32)
        ones_bf = Buf(P, stack, "ones_bf", [128, 128], BF16)
        lnp_sb = Buf(P, stack, "lnp_sb", [128, depth * 6 * NC_], F32)
        it, isl = ident.next()
        P.dma("pool", "const", lambda e: e.dma_start(out=it[:], in_=ident_in[:, :]), writes=[isl])
        ot, osl = ones_bf.next()
        P.op("pool", lambda e: e.memset(ot[:], 1.0), writes=[osl])
        lt, lsl = lnp_sb.next()
        if not (cfg.get("dbg", 0) & 2):
            P.dma("pool", "const", lambda e: e.dma_start(out=lt[:], in_=lnp[:, :]), writes=[lsl])

        def lnvec(l, which, c):
            o = (l * 6 + which) * NC_ + c
            return lt[:, o:o + 1]

        banks = Buf(P, stack, "bank", [128, 512], F32, n=8, psum=True)
        bk = list(zip(banks.t, banks.s))

        ph = contextlib.ExitStack()
        NSTG = 3
        stg = Buf(P, ph, "stg", [128, NC_ * 512], F32, n=NSTG)
        stgb = Buf(P, ph, "stgb", [128, NC_ * 512], BF16, n=NSTG)
        ci = [0]

        def precast_span(src2d, row0, nk, col0, ncc, dsts, slots):
            st, ss = stg.next()
            sb, sbs = stgb.next()
            k_ = ci[0]
            ci[0] += 1
            i_ = stg.i
            qe = "sp" if k_ % 2 == 0 else "act"
            W = ncc * 128
            v = st[:, :nk * W].rearrange("p (k w) -> p k w", k=nk)
            srcap = src2d[row0:row0 + nk * 128, col0:col0 + W].rearrange("(k p) w -> p k w", p=128)
            P.dma("act", f"pcl{i_}", lambda e: e.dma_start(out=v, in_=srcap), writes=[ss])
            qe = "sp"
            ce = ("dve", "act", "dve", "act", "pool")[k_ % 5]
            ov = sb[:, :nk * W].rearrange("p (h k c) -> p h k c", h=ncc, k=nk)
            iv = st[:, :nk * W].rearrange("p (k h c) -> p h k c", k=nk, h=ncc)
            if ce == "act":
                P.op("act", lambda e: e.copy(out=ov, in_=iv), reads=[ss], writes=[sbs])
            else:
                P.op(ce, lambda e: e.tensor_copy(out=ov, in_=iv), reads=[ss], writes=[sbs])
            for h in range(ncc):
                P.dma(qe, f"pcs{i_}", lambda e, h=h: e.dma_start(out=dsts[h], in_=sb[:, h * nk * 128:(h + 1) * nk * 128]),
                      reads=[sbs], writes=[slots[h]])

        def precast_ffn(l, f, wi, wo):
            for j in range(NFC):
                wslot["in", l, f, j] = P.slot()
            for c in range(NC_):
                wslot["out", l, f, c] = P.slot()
            for J in range(NFC // 4):
                for h in range(2):
                    precast_span(wi[l], 0, NC_, h * DFF + J * 512, 4,
                                 [WIN[l, f][4 * J + cc][:, h].rearrange("p k c -> p (k c)") for cc in range(4)],
                                 [wslot["in", l, f, 4 * J + cc] for cc in range(4)])
            for C in range(NC_ // 4):
                for (k0, nk) in ((0, 16), (16, 16), (32, 12)):
                    precast_span(wo[l], k0 * 128, nk, C * 512, 4,
                                 [WOUT[l, f][4 * C + cc][:, k0:k0 + nk, :].rearrange("p k c -> p (k c)") for cc in range(4)],
                                 [wslot["out", l, f, 4 * C + cc] for cc in range(4)])

        def precast_mix(l):
            for (kind, W_, src, n) in (("pr", WPR, w_pr, 56), ("sw", WSW, w_sw, 16), ("mo", WMO, w_mo, 16)):
                for m in range(n):
                    wslot[kind, l, m] = P.slot()
                for s4 in range(n // 4):
                    precast_span(src[l], 0, NC_, s4 * 512, 4,
                                 [W_[l][4 * s4 + cc].rearrange("p k c -> p (k c)") for cc in range(4)],
                                 [wslot[kind, l, 4 * s4 + cc] for cc in range(4)])
            st, ss = stg.next()
            sb, sbs = stgb.next()
            ci[0] += 1
            i_ = stg.i
            v = st[:, :NC_ * 16].rearrange("p (k w) -> p k w", k=NC_)
            srcap = w_pr[l][:, 7168:7184].rearrange("(k p) w -> p k w", p=128)
            P.dma("sp", f"pcl{i_}", lambda e: e.dma_start(out=v, in_=srcap), writes=[ss])
            P.op("dve", lambda e: e.tensor_copy(out=sb[:, :NC_ * 16], in_=st[:, :NC_ * 16]), reads=[ss], writes=[sbs])
            sl = P.slot()
            wslot["ba", l] = sl
            P.dma("sp", f"pcs{i_}", lambda e: e.dma_start(out=WBA[l].rearrange("p k c -> p (k c)"), in_=sb[:, :NC_ * 16]),
                  reads=[sbs], writes=[sl])

        for l in range(depth):
            if "mix" in stages:
                precast_mix(l)
            if "ffn1" in stages:
                precast_ffn(l, 1, w1i, w1o)
            if "ffn2" in stages:
                precast_ffn(l, 2, w2i, w2o)

        P.fence()
        ph.close()
        ph = contextlib.ExitStack()
        xtok = Buf(P, ph, "xtok", [128, D], F32, n=2)
        xfst = Buf(P, ph, "xfst", [128, NC_, TT], F32, n=2)
        xbst = Buf(P, ph, "xbst", [128, NC_, TT], BF16, n=2)
        for t in range(ntile):
            ft, fs = xfst.next()
            bt, bs = xbst.next()
            for q4 in range(4):
                tt = t * 4 + q4
                xt, xs = xtok.next()
                P.dma("pool", f"xtok{xtok.i}", lambda e, xt=xt, tt=tt: e.dma_start(out=xt[:], in_=x_in[tt * 128:(tt + 1) * 128, :]),
                      writes=[xs])
                for g in range(4):
                    pb, ps = bk[g % 2]
                    for i in range(4):
                        c = 4 * g + i
                        P.op("pe", lambda e, pb=pb, xt=xt, c=c, i=i: e.transpose(pb[:, i * 128:(i + 1) * 128], xt[:, c * 128:(c + 1) * 128], it[:]),
                             reads=[xs, isl], writes=[ps], signal=(i == 3))
                    pv = pb[:, :].rearrange("p (a b) -> p a b", a=4)
                    P.op("act", lambda e, ft=ft, pv=pv, g=g, q4=q4: e.copy(out=ft[:, 4 * g:4 * g + 4, q4 * 128:(q4 + 1) * 128], in_=pv),
                         reads=[ps], writes=[fs])
                    P.op("dve", lambda e, bt=bt, ft=ft, g=g, q4=q4: e.tensor_copy(out=bt[:, 4 * g:4 * g + 4, q4 * 128:(q4 + 1) * 128],
                                                                                 in_=ft[:, 4 * g:4 * g + 4, q4 * 128:(q4 + 1) * 128]),
                         reads=[fs], writes=[bs])
            tok = slice(t * TT, (t + 1) * TT)
            P.dma("pool", f"xfst{xfst.i}", lambda e, ft=ft, t=t: e.dma_start(out=XF[t], in_=ft[:]),
                  reads=[fs], writes=[xf_slots[t]])
            P.dma("pool", f"xbst{xbst.i}", lambda e, bt=bt, t=t: e.dma_start(out=XB[t], in_=bt[:]),
                  reads=[bs], writes=[xb_slots[t]])

        P.fence()
        ph.close()

        def ln_bufs(ph, b_ny=1):
            b = {}
            b["yb"] = Buf(P, ph, "ybuf", [128, NC_, TT], F32, n=b_ny)
            b["xnb"] = Buf(P, ph, "xnb", [128, TT], BF16, n=3)
            b["ybf"] = Buf(P, ph, "ybf", [128, TT], BF16, n=3)
            b["ysq"] = Buf(P, ph, "ysq", [128, TT], BF16, n=3)
            b["mean"] = Buf(P, ph, "mean", [128, TT], F32, n=2)
            b["rstd"] = Buf(P, ph, "rstd", [128, TT], F32, n=2)
            b["tmp"] = Buf(P, ph, "tmpn", [128, TT], F32, n=2)
            b["y_cs_all"] = [[P.slot() for c in range(NC_)] for _ in range(b_ny)]
            return b

        def out_ln(l, which_ln, t, nk, rhs_fn, wload, s_res, b):
            eps = LN_EPS / (ALPHA * ALPHA)
            y_t, y_s = b["yb"].next()
            y_cs = b["y_cs_all"][b["yb"].i]
            mean_t, mean_s = b["mean"].next()
            rstd_t, rstd_s = b["rstd"].next()
            tmp, ybf, ysq = b["tmp"], b["ybf"], b["ysq"]
            S1, S1s = bk[6]
            S2, S2s = bk[7]
            P.dma("pool", "yld", lambda e: e.dma_start(out=y_t[:], in_=XF[t]), reads=[xf_slots[t]], writes=[y_s] + y_cs)
            pend = [None]
            for c in range(NC_):
                wt, ws = wload(c)
                pb, ps = bk[4 + (c % 2)]
                for j in range(nk):
                    ra, rs = rhs_fn(j)
                    P.op("pe", lambda e, pb=pb, wt=wt, j=j, ra=ra: e.matmul(pb[:, :], wt[:, j, :], ra, start=(j == 0), stop=(j == nk - 1)),
                         reads=[ws, rs], writes=[ps], signal=(j == nk - 1))
                P.op("dve", lambda e, pb=pb, c=c: e.scalar_tensor_tensor(
                    out=y_t[:, c, :], in0=pb[:, :], scalar=s_res, in1=y_t[:, c, :], op0=ALU.mult, op1=ALU.add),
                    reads=[ps, y_cs[c]], writes=[y_cs[c]])
                yt, ys = ybf.next()
                qt, qs = ysq.next()
                P.op("pool", lambda e, yt=yt, c=c: e.tensor_copy(out=yt[:], in_=y_t[:, c, :]), reads=[y_cs[c]], writes=[ys])
                P.op("act", lambda e, qt=qt, c=c: e.activation(out=qt[:], in_=y_t[:, c, :], func=AF.Square), reads=[y_cs[c]], writes=[qs])

                def stats(yt=yt, ys=ys, qt=qt, qs=qs, c=c):
                    P.op("pe", lambda e: e.matmul(S1[:, :], ot[:], yt[:], start=(c == 0), stop=(c == NC_ - 1)), reads=[osl, ys], writes=[S1s])
                    P.op("pe", lambda e: e.matmul(S2[:, :], ot[:], qt[:], start=(c == 0), stop=(c == NC_ - 1)), reads=[osl, qs], writes=[S2s])
                if pend[0] is not None:
                    pend[0]()
                pend[0] = stats
            pend[0]()
            P.op("dve", lambda e: e.tensor_scalar(out=mean_t[:], in0=S1[:, :], scalar1=1.0 / D, scalar2=None, op0=ALU.mult), reads=[S1s], writes=[mean_s])
            m2, m2s = tmp.next()
            P.op("dve", lambda e: e.tensor_tensor(out=m2[:], in0=mean_t[:], in1=mean_t[:], op=ALU.mult), reads=[mean_s], writes=[m2s])
            P.op("dve", lambda e: e.scalar_tensor_tensor(out=rstd_t[:], in0=S2[:, :], scalar=1.0 / D, in1=m2[:], op0=ALU.mult, op1=ALU.subtract),
                 reads=[S2s, m2s], writes=[rstd_s])
            P.op("dve", lambda e: e.tensor_scalar(out=rstd_t[:], in0=rstd_t[:], scalar1=eps, scalar2=None, op0=ALU.add), reads=[rstd_s], writes=[rstd_s])
            P.op("act", lambda e: e.activation(out=rstd_t[:], in_=rstd_t[:], func=AF.Sqrt), reads=[rstd_s], writes=[rstd_s])
            P.op("dve", lambda e: e.reciprocal(out=rstd_t[:], in_=rstd_t[:]), reads=[rstd_s], writes=[rstd_s])
            return ln_norm(l, which_ln, t, b, y_t, y_s, y_cs, mean_t, mean_s, rstd_t, rstd_s, b.get('engs', ('pool',)))

        def ln_norm(l, which_ln, t, b, y_t, y_s, y_cs, mean_t, mean_s, rstd_t, rstd_s, engs=("pool",)):
            for c in range(NC_):
                en = engs[c % len(engs)]
                g_ap = lnvec(l, 2 * which_ln, c)
                b_ap = lnvec(l, 2 * which_ln + 1, c)
                P.op(en, lambda e, c=c: e.tensor_tensor(out=y_t[:, c, :], in0=y_t[:, c, :], in1=mean_t[:], op=ALU.subtract),
                     reads=[mean_s], writes=[y_cs[c]])
                P.op(en, lambda e, c=c: e.tensor_tensor(out=y_t[:, c, :], in0=y_t[:, c, :], in1=rstd_t[:], op=ALU.mult),
                     reads=[rstd_s], writes=[y_cs[c]])
                P.op(en, lambda e, c=c, g_ap=g_ap, b_ap=b_ap: e.tensor_scalar(out=y_t[:, c, :], in0=y_t[:, c, :], scalar1=g_ap, scalar2=b_ap, op0=ALU.mult, op1=ALU.add),
                     reads=[lsl], writes=[y_cs[c]])
                xn_t, xn_s = b["xnb"].next()
                P.op(en, lambda e, c=c, xn_t=xn_t: e.tensor_copy(out=xn_t[:], in_=y_t[:, c, :]), reads=[y_cs[c]], writes=[xn_s])
                P.dma("pool", f"xnst{b['xnb'].i}", lambda e, c=c, xn_t=xn_t: e.dma_start(out=XB[t][:, c, :], in_=xn_t[:]), reads=[xn_s], writes=[xb_slots[t]])
                yield
            P.dma("pool", "yst", lambda e: e.dma_start(out=XF[t], in_=y_t[:]), reads=[y_s] + y_cs, writes=[xf_slots[t]])
            yield

        def ffn_stage(l, f, which_ln):
            ph = contextlib.ExitStack()
            xT = Buf(P, ph, "xT", [128, NC_, TT], BF16, n=2)
            aT = Buf(P, ph, "aT", [128, NFC, TT], BF16)
            wib = Buf(P, ph, "wib", [128, 2, NC_, 128], BF16, n=4)
            wob = Buf(P, ph, "wob", [128, NFC, 128], BF16, n=2)
            sg = Buf(P, ph, "sg", [128, TT], F32, n=2)
            b = ln_bufs(ph)
            aT_t, aT_s = aT.next()
            aT_cs = [P.slot(f"aT{j}") for j in range(NFC)]
            ep = iter(())
            nxt = None
            for t in range(ntile):
                if nxt is None:
                    xt, xs = xT.next()
                    P.dma("pool", f"xT{xT.i}", lambda e, xt=xt, t=t: e.dma_start(out=xt[:], in_=XB[t]), reads=[xb_slots[t]], writes=[xs])
                else:
                    xt, xs = nxt
                for j in range(NFC):
                    wt, ws = wib.next()
                    P.dma("sp", f"wib{wib.i}", lambda e, wt=wt, j=j: e.dma_start(out=wt[:], in_=WIN[l, f][j]),
                          reads=[wslot["in", l, f, j]], writes=[ws])
                    gb, gs = bk[(j % 2) * 2]
                    ub, us = bk[(j % 2) * 2 + 1]
                    for h, (pb, ps) in enumerate(((gb, gs), (ub, us))):
                        for k in range(NC_):
                            P.op("pe", lambda e, pb=pb, wt=wt, xt=xt, k=k, h=h: e.matmul(
                                pb[:, :], wt[:, h, k, :], xt[:, k, :], start=(k == 0), stop=(k == NC_ - 1)),
                                reads=[ws, xs], writes=[ps], signal=(k == NC_ - 1))
                    st_, ss_ = sg.next()
                    P.op("act", lambda e, st_=st_, gb=gb: e.activation(out=st_[:], in_=gb[:, :], func=AF.Silu), reads=[gs], writes=[ss_])
                    P.op("dve", lambda e, st_=st_, ub=ub, j=j: e.tensor_tensor(out=aT_t[:, j, :], in0=ub[:, :], in1=st_[:], op=ALU.mult),
                         reads=[us, ss_], writes=[aT_cs[j]])

                def wload(c):
                    wt, ws = wob.next()
                    P.dma("sp", f"wob{wob.i}", lambda e: e.dma_start(out=wt[:], in_=WOUT[l, f][c]), reads=[wslot["out", l, f, c]], writes=[ws])
                    return wt, ws
                if t + 1 < ntile:
                    nxt = xT.next()
                    P.dma("pool", f"xT{xT.i}", lambda e, xt2=nxt[0], t=t: e.dma_start(out=xt2[:], in_=XB[t + 1]), reads=[xb_slots[t + 1]], writes=[nxt[1]])
                for _ in out_ln(l, which_ln, t, NFC, lambda j: (aT_t[:, j, :], aT_cs[j]), wload, 0.5 / ALPHA, b):
                    pass
            P.fence()
            ph.close()

        AQ, AK, AV, DQ, DK, DV, DZ = 0, 8, 16, 24, 32, 40, 48
        NB = SQ // 128
        NQ = SQ // 512
        AX = mybir.AxisListType.X
        if "mix" in stages:
            def cload(name, shape, src, dt=F32):
                bf = Buf(P, stack, name, shape, dt)
                t_, s_ = bf.next()
                P.dma("pool", "const", lambda e: e.dma_start(out=t_[:], in_=src), writes=[s_])
                return t_, s_
            mk_t, mk_s = cload("masks", [128, 640], masks_in[:, :])
            cw_t, cw_s = cload("convw", [128, depth * 96], convw_in[:, :])
            hp_t, hp_s = cload("hp8", [128, depth * 2], hp8_in[:, :])
            lv_t, lv_s = cload("lamv", [128, depth * 256], lamv_in[:, :])
            ng_t, ng_s = cload("normg", [128, depth * 256], normg_in[:, :])
            rc_t, rc_s = cload("ropec", [128, 2], ropec_in[:, :])
            TRIf, LSTR, UINC, CIND, BLKM = mk_t[:, 0:128], mk_t[:, 128:256], mk_t[:, 256:384], mk_t[:, 384:386], mk_t[:, 512:640]
            tribf_t, tribf_s = Buf(P, stack, "tribf", [128, 128], BF16).next()
            P.op("dve", lambda e: e.tensor_copy(out=tribf_t[:], in_=TRIf), reads=[mk_s], writes=[tribf_s])
            zer_t, zer_s = Buf(P, stack, "zerf", [128, 128], F32).next()
            P.op("pool", lambda e: e.memset(zer_t[:], 0.0), writes=[zer_s])
            onf_t, onf_s = Buf(P, stack, "onesf", [128, 128], F32).next()
            P.op("pool", lambda e: e.memset(onf_t[:], 1.0), writes=[onf_s])
            lam_t, lam_s = Buf(P, stack, "lamt", [128, depth * 2], F32).next()
            nA_t, nA_s = Buf(P, stack, "negA", [128, depth], F32).next()
            sc_t, sc_s = Buf(P, stack, "lamsc", [128, 64], F32).next()
            sc2_t, sc2_s = Buf(P, stack, "lamsc2", [128, 4], F32).next()
            for l in range(depth):
                lam_init = 0.8 - 0.6 * math.exp(-0.3 * l)
                for z in range(2):
                    o_ = l * 256 + z * 128
                    P.op("dve", lambda e, o_=o_: e.tensor_tensor(out=sc_t[:], in0=lv_t[:, o_:o_ + 64], in1=lv_t[:, o_ + 64:o_ + 128], op=ALU.mult),
                         reads=[lv_s], writes=[sc_s])
                    P.op("dve", lambda e, z=z: e.reduce_sum(out=sc2_t[:, z:z + 1], in_=sc_t[:], axis=AX), reads=[sc_s], writes=[sc2_s])
                P.op("act", lambda e: e.activation(out=sc2_t[:, 0:2], in_=sc2_t[:, 0:2], func=AF.Exp), reads=[sc2_s], writes=[sc2_s])
                P.op("dve", lambda e: e.tensor_tensor(out=sc2_t[:, 2:3], in0=sc2_t[:, 0:1], in1=sc2_t[:, 1:2], op=ALU.subtract), reads=[sc2_s], writes=[sc2_s])
                P.op("dve", lambda e, l=l, lam_init=lam_init: e.tensor_scalar(out=lam_t[:, 2 * l:2 * l + 1], in0=sc2_t[:, 2:3], scalar1=lam_init, scalar2=None, op0=ALU.add),
                     reads=[sc2_s], writes=[lam_s])
                P.op("dve", lambda e, l=l: e.tensor_scalar(out=lam_t[:, 2 * l + 1:2 * l + 2], in0=lam_t[:, 2 * l:2 * l + 1], scalar1=-1.0, scalar2=None, op0=ALU.mult),
                     reads=[lam_s], writes=[lam_s])
                P.op("dve", lambda e, l=l, lam_init=lam_init: e.tensor_scalar(out=ng_t[:, l * 256:l * 256 + 128], in0=ng_t[:, l * 256:l * 256 + 128],
                                                                              scalar1=1.0 - lam_init, scalar2=None, op0=ALU.mult), reads=[ng_s], writes=[ng_s])
                P.op("act", lambda e, l=l: e.activation(out=nA_t[:, l:l + 1], in_=hp_t[:, 2 * l:2 * l + 1], func=AF.Exp), reads=[hp_s], writes=[nA_s])
                P.op("dve", lambda e, l=l: e.tensor_scalar(out=nA_t[:, l:l + 1], in0=nA_t[:, l:l + 1], scalar1=-1.0, scalar2=None, op0=ALU.mult), reads=[nA_s], writes=[nA_s])

        def rope_tables():
            P.fence()
            ph = contextlib.ExitStack()
            posi, posi_s = Buf(P, ph, "posi", [128, SQ], I32).next()
            pf, pf_s = Buf(P, ph, "posf", [128, SQ], F32).next()
            ti, ti_s = Buf(P, ph, "rti", [128, SQ], I32).next()
            tf, tf_s = Buf(P, ph, "rtf", [128, SQ], F32).next()
            u, u_s = Buf(P, ph, "ru", [128, SQ], F32).next()
            tabs = Buf(P, ph, "rtab", [128, SQ], F32, n=2)
            for q in range(nseq):
                P.dma("pool", "posld", lambda e, q=q: e.dma_start(out=posi[:], in_=pos_in[:, q * SQ:(q + 1) * SQ]), writes=[posi_s])
                P.op("dve", lambda e: e.tensor_copy(out=pf[:], in_=posi[:]), reads=[posi_s], writes=[pf_s])
                P.op("dve", lambda e: e.tensor_scalar(out=pf[:], in0=pf[:], scalar1=rc_t[:, 0:1], scalar2=None, op0=ALU.mult), reads=[pf_s, rc_s], writes=[pf_s])
                for idx, off in ((1, 0.0), (0, 0.25)):
                    P.op("dve", lambda e, off=off: e.tensor_scalar(out=u[:], in0=pf[:], scalar1=off, scalar2=None, op0=ALU.add), reads=[pf_s], writes=[u_s])
                    P.op("dve", lambda e: e.tensor_copy(out=ti[:], in_=u[:]), reads=[u_s], writes=[ti_s])
                    P.op("dve", lambda e: e.tensor_copy(out=tf[:], in_=ti[:]), reads=[ti_s], writes=[tf_s])
                    P.op("dve", lambda e: e.tensor_tensor(out=u[:], in0=u[:], in1=tf[:], op=ALU.subtract), reads=[u_s, tf_s], writes=[u_s])
                    P.op("dve", lambda e: e.tensor_single_scalar(out=tf[:], in_=u[:], scalar=0.5, op=ALU.is_gt), reads=[u_s], writes=[tf_s])
                    P.op("dve", lambda e: e.tensor_tensor(out=u[:], in0=u[:], in1=tf[:], op=ALU.subtract), reads=[u_s, tf_s], writes=[u_s])
                    P.op("dve", lambda e: e.tensor_single_scalar(out=tf[:], in_=u[:], scalar=-0.5, op=ALU.is_lt), reads=[u_s], writes=[tf_s])
                    P.op("dve", lambda e: e.tensor_tensor(out=u[:], in0=u[:], in1=tf[:], op=ALU.add), reads=[u_s, tf_s], writes=[u_s])
                    tb, tbs = tabs.next()
                    P.op("act", lambda e, tb=tb: e.activation(out=tb[:], in_=u[:], func=AF.Sin, scale=2.0 * math.pi), reads=[u_s], writes=[tbs])
                    if idx == 1:
                        P.op("dve", lambda e, tb=tb: e.tensor_scalar(out=tb[:], in0=tb[:], scalar1=rc_t[:, 1:2], scalar2=None, op0=ALU.mult), reads=[tbs, rc_s], writes=[tbs])
                    P.dma("pool", f"ropest{tabs.i}", lambda e, tb=tb, q=q, idx=idx: e.dma_start(out=ROPE[q, idx], in_=tb[:]), reads=[tbs], writes=[rope_slots[q]])
            P.fence()
            ph.close()

        def mixer_stage(l, q, parts=("diff", "gdn")):
            tiles = list(range(q * NQ, (q + 1) * NQ))
            phm = contextlib.ExitStack()
            xTs, xTs_s = Buf(P, phm, "xTs", [128, NC_, SQ], BF16).next()
            for iq, t in enumerate(tiles):
                P.dma("pool", "xTsld", lambda e, iq=iq, t=t: e.dma_start(out=xTs[:, :, iq * 512:(iq + 1) * 512], in_=XB[t]), reads=[xb_slots[t]], writes=[xTs_s])
            wpt = Buf(P, phm, "wpt", [128, NC_, 128], BF16, n=3)

            def wget(kind, m):
                wt, ws = wpt.next()
                src = {"pr": WPR, "sw": WSW}[kind][l][m]
                P.dma("sp", f"wpt{wpt.i}", lambda e: e.dma_start(out=wt[:], in_=src), reads=[wslot[kind, l, m]], writes=[ws])
                return wt, ws

            def proj_fm(wt, ws, iq, pb, ps, mlo=0, mhi=128):
                for k in range(NC_):
                    P.op("pe", lambda e, k=k: e.matmul(pb[0:mhi - mlo, :], wt[:, k, mlo:mhi], xTs[:, k, iq * 512:(iq + 1) * 512],
                                                      start=(k == 0), stop=(k == NC_ - 1)),
                         reads=[ws, xTs_s], writes=[ps], signal=(k == NC_ - 1))

            def store_mix(oT, oT_s, j):
                for iq, t in enumerate(tiles):
                    P.dma("pool", "mixst", lambda e, iq=iq, t=t: e.dma_start(out=MIX[t][:, j, :], in_=oT[:, iq * 512:(iq + 1) * 512]),
                          reads=[oT_s], writes=[mix_slots[t][j]])

            if "diff" in parts:
                ph = contextlib.ExitStack()
                Ct, Ct_s = Buf(P, ph, "Ct", [128, SQ], F32).next()
                St, St_s = Buf(P, ph, "St", [128, SQ], F32).next()
                P.dma("pool", "ropeld", lambda e: e.dma_start(out=Ct[:], in_=ROPE[q, 0]), reads=[rope_slots[q]], writes=[Ct_s])
                P.dma("pool", "ropeld", lambda e: e.dma_start(out=St[:], in_=ROPE[q, 1]), reads=[rope_slots[q]], writes=[St_s])
                qTb = Buf(P, ph, "qT", [128, SQ], BF16, n=2)
                kTb = Buf(P, ph, "kT", [128, SQ], BF16, n=2)
                vab = Buf(P, ph, "vaug", [128, NB, 132], BF16, n=2)
                for va_, va_s_ in zip(vab.t, vab.s):
                    P.op("pool", lambda e, va_=va_: e.memset(va_[:, :, 128:132], 1.0), writes=[va_s_])
                Ea, _ = Buf(P, ph, "Eall", [128, NB, 512], BF16).next()
                Es = [P.slot() for _ in range(NB)]
                r1 = Buf(P, ph, "r1", [128, 512], F32, n=2)
                r2 = Buf(P, ph, "r2", [128, 512], F32, n=2)
                o1, o1_s = Buf(P, ph, "o1", [128, 4, 128], F32).next()
                ofb = Buf(P, ph, "of", [128, 128], F32, n=2)
                osq = Buf(P, ph, "osq", [128, 128], F32, n=2)
                onb = Buf(P, ph, "onb", [128, 128], F32, n=3)
                sm = Buf(P, ph, "smd", [128, 4], F32, n=4)
                oTb = Buf(P, ph, "oTd", [128, SQ], BF16, n=2)
                nlam = lam_t[:, 2 * l + 1:2 * l + 2]
                gdr = ng_t[:, l * 256:l * 256 + 128]
                dres = {}

                def thrDP(h):
                    qT, qT_s = qTb.next()
                    kT, kT_s = kTb.next()
                    va, va_s = vab.next()
                    dres[h] = (qT, qT_s, kT, kT_s, va, va_s)
                    for (ma, mb, dst, dst_s) in ((AQ + h, h, qT, qT_s), (AK + h, 8 + h, kT, kT_s)):
                        wa, was = wget("pr", ma)
                        wb, wbs = wget("sw", mb)
                        for iq in range(NQ):
                            tok = slice(iq * 512, (iq + 1) * 512)
                            pa, pas = bk[4]
                            pb_, pbs = bk[5]
                            proj_fm(wa, was, iq, pa, pas)
                            proj_fm(wb, wbs, iq, pb_, pbs)
                            t1, t1s = r1.next()
                            t2, t2s = r2.next()
                            P.op("dve", lambda e, t1=t1, tok=tok, pb_=pb_: e.tensor_tensor(out=t1[:], in0=pb_[:, :], in1=St[:, tok], op=ALU.mult), reads=[St_s], writes=[t1s, pbs])
                            P.op("dve", lambda e, t2=t2, tok=tok, pa=pa: e.tensor_tensor(out=t2[:], in0=pa[:, :], in1=Ct[:, tok], op=ALU.mult), reads=[Ct_s], writes=[t2s, pas])
                            P.op("pool", lambda e, t1=t1, t2=t2, dst=dst, tok=tok: e.tensor_tensor(out=dst[:, tok], in0=t1[:], in1=t2[:], op=ALU.add),
                                 reads=[t1s, t2s], writes=[dst_s])
                            yield
                    wv, wvs = wget("pr", AV + h)
                    for g4 in range(NQ):
                        pb, ps = bk[7]
                        for i4 in range(4):
                            tt = g4 * 4 + i4
                            for k in range(NC_):
                                P.op("pe", lambda e, pb=pb, i4=i4, tt=tt, k=k, wv=wv: e.matmul(pb[:, i4 * 128:(i4 + 1) * 128], xTs[:, k, tt * 128:(tt + 1) * 128], wv[:, k, :],
                                                                                      start=(k == 0), stop=(k == NC_ - 1)),
                                     reads=[wvs, xTs_s], writes=[ps], signal=(k == NC_ - 1))
                            yield
                        P.op("act", lambda e, pb=pb, g4=g4, va=va: e.copy(out=va[:, g4 * 4:(g4 + 1) * 4, 0:128], in_=pb[:, :].rearrange("p (a b) -> p a b", a=4)),
                             reads=[], writes=[va_s, ps])
                        yield

                def thrDA(h):
                    qT, qT_s, kT, kT_s, va, va_s = dres[h]
                    oT, oT_s = oTb.next()
                    for J in range(NQ):
                        for c in range(2):
                            cs = slice(c * 64, (c + 1) * 64)
                            for i in range(4 * J + 4):
                                sb_, ss_ = bk[i % 2]
                                P.op("pe", lambda e, sb_=sb_, i=i, cs=cs, J=J: e.matmul(sb_[:, :], kT[cs, i * 128:(i + 1) * 128], qT[cs, J * 512:(J + 1) * 512], start=True, stop=True),
                                     reads=[kT_s, qT_s], writes=[ss_])
                                P.op("act", lambda e, sb_=sb_, i=i: e.activation(out=Ea[:, i, :], in_=sb_[:, :], func=AF.Exp, scale=0.125), reads=[], writes=[Es[i], ss_])
                                r = i - 4 * J
                                if r >= 0:
                                    P.op("pool", lambda e, i=i, r=r: e.tensor_tensor(out=Ea[:, i, r * 128:(r + 1) * 128], in0=Ea[:, i, r * 128:(r + 1) * 128], in1=tribf_t[:], op=ALU.mult),
                                         reads=[tribf_s], writes=[Es[i]])
                                if i % 2 == 1:
                                    yield
                            pend = None
                            for u in range(4):
                                ob, obs = bk[2 + u % 2]
                                last = 4 * J + u
                                for i in range(last + 1):
                                    P.op("pe", lambda e, ob=ob, i=i, u=u, last=last: e.matmul(ob[:, 0:129], Ea[:, i, u * 128:(u + 1) * 128], va[:, i, 0:129], start=(i == 0), stop=(i == last)),
                                         reads=[Es[i], va_s], writes=[obs], signal=(i == last))
                                if pend is not None:
                                    pend()
                                    pend = None
                                s4, s4s = sm.next()
                                P.op("dve", lambda e, ob=ob, s4=s4: e.reciprocal(out=s4[:, 0:1], in_=ob[:, 128:129]), reads=[], writes=[s4s, obs])
                                if c == 0:
                                    P.op("act", lambda e, ob=ob, s4=s4, u=u: e.activation(out=o1[:, u, :], in_=ob[:, 0:128], func=AF.Identity, scale=s4[:, 0:1]),
                                         reads=[s4s], writes=[o1_s, obs])
                                else:
                                    P.op("dve", lambda e, s4=s4: e.tensor_scalar(out=s4[:, 1:2], in0=s4[:, 0:1], scalar1=nlam, scalar2=None, op0=ALU.mult), reads=[lam_s], writes=[s4s])
                                    of, ofs = ofb.next()
                                    P.op("dve", lambda e, ob=ob, s4=s4, u=u, of=of: e.scalar_tensor_tensor(out=of[:], in0=ob[:, 0:128], scalar=s4[:, 1:2], in1=o1[:, u, :],
                                                                                                              op0=ALU.mult, op1=ALU.add), reads=[s4s, o1_s], writes=[ofs, obs])
                                    sq, sqs = osq.next()
                                    P.op("dve", lambda e, sq=sq, of=of: e.tensor_tensor(out=sq[:], in0=of[:], in1=of[:], op=ALU.mult), reads=[ofs], writes=[sqs])
                                    P.op("dve", lambda e, sq=sq, s4=s4: e.reduce_sum(out=s4[:, 2:3], in_=sq[:], axis=AX), reads=[sqs], writes=[s4s])
                                    P.op("dve", lambda e, s4=s4: e.tensor_scalar(out=s4[:, 2:3], in0=s4[:, 2:3], scalar1=1.0 / 128, scalar2=1e-5, op0=ALU.mult, op1=ALU.add), reads=[s4s], writes=[s4s])
                                    P.op("act", lambda e, s4=s4: e.activation(out=s4[:, 2:3], in_=s4[:, 2:3], func=AF.Sqrt), reads=[s4s], writes=[s4s])
                                    P.op("dve", lambda e, s4=s4: e.reciprocal(out=s4[:, 3:4], in_=s4[:, 2:3]), reads=[s4s], writes=[s4s])
                                    on, ons = onb.next()
                                    P.op("dve", lambda e, on=on, of=of, s4=s4: e.scalar_tensor_tensor(out=on[:], in0=of[:], scalar=s4[:, 3:4], in1=gdr, op0=ALU.mult, op1=ALU.mult),
                                         reads=[ofs, s4s, ng_s], writes=[ons])

                                    def pend(on=on, ons=ons, u=u):
                                        tb, tbs = bk[6]
                                        P.op("pe", lambda e: e.transpose(tb[:, u * 128:(u + 1) * 128], on[:], it[:]), reads=[ons, isl], writes=[tbs])
                                yield
                            if c == 1:
                                pend()
                                tb, tbs = bk[6]
                                P.op("act", lambda e, tb=tb, J=J, oT=oT: e.copy(out=oT[:, J * 512:(J + 1) * 512], in_=tb[:, :]), reads=[], writes=[oT_s, tbs])
                    store_mix(oT, oT_s, h)

                def drain_(g):
                    for _ in g:
                        pass
                drain_(thrDP(0))
                for h in range(8):
                    gp = thrDP(h + 1) if h + 1 < 8 else iter(())
                    cnt_ = 0
                    for _ in thrDA(h):
                        cnt_ += 1
                        if cnt_ % 3 == 0:
                            next(gp, None)
                    drain_(gp)
                P.fence()
                ph.close()

            if "gdn" in parts:
                ph = contextlib.ExitStack()

                def tmb(name):
                    return Buf(P, ph, name, [128, NB, 8], F32).next()
                btm, btm_s = tmb("btm")
                nbtm, nbtm_s = tmb("nbtm")
                gtm, gtm_s = tmb("gtm")
                gctm, gctm_s = tmb("gctm")
                gltm, gltm_s = tmb("gltm")
                egtm, egtm_s = tmb("egtm")
                kdtm, kdtm_s = tmb("kdtm")
                bkeg, bkeg_s = tmb("bkeg")
                ph2 = contextlib.ExitStack()
                wba, wba_s = Buf(P, ph2, "wba", [128, NC_, 16], BF16).next()
                P.dma("sp", "wba", lambda e: e.dma_start(out=wba[:], in_=WBA[l]), reads=[wslot["ba", l]], writes=[wba_s])
                bfm, bfm_s = Buf(P, ph2, "bfm", [8, SQ], F32).next()
                gfm, gfm_s = Buf(P, ph2, "gfm", [8, SQ], F32).next()
                for iq in range(NQ):
                    tok = slice(iq * 512, (iq + 1) * 512)
                    pa, pas = bk[0]
                    pb_, pbs = bk[1]
                    proj_fm(wba, wba_s, iq, pa, pas, 0, 8)
                    proj_fm(wba, wba_s, iq, pb_, pbs, 8, 16)
                    P.op("act", lambda e, tok=tok: e.activation(out=bfm[:, tok], in_=pa[0:8, :], func=AF.Sigmoid), reads=[], writes=[bfm_s, pas])
                    P.op("act", lambda e, tok=tok: e.activation(out=gfm[:, tok], in_=pb_[0:8, :], func=AF.Exp, bias=hp_t[0:8, 2 * l + 1:2 * l + 2], scale=1.0),
                         reads=[hp_s], writes=[gfm_s, pbs])
                P.op("dve", lambda e: e.tensor_scalar(out=gfm[:, :], in0=gfm[:, :], scalar1=1.0, scalar2=None, op0=ALU.add), reads=[gfm_s], writes=[gfm_s])
                P.op("act", lambda e: e.activation(out=gfm[:, :], in_=gfm[:, :], func=AF.Ln), reads=[gfm_s], writes=[gfm_s])
                P.op("dve", lambda e: e.tensor_scalar(out=gfm[:, :], in0=gfm[:, :], scalar1=nA_t[0:8, l:l + 1], scalar2=None, op0=ALU.mult), reads=[gfm_s, nA_s], writes=[gfm_s])
                for (src, src_s, dst, dst_s, bki) in ((bfm, bfm_s, btm, btm_s, 2), (gfm, gfm_s, gtm, gtm_s, 3)):
                    pb, ps = bk[bki]
                    for b in range(NB):
                        P.op("pe", lambda e, pb=pb, src=src, b=b: e.transpose(pb[:, b * 8:(b + 1) * 8], src[0:8, b * 128:(b + 1) * 128], it[0:8, 0:8]),
                             reads=[src_s, isl], writes=[ps], signal=(b == NB - 1))
                    P.op("dve", lambda e, pb=pb, dst=dst: e.tensor_copy(out=dst[:, :, :].rearrange("p a b -> p (a b)"), in_=pb[:, 0:NB * 8]), reads=[], writes=[dst_s, ps])
                for (msk, dst, dst_s, bki) in ((UINC, gctm, gctm_s, 4), (BLKM, gltm, gltm_s, 5)):
                    pb, ps = bk[bki]
                    for b in range(NB):
                        P.op("pe", lambda e, pb=pb, msk=msk, b=b: e.matmul(pb[:, b * 8:(b + 1) * 8], msk, gtm[:, b, :], start=True, stop=True),
                             reads=[mk_s, gtm_s], writes=[ps], signal=(b == NB - 1))
                    P.op("dve", lambda e, pb=pb, dst=dst: e.tensor_copy(out=dst[:, :, :].rearrange("p a b -> p (a b)"), in_=pb[:, 0:NB * 8]), reads=[], writes=[dst_s, ps])
                fl = lambda t_: t_[:, :, :].rearrange("p a b -> p (a b)")
                P.op("act", lambda e: e.activation(out=fl(egtm), in_=fl(gctm), func=AF.Exp), reads=[gctm_s], writes=[egtm_s])
                P.op("dve", lambda e: e.tensor_tensor(out=fl(kdtm), in0=fl(gltm), in1=fl(gctm), op=ALU.subtract), reads=[gltm_s, gctm_s], writes=[kdtm_s])
                P.op("act", lambda e: e.activation(out=fl(kdtm), in_=fl(kdtm), func=AF.Exp), reads=[kdtm_s], writes=[kdtm_s])
                P.op("dve", lambda e: e.tensor_scalar(out=fl(nbtm), in0=fl(btm), scalar1=-1.0, scalar2=None, op0=ALU.mult), reads=[btm_s], writes=[nbtm_s])
                P.op("dve", lambda e: e.tensor_tensor(out=fl(bkeg), in0=fl(btm), in1=fl(egtm), op=ALU.mult), reads=[btm_s, egtm_s], writes=[bkeg_s])
                P.fence()
                ph2.close()
                upad, upad_s = Buf(P, ph, "upad", [128, SQ + 4], F32).next()
                cv, cv_s = Buf(P, ph, "cv", [128, SQ], F32).next()
                gqTb = Buf(P, ph, "gqT", [128, SQ], BF16, n=2)
                gkTb = Buf(P, ph, "gkT", [128, SQ], BF16, n=2)
                ktmb = Buf(P, ph, "ktm", [128, NB, 128], BF16, n=2)
                vtmb = Buf(P, ph, "vtm", [128, NB, 128], BF16, n=2)
                qdTb = Buf(P, ph, "qdT", [128, SQ], BF16, n=2)
                qdT_qs = [[P.slot() for _ in range(NB // 4)] for _ in range(2)]
                zs, zs_s = Buf(P, ph, "zsb", [128, SQ], BF16).next()
                otm, otm_s = Buf(P, ph, "otm", [128, NB, 128], F32).next()
                goTb = Buf(P, ph, "oTg", [128, SQ], BF16, n=1)
                sqb = Buf(P, ph, "sqb", [128, 512], BF16, n=2)
                gbt = Buf(P, ph, "gbt", [128, 128], F32, n=3)
                EGr, EGr_s = Buf(P, ph, "EGr", [128, 512], F32).next()
                cdqb = Buf(P, ph, "cdq", [128, 8], F32, n=2)
                Dt, Dt_s = Buf(P, ph, "Dt", [128, 4, 128], F32).next()
                DTt, DTt_s = Buf(P, ph, "DTt", [128, 4, 128], F32).next()
                Mf, Mf_s = Buf(P, ph, "Mf", [128, 4, 128], F32).next()
                Yf, Yf_s = Buf(P, ph, "Yf", [128, 4, 128], F32).next()
                Pbs = Buf(P, ph, "Pb", [128, 4, 128], BF16, n=2)
                Qbs = Buf(P, ph, "Qb", [128, 4, 128], BF16, n=2)
                Ybb = Buf(P, ph, "Yb", [128, 4, 128], BF16, n=2)
                aTtb = Buf(P, ph, "attnT", [128, 4, 128], BF16, n=2)
                rvb = Buf(P, ph, "rhsv", [128, 4, 128], BF16, n=2)
                rk, rk_s = Buf(P, ph, "rhsk", [128, 4, 128], BF16).next()
                kdb = Buf(P, ph, "kdec", [128, 4, 128], BF16, n=2)
                nwb = Buf(P, ph, "nwT", [128, 4, 128], BF16, n=2)
                vn, vn_s = Buf(P, ph, "vnew", [128, 128], BF16).next()
                Sf, Sf_s = Buf(P, ph, "Sf", [128, 128], F32).next()
                Sb, Sb_s = Buf(P, ph, "Sb", [128, 128], BF16).next()
                gonb = Buf(P, ph, "gonb", [128, 128], F32, n=3)
                gsq_all, gsq_s = Buf(P, ph, "gsqall", [128, NB, 128], F32).next()
                grs, grs_s = Buf(P, ph, "grs", [128, 2 * NB], F32).next()
                gdl = ng_t[:, l * 256 + 128:l * 256 + 256]
                flq = lambda t_: t_[:, :, :].rearrange("p a b -> p (a b)")
                NQD = NB // 4
                hd = {}
                qres = {}

                def conv_silu(m, ch):
                    wt, ws = wget("pr", m)
                    P.op("pool", lambda e: e.memset(upad[:, 0:3], 0.0), writes=[upad_s])
                    for iq in range(NQ):
                        pb, ps = bk[0]
                        proj_fm(wt, ws, iq, pb, ps)
                        P.op("act", lambda e, pb=pb, iq=iq: e.copy(out=upad[:, 3 + iq * 512:3 + (iq + 1) * 512], in_=pb[:, :]), reads=[], writes=[upad_s, ps])
                        yield
                    base = (l * 24 + ch) * 4
                    P.op("dve", lambda e: e.tensor_scalar(out=cv[:], in0=upad[:, 0:SQ], scalar1=cw_t[:, base:base + 1], scalar2=None, op0=ALU.mult),
                         reads=[upad_s, cw_s], writes=[cv_s])
                    for j in range(1, 4):
                        P.op("dve", lambda e, j=j: e.scalar_tensor_tensor(out=cv[:], in0=upad[:, j:j + SQ], scalar=cw_t[:, base + j:base + j + 1], in1=cv[:],
                                                                          op0=ALU.mult, op1=ALU.add), reads=[upad_s, cw_s, cv_s], writes=[cv_s])
                    P.op("act", lambda e: e.activation(out=cv[:], in_=cv[:], func=AF.Silu), reads=[cv_s], writes=[cv_s])
                    yield

                def l2n():
                    for iq in range(NQ):
                        tok = slice(iq * 512, (iq + 1) * 512)
                        sq, sqs = sqb.next()
                        P.op("act", lambda e, sq=sq, tok=tok: e.activation(out=sq[:], in_=cv[:, tok], func=AF.Square), reads=[cv_s], writes=[sqs])
                        pb, ps = bk[0]
                        P.op("pe", lambda e, pb=pb, sq=sq: e.matmul(pb[:, :], ot[:], sq[:], start=True, stop=True), reads=[osl, sqs], writes=[ps])
                        P.op("dve", lambda e, pb=pb, tok=tok: e.tensor_scalar(out=upad[:, tok], in0=pb[:, :], scalar1=1e-6, scalar2=None, op0=ALU.add), reads=[], writes=[upad_s, ps])
                    P.op("act", lambda e: e.activation(out=upad[:, 0:SQ], in_=upad[:, 0:SQ], func=AF.Sqrt), reads=[upad_s], writes=[upad_s])
                    P.op("dve", lambda e: e.reciprocal(out=upad[:, 0:SQ], in_=upad[:, 0:SQ]), reads=[upad_s], writes=[upad_s])
                    yield

                def to_tm(dst, dst_s):
                    for b4 in range(NB // 4):
                        pb, ps = bk[1]
                        for i4 in range(4):
                            b = b4 * 4 + i4
                            P.op("pe", lambda e, pb=pb, i4=i4, b=b: e.transpose(pb[:, i4 * 128:(i4 + 1) * 128], cv[:, b * 128:(b + 1) * 128], it[:]),
                                 reads=[cv_s, isl], writes=[ps], signal=(i4 == 3))
                        P.op("act", lambda e, pb=pb, b4=b4: e.copy(out=dst[:, b4 * 4:(b4 + 1) * 4, :], in_=pb[:, :].rearrange("p (a b) -> p a b", a=4)),
                             reads=[], writes=[dst_s, ps])
                        yield

                def thrA(h):
                    r = {}
                    r["gqT"], r["gqT_s"] = gqTb.next()
                    r["gkT"], r["gkT_s"] = gkTb.next()
                    r["ktm"], r["ktm_s"] = ktmb.next()
                    r["vtm"], r["vtm_s"] = vtmb.next()
                    r["qdT"], _ = qdTb.next()
                    r["qdT_qs"] = qdT_qs[qdTb.i]
                    hd[h] = r
                    gqT, gqT_s, gkT, gkT_s = r["gqT"], r["gqT_s"], r["gkT"], r["gkT_s"]
                    yield from conv_silu(DQ + h, h)
                    yield from l2n()
                    P.op("dve", lambda e: e.scalar_tensor_tensor(out=gqT[:], in0=cv[:], scalar=128 ** -0.5, in1=upad[:, 0:SQ], op0=ALU.mult, op1=ALU.mult),
                         reads=[cv_s, upad_s], writes=[gqT_s])
                    yield
                    yield from conv_silu(DK + h, 8 + h)
                    yield from l2n()
                    P.op("dve", lambda e: e.tensor_tensor(out=cv[:], in0=cv[:], in1=upad[:, 0:SQ], op=ALU.mult), reads=[cv_s, upad_s], writes=[cv_s])
                    P.op("pool", lambda e: e.tensor_copy(out=gkT[:], in_=cv[:]), reads=[cv_s], writes=[gkT_s])
                    yield from to_tm(r["ktm"], r["ktm_s"])
                    yield from conv_silu(DV + h, 16 + h)
                    yield from to_tm(r["vtm"], r["vtm_s"])

                def thrB(h, qd):
                    r = hd[h]
                    gqT, gqT_s, gkT, gkT_s, ktm, ktm_s, vtm, vtm_s = r["gqT"], r["gqT_s"], r["gkT"], r["gkT_s"], r["ktm"], r["ktm_s"], r["vtm"], r["vtm_s"]
                    qdT, qdT_s = r["qdT"], r["qdT_qs"][qd]
                    hs = slice(h, h + 1)
                    Yb, Yb_s = Ybb.next()
                    aTt, aTt_s = aTtb.next()
                    rv, rv_s = rvb.next()
                    kd, kd_s = kdb.next()
                    nw, nw_s = nwb.next()
                    cdq, cdq_s = cdqb.next()
                    qres[h, qd] = dict(Yb=Yb, Yb_s=Yb_s, aTt=aTt, aTt_s=aTt_s, rv=rv, rv_s=rv_s, kd=kd, kd_s=kd_s, nw=nw, nw_s=nw_s, cdq=cdq, cdq_s=cdq_s)
                    qtok = slice(qd * 512, (qd + 1) * 512)
                    g0, g0s = bk[1]
                    g1, g1s = bk[2]
                    for j4 in range(4):
                        b = qd * 4 + j4
                        gb, gbs = gbt.next()
                        P.op("pool", lambda e, gb=gb, b=b: e.tensor_scalar(out=gb[:], in0=onf_t[:], scalar1=gtm[:, b, hs], scalar2=None, op0=ALU.mult),
                             reads=[onf_s, gtm_s], writes=[gbs])
                        P.op("pe", lambda e, gb=gb, j4=j4: e.matmul(g0[:, j4 * 128:(j4 + 1) * 128], gb[:], UINC, start=True, stop=True), reads=[gbs, mk_s], writes=[g0s])
                        P.op("pe", lambda e, gb=gb, j4=j4: e.matmul(g1[:, j4 * 2:(j4 + 1) * 2], gb[:], CIND, start=True, stop=True), reads=[gbs, mk_s], writes=[g1s])
                    yield
                    P.op("act", lambda e: e.activation(out=EGr[:], in_=g0[:, :], func=AF.Exp), reads=[], writes=[EGr_s, g0s])
                    P.op("act", lambda e: e.activation(out=cdq[:], in_=g1[:, 0:8], func=AF.Exp), reads=[], writes=[cdq_s, g1s])
                    P.op("dve", lambda e: e.tensor_tensor(out=qdT[:, qtok], in0=gqT[:, qtok], in1=EGr[:], op=ALU.mult), reads=[gqT_s, EGr_s], writes=[qdT_s])
                    for j4 in range(4):
                        b = qd * 4 + j4
                        gsl = slice(j4 * 128, (j4 + 1) * 128)
                        P.op("dve", lambda e, j4=j4, b=b, gsl=gsl: e.scalar_tensor_tensor(out=Dt[:, j4, :], in0=g0[:, gsl], scalar=gctm[:, b, hs], in1=zer_t[:],
                                                                                     op0=ALU.subtract, op1=ALU.max), reads=[gctm_s, zer_s], writes=[Dt_s, g0s])
                        P.op("dve", lambda e, j4=j4, b=b, gsl=gsl: e.scalar_tensor_tensor(out=DTt[:, j4, :], in0=g0[:, gsl], scalar=gctm[:, b, hs], in1=zer_t[:],
                                                                                     op0=ALU.subtract, op1=ALU.min), reads=[gctm_s, zer_s], writes=[DTt_s, g0s])
                    yield
                    P.op("act", lambda e: e.activation(out=flq(Dt), in_=flq(Dt), func=AF.Exp, scale=-1.0), reads=[Dt_s], writes=[Dt_s])
                    P.op("act", lambda e: e.activation(out=flq(DTt), in_=flq(DTt), func=AF.Exp), reads=[DTt_s], writes=[DTt_s])
                    for j4 in range(4):
                        P.op("pool", lambda e, j4=j4: e.tensor_tensor(out=Dt[:, j4, :], in0=Dt[:, j4, :], in1=LSTR, op=ALU.mult), reads=[mk_s], writes=[Dt_s])
                        P.op("pool", lambda e, j4=j4: e.tensor_tensor(out=DTt[:, j4, :], in0=DTt[:, j4, :], in1=UINC, op=ALU.mult), reads=[mk_s], writes=[DTt_s])
                    yield
                    a2, a2s = bk[3]
                    a3, a3s = bk[1]
                    Pb, Pb_s = Pbs.next()
                    Qb, Qb_s = Qbs.next()
                    for j4 in range(4):
                        blk = slice((qd * 4 + j4) * 128, (qd * 4 + j4 + 1) * 128)
                        P.op("pe", lambda e, j4=j4, blk=blk: e.matmul(a2[:, j4 * 128:(j4 + 1) * 128], gkT[:, blk], gkT[:, blk], start=True, stop=True),
                             reads=[gkT_s], writes=[a2s], signal=(j4 == 3))
                    for j4 in range(4):
                        b = qd * 4 + j4
                        P.op("dve", lambda e, j4=j4, b=b: e.scalar_tensor_tensor(out=Mf[:, j4, :], in0=a2[:, j4 * 128:(j4 + 1) * 128], scalar=nbtm[:, b, hs], in1=Dt[:, j4, :],
                                                                                 op0=ALU.mult, op1=ALU.mult), reads=[nbtm_s, Dt_s], writes=[Mf_s, a2s])
                    P.op("pool", lambda e, Pb=Pb: e.tensor_copy(out=flq(Pb), in_=flq(Mf)), reads=[Mf_s], writes=[Pb_s])
                    yield
                    for j4 in range(4):
                        P.op("pe", lambda e, j4=j4: e.transpose(a3[:, j4 * 128:(j4 + 1) * 128], Mf[:, j4, :], it[:]), reads=[Mf_s, isl], writes=[a3s], signal=(j4 == 3))
                    P.op("act", lambda e, Qb=Qb: e.copy(out=flq(Qb), in_=a3[:, :]), reads=[], writes=[Qb_s, a3s])
                    for j4 in range(4):
                        P.op("dve", lambda e, j4=j4: e.tensor_tensor(out=Yf[:, j4, :], in0=a3[:, j4 * 128:(j4 + 1) * 128], in1=it[:], op=ALU.add), reads=[isl], writes=[Yf_s, a3s])
                    P.op("pool", lambda e: e.tensor_copy(out=flq(Yb), in_=flq(Yf)), reads=[Yf_s], writes=[Yb_s])
                    yield
                    for n in range(5):
                        p4, p4s = bk[2]
                        p5, p5s = bk[3]
                        p6, p6s = bk[1]
                        Pn, Pn_s = Pbs.next()
                        for j4 in range(4):
                            P.op("pe", lambda e, j4=j4, Qb=Qb, Pb=Pb: e.matmul(p4[:, j4 * 128:(j4 + 1) * 128], Qb[:, j4, :], Pb[:, j4, :], start=True, stop=True),
                                 reads=[Qb_s, Pb_s], writes=[p4s], signal=(j4 == 3))
                        if n < 4:
                            Qn, Qn_s = Qbs.next()
                            for j4 in range(4):
                                P.op("pe", lambda e, j4=j4, Qb=Qb, Pb=Pb: e.matmul(p5[:, j4 * 128:(j4 + 1) * 128], Pb[:, j4, :], Qb[:, j4, :], start=True, stop=True),
                                     reads=[Qb_s, Pb_s], writes=[p5s], signal=(j4 == 3))
                        P.op("dve", lambda e, Pn=Pn: e.tensor_copy(out=flq(Pn), in_=p4[:, :]), reads=[], writes=[Pn_s, p4s])
                        if n < 4:
                            P.op("act", lambda e, Qn=Qn: e.copy(out=flq(Qn), in_=p5[:, :]), reads=[], writes=[Qn_s, p5s])
                        yield
                        for j4 in range(4):
                            P.op("pe", lambda e, j4=j4, Pn=Pn: e.matmul(p6[:, j4 * 128:(j4 + 1) * 128], Pn[:, j4, :], Yb[:, j4, :], start=True, stop=True),
                                 reads=[Pn_s, Yb_s], writes=[p6s], signal=(j4 == 3))
                        P.op("dve", lambda e: e.tensor_tensor(out=flq(Yf), in0=p6[:, :], in1=flq(Yf), op=ALU.add), reads=[Yf_s], writes=[Yf_s, p6s])
                        P.op("act", lambda e: e.copy(out=flq(Yb), in_=flq(Yf)), reads=[Yf_s], writes=[Yb_s])
                        Pb, Pb_s = Pn, Pn_s
                        if n < 4:
                            Qb, Qb_s = Qn, Qn_s
                        yield
                    aq2, aq2s = bk[2]
                    aw3, aw3s = bk[3]
                    for j4 in range(4):
                        blk = slice((qd * 4 + j4) * 128, (qd * 4 + j4 + 1) * 128)
                        P.op("pe", lambda e, j4=j4, blk=blk: e.matmul(aq2[:, j4 * 128:(j4 + 1) * 128], gkT[:, blk], gqT[:, blk], start=True, stop=True),
                             reads=[gkT_s, gqT_s], writes=[aq2s], signal=(j4 == 3))
                    P.op("dve", lambda e: e.tensor_tensor(out=flq(aTt), in0=aq2[:, :], in1=flq(DTt), op=ALU.mult), reads=[DTt_s], writes=[aTt_s, aq2s])
                    for j4 in range(4):
                        b = qd * 4 + j4
                        P.op("act", lambda e, j4=j4, b=b: e.activation(out=rv[:, j4, :], in_=vtm[:, b, :], func=AF.Copy, scale=btm[:, b, hs]),
                             reads=[vtm_s, btm_s], writes=[rv_s])
                        P.op("act", lambda e, j4=j4, b=b: e.activation(out=rk[:, j4, :], in_=ktm[:, b, :], func=AF.Copy, scale=bkeg[:, b, hs]),
                             reads=[ktm_s, bkeg_s], writes=[rk_s])
                        P.op("act", lambda e, j4=j4, b=b: e.activation(out=kd[:, j4, :], in_=ktm[:, b, :], func=AF.Copy, scale=kdtm[:, b, hs]),
                             reads=[ktm_s, kdtm_s], writes=[kd_s])
                    yield
                    for j4 in range(4):
                        P.op("pe", lambda e, j4=j4: e.matmul(aw3[:, j4 * 128:(j4 + 1) * 128], rk[:, j4, :], Yb[:, j4, :], start=True, stop=True),
                             reads=[rk_s, Yb_s], writes=[aw3s], signal=(j4 == 3))
                    P.op("act", lambda e: e.mul(out=flq(nw), in_=aw3[:, :], mul=-1.0), reads=[], writes=[nw_s, aw3s])
                    yield

                def thrC(h, qd):
                    r = hd[h]
                    qdT, qdT_s = r["qdT"], r["qdT_qs"][qd]
                    z = qres[h, qd]
                    Yb, Yb_s, aTt, aTt_s, rv, rv_s, kd, kd_s, nw, nw_s, cdq, cdq_s = (z["Yb"], z["Yb_s"], z["aTt"], z["aTt_s"], z["rv"], z["rv_s"],
                                                                                      z["kd"], z["kd_s"], z["nw"], z["nw_s"], z["cdq"], z["cdq_s"])
                    if qd == 0:
                        P.op("pool", lambda e: e.memset(Sf[:], 0.0), writes=[Sf_s])
                        P.op("pool", lambda e: e.memset(Sb[:], 0.0), writes=[Sb_s])
                    for j4 in range(4):
                        b = qd * 4 + j4
                        blk = slice(b * 128, (b + 1) * 128)
                        for x in range(2):
                            R = slice(64 * x, 64 * x + 64)
                            pv, pvs = bk[7]
                            po, pos_ = bk[4 + x]
                            pst, psts = bk[6]
                            P.op("pe", lambda e, j4=j4: e.matmul(pv[:, 0:128], Yb[:, j4, :], rv[:, j4, :], start=True, stop=False), reads=[Yb_s, rv_s], writes=[pvs], signal=False)
                            P.op("pe", lambda e, j4=j4: e.matmul(pv[:, 0:128], nw[:, j4, :], Sb[:], start=False, stop=True), reads=[nw_s, Sb_s], writes=[pvs])
                            P.op("act", lambda e, R=R: e.copy(out=vn[R, :], in_=pv[R, 0:128]), reads=[], writes=[vn_s, pvs])
                            P.op("pe", lambda e, blk=blk, po=po: e.matmul(po[:, 0:128], qdT[:, blk], Sb[:], start=True, stop=False), reads=[qdT_s, Sb_s], writes=[pos_], signal=False)
                            P.op("pe", lambda e, j4=j4, R=R, po=po: e.matmul(po[:, 0:128], aTt[R, j4, :], vn[R, :], start=False, stop=True), reads=[aTt_s, vn_s], writes=[pos_])
                            P.op("act", lambda e, R=R, b=b, po=po: e.copy(out=otm[R, b, :], in_=po[R, 0:128]), reads=[], writes=[otm_s, pos_])
                            P.op("pe", lambda e, j4=j4, R=R: e.matmul(pst[:, 0:128], kd[R, j4, :], vn[R, :], start=True, stop=True), reads=[kd_s, vn_s], writes=[psts])
                            ci_ = 2 * j4 + x
                            P.op("dve", lambda e, ci_=ci_: e.scalar_tensor_tensor(out=Sb[:], in0=Sf[:], scalar=cdq[:, ci_:ci_ + 1], in1=pst[:, 0:128], op0=ALU.mult, op1=ALU.add),
                                 reads=[cdq_s, Sf_s], writes=[Sb_s, psts])
                            P.op("dve", lambda e, ci_=ci_: e.scalar_tensor_tensor(out=Sf[:], in0=Sf[:], scalar=cdq[:, ci_:ci_ + 1], in1=pst[:, 0:128], op0=ALU.mult, op1=ALU.add),
                                 reads=[cdq_s], writes=[Sf_s, psts])
                            yield

                def thrG(h):
                    wz, wzs = wget("pr", DZ + h)
                    for iq in range(NQ):
                        pb, ps = bk[4 + iq % 2]
                        proj_fm(wz, wzs, iq, pb, ps)
                        P.op("act", lambda e, pb=pb, iq=iq: e.activation(out=zs[:, iq * 512:(iq + 1) * 512], in_=pb[:, :], func=AF.Silu), reads=[], writes=[zs_s, ps])
                        yield
                    goT, goT_s = goTb.next()
                    P.op("dve", lambda e: e.tensor_tensor(out=flq(gsq_all), in0=flq(otm), in1=flq(otm), op=ALU.mult), reads=[otm_s], writes=[gsq_s])
                    P.op("dve", lambda e: e.reduce_sum(out=grs[:, 0:NB], in_=gsq_all[:, :, :], axis=AX), reads=[gsq_s], writes=[grs_s])
                    P.op("dve", lambda e: e.tensor_scalar(out=grs[:, 0:NB], in0=grs[:, 0:NB], scalar1=1.0 / 128, scalar2=1e-6, op0=ALU.mult, op1=ALU.add), reads=[grs_s], writes=[grs_s])
                    P.op("act", lambda e: e.activation(out=grs[:, 0:NB], in_=grs[:, 0:NB], func=AF.Sqrt), reads=[grs_s], writes=[grs_s])
                    P.op("dve", lambda e: e.reciprocal(out=grs[:, NB:2 * NB], in_=grs[:, 0:NB]), reads=[grs_s], writes=[grs_s])
                    yield
                    for b4 in range(NB // 4):
                        tb, tbs = bk[4 + b4 % 2]
                        for i4 in range(4):
                            b = b4 * 4 + i4
                            on, ons = gonb.next()
                            P.op("act", lambda e, on=on, b=b: e.activation(out=on[:], in_=otm[:, b, :], func=AF.Copy, scale=grs[:, NB + b:NB + b + 1]),
                                 reads=[otm_s, grs_s], writes=[ons])
                            P.op("pool", lambda e, on=on: e.tensor_tensor(out=on[:], in0=on[:], in1=gdl, op=ALU.mult), reads=[ng_s], writes=[ons])
                            P.op("pe", lambda e, tb=tb, on=on, i4=i4: e.transpose(tb[:, i4 * 128:(i4 + 1) * 128], on[:], it[:]), reads=[ons, isl], writes=[tbs])
                        P.op("dve", lambda e, tb=tb, b4=b4, goT=goT: e.tensor_tensor(out=goT[:, b4 * 512:(b4 + 1) * 512], in0=tb[:, :], in1=zs[:, b4 * 512:(b4 + 1) * 512], op=ALU.mult),
                             reads=[zs_s], writes=[goT_s, tbs])
                        yield
                    store_mix(goT, goT_s, 8 + h)

                def drain(g):
                    for _ in g:
                        pass

                def merge(primary, others):
                    for _ in primary:
                        for g, w in others:
                            for _i in range(w):
                                next(g, None)

                NH = 8
                gA = thrA(0)
                drain(gA)
                gB = thrB(0, 0)
                drain(gB)
                for h in range(NH):
                    gA = thrA(h + 1) if h + 1 < NH else iter(())
                    for qd in range(NQD):
                        if qd + 1 < NQD:
                            gB = thrB(h, qd + 1)
                        elif h + 1 < NH:
                            drain(gA)
                            gB = thrB(h + 1, 0)
                        else:
                            gB = iter(())
                        merge(thrC(h, qd), [(gB, 2), (gA, 2)])
                        drain(gB)
                    drain(gA)
                    drain(thrG(h))
                P.fence()
                ph.close()
            if len(parts) < 2:
                ph = contextlib.ExitStack()
                zT, zT_s = Buf(P, ph, "zT", [128, SQ], BF16).next()
                P.op("pool", lambda e: e.memset(zT[:], 0.0), writes=[zT_s])
                for j in (range(8, 16) if "gdn" not in parts else range(0, 8)):
                    store_mix(zT, zT_s, j)
                P.fence()
                ph.close()
            P.fence()
            phm.close()
            ph = contextlib.ExitStack()
            b_ = ln_bufs(ph, 2)
            b_["engs"] = ("dve", "pool", "dve")
            mtb = Buf(P, ph, "mixt", [128, NC_, TT], BF16, n=2)
            wmb = Buf(P, ph, "wmo", [128, NC_, 128], BF16, n=NC_)
            wres = []
            for c in range(NC_):
                wt, ws = wmb.next()
                P.dma("sp", f"wmo{c % 4}", lambda e, wt=wt, c=c: e.dma_start(out=wt[:], in_=WMO[l][c]), reads=[wslot["mo", l, c]], writes=[ws])
                wres.append((wt, ws))
            epc = iter(())
            for t in tiles:
                mt, mts = mtb.next()
                P.dma("pool", f"mixld{mtb.i}", lambda e, mt=mt, t=t: e.dma_start(out=mt[:], in_=MIX[t]), reads=mix_slots[t], writes=[mts])

                def wload(c):
                    return wres[c]
                ep_new = out_ln(l, 1, t, NC_, lambda j, mt=mt, mts=mts: (mt[:, j, :], mts), wload, 1.0 / ALPHA, b_)
                for _ in epc:
                    pass
                epc = ep_new
            for _ in epc:
                pass
            P.fence()
            ph.close()

        if "mix" in stages:
            rope_tables()
        for l in range(depth):
            if "ffn1" in stages:
                ffn_stage(l, 1, 0)
            if "mix" in stages:
                for q in range(nseq):
                    mixer_stage(l, q, cfg.get("parts", ("diff", "gdn")))
            if "ffn2" in stages:
                ffn_stage(l, 2, 2)

        outs = []
        ph = contextlib.ExitStack()
        xfst = Buf(P, ph, "xfst", [128, NC_, TT], F32, n=2)
        ytok = Buf(P, ph, "ytok", [128, D], F32, n=2)
        for t in range(ntile):
            ft, fs = xfst.next()
            P.dma("pool", f"xfld{xfst.i}", lambda e, ft=ft, t=t: e.dma_start(out=ft[:], in_=XF[t]),
                  reads=[xf_slots[t]], writes=[fs])
            for q4 in range(4):
                tt = t * 4 + q4
                yt, ys = ytok.next()
                for g in range(4):
                    pb, ps = bk[g % 2]
                    for i in range(4):
                        c = 4 * g + i
                        P.op("pe", lambda e, pb=pb, ft=ft, c=c, i=i, q4=q4: e.transpose(pb[:, i * 128:(i + 1) * 128], ft[:, c, q4 * 128:(q4 + 1) * 128], it[:]),
                             reads=[fs, isl], writes=[ps], signal=(i == 3))
                    if g % 2 == 0:
                        P.op("act", lambda e, yt=yt, pb=pb, g=g: e.copy(out=yt[:, g * 512:(g + 1) * 512], in_=pb[:, :]), reads=[ps], writes=[ys])
                    else:
                        P.op("dve", lambda e, yt=yt, pb=pb, g=g: e.tensor_copy(out=yt[:, g * 512:(g + 1) * 512], in_=pb[:, :]), reads=[ps], writes=[ys])
                osl_ = P.slot()
                outs.append(osl_)
                P.dma("pool", f"ytok{ytok.i}", lambda e, yt=yt, tt=tt: e.dma_start(out=y_out[tt * 128:(tt + 1) * 128, :], in_=yt[:]),
                      reads=[ys], writes=[osl_])
        P.wait_all("pool", outs)
        ph.close()
        P.emit(stack)
    return nc


def host_consts(depth, ln):
    a = np.stack(ln, axis=1)
    a = a.reshape(depth, 6, NC_, 128)
    a = np.transpose(a, (3, 0, 1, 2)).reshape(128, depth * 6 * NC_)
    return np.ascontiguousarray(a.astype(np.float32))


def host_mix_inputs(inp, depth, pos_core):
    f32 = np.float32
    w_in = np.asarray(inp["w_in"])[:depth]
    perm = np.arange(2048)
    d = perm % 64
    perm = np.where(d < 8, perm + 8, np.where(d < 16, perm - 8, perm))
    w_sw = np.ascontiguousarray(w_in[:, :, perm])
    conv_w = np.asarray(inp["conv_w"])[:depth]
    cw = conv_w.reshape(depth, 4, 24, 128).transpose(3, 0, 2, 1).reshape(128, depth * 96)
    hp8 = np.zeros((128, depth * 2), f32)
    hp8[:8, 0::2] = np.asarray(inp["a_log"])[:depth].T
    hp8[:8, 1::2] = np.asarray(inp["dt_bias"])[:depth].T
    lam = np.concatenate([np.asarray(inp[k])[:depth] for k in ("lam_q1", "lam_k1", "lam_q2", "lam_k2")], axis=1)
    lamv = np.broadcast_to(lam.reshape(1, depth * 256), (128, depth * 256))
    ng = np.concatenate([np.asarray(inp["diff_norm_g"])[:depth], np.asarray(inp["delta_norm_g"])[:depth]], axis=1)
    normg = np.broadcast_to(ng.reshape(1, depth * 256), (128, depth * 256))
    pos = np.broadcast_to(np.asarray(pos_core).reshape(1, -1).astype(np.int32), (128, pos_core.size))
    return {"w_in": np.ascontiguousarray(w_in), "w_in_sw": w_sw, "w_out": np.ascontiguousarray(np.asarray(inp["w_out"])[:depth]),
            "convw": np.ascontiguousarray(cw.astype(f32)), "hp8": hp8, "lamv": np.ascontiguousarray(lamv.astype(f32)),
            "normg": np.ascontiguousarray(normg.astype(f32)), "pos": np.ascontiguousarray(pos)}


def host_static_consts():
    f32 = np.float32
    p = np.arange(128)
    d = p % 64
    inv = 500000.0 ** (-(d % 8) / 8.0)
    ropec = np.zeros((128, 2), f32)
    ropec[:, 0] = np.where(d < 16, inv / (2 * np.pi), 0.0)
    ropec[:, 1] = np.where(d < 8, -1.0, np.where(d < 16, 1.0, 0.0))
    i = p[:, None]
    j = p[None, :]
    same = (i // 64) == (j // 64)
    masks = np.zeros((128, 640), f32)
    masks[:, 0:128] = (i <= j)
    masks[:, 128:256] = same & (i > j)
    masks[:, 256:384] = same & (i <= j)
    masks[:, 384] = (p < 64)
    masks[:, 385] = (p >= 64)
    masks[:, 512:640] = same
    return {"ropec": ropec, "masks": masks, "ident": np.eye(128, dtype=f32)}


def make_in_maps(inputs, n_cores, depth, stages=("ffn1", "mix", "ffn2")):
    x = np.asarray(inputs["x"])
    B, S, _ = x.shape
    per = B // n_cores
    pos = np.asarray(inputs["positions"])
    lnp = host_consts(depth, [np.asarray(inputs[k])[:depth] for k in ("ln1_g", "ln1_b", "ln2_g", "ln2_b", "ln3_g", "ln3_b")])
    st = host_static_consts()
    shared = {"lnp": lnp, "ident": st["ident"]}
    if "ffn1" in stages:
        shared["ffn1_w_in"] = np.ascontiguousarray(np.asarray(inputs["ffn1_w_in"])[:depth])
        shared["ffn1_w_out"] = np.ascontiguousarray(np.asarray(inputs["ffn1_w_out"])[:depth])
    if "ffn2" in stages:
        shared["ffn2_w_in"] = np.ascontiguousarray(np.asarray(inputs["ffn2_w_in"])[:depth])
        shared["ffn2_w_out"] = np.ascontiguousarray(np.asarray(inputs["ffn2_w_out"])[:depth])
    in_maps = []
    for c in range(n_cores):
        m = dict(shared)
        m["x"] = np.ascontiguousarray(x[c * per:(c + 1) * per].reshape(per * S, D))
        if "mix" in stages:
            mm = host_mix_inputs(inputs, depth, pos[c * per:(c + 1) * per].reshape(-1))
            if c > 0:
                for k in ("w_in", "w_in_sw", "w_out", "convw", "hp8", "lamv", "normg"):
                    mm[k] = in_maps[0][k]
            m.update(mm)
            m["ropec"] = st["ropec"]
            m["masks"] = st["masks"]
        in_maps.append(m)
    return in_maps, per, S


def kernel(**inputs):
    n = 8
    in_maps, per, S = make_in_maps(inputs, n, DEPTH)
    nc = build_program(dict(ntok=per * S, depth=DEPTH))
    res = run_bass_kernel_spmd(nc, in_maps, core_ids=list(range(n)))
    out = np.concatenate([np.asarray(r["y"]).reshape(per, S, D) for r in res.results], axis=0)
    return out.astype(np.float32)
```
